# Optimizing a Trainium2 kernel written in Bass

```python
import jax, jax.numpy as jnp
from jax import lax
import numpy as np

D_MODEL = 1024
BATCH = 16
SEQ = 4096
DEPTH = 4

N_META = 16
D_MIX = 1024
FOURIER_HEADS = 4
FOURIER_HEAD_DIM = 64
D_FOURIER = FOURIER_HEADS * FOURIER_HEAD_DIM
POOL_WINDOWS = (2, 4, 8, 16)
N_POOL_GROUPS = 4
POOL_GROUP_DIM = 64
D_POOL = N_POOL_GROUPS * POOL_GROUP_DIM
MLA_HEADS = 4
QK_NOPE_DIM = 128
QK_ROPE_DIM = 64
V_HEAD_DIM = 128
D_ATTN = MLA_HEADS * V_HEAD_DIM
Q_LORA_RANK = 384
KV_LORA_RANK = 256
ROPE_THETA = 10000.0
QUERY_BLOCK = 128
NORM_EPS = 1e-6
IN_SIZES = (D_FOURIER, D_FOURIER, D_POOL, D_POOL, Q_LORA_RANK, KV_LORA_RANK, QK_ROPE_DIM, D_ATTN)
D_IN = 2 * D_FOURIER + 2 * D_POOL + Q_LORA_RANK + KV_LORA_RANK + QK_ROPE_DIM + D_ATTN

kernel_name = "hymba_fnet_pool_mla_encoder"


def rms_norm(x, w):
    xf = x.astype(jnp.float32)
    y = xf * lax.rsqrt(jnp.mean(xf * xf, axis=-1, keepdims=True) + NORM_EPS)
    return (y * w.astype(jnp.float32)).astype(x.dtype)


def rope_tables(length):
    inv = 1.0 / (ROPE_THETA ** (jnp.arange(0, QK_ROPE_DIM, 2, dtype=jnp.float32) / QK_ROPE_DIM))
    ang = jnp.arange(length, dtype=jnp.float32)[:, None] * inv[None, :]
    return jnp.cos(ang), jnp.sin(ang)


def apply_rope(t, cos, sin):
    tf = t.astype(jnp.float32)
    half = QK_ROPE_DIM // 2
    t1, t2 = tf[..., :half], tf[..., half:]
    out = jnp.concatenate([t1 * cos - t2 * sin, t2 * cos + t1 * sin], axis=-1)
    return out.astype(t.dtype)


def fourier_mix(f_in, fourier_w):
    B, L, _ = f_in.shape
    xf = f_in.reshape(B, L, FOURIER_HEADS, FOURIER_HEAD_DIM).astype(jnp.float32)
    y = jnp.fft.fft2(xf, axes=(1, 3), norm="ortho").real.astype(f_in.dtype)
    y = jnp.einsum('blhc,hcd->blhd', y, fourier_w)
    return y.reshape(B, L, D_FOURIER)


def pool_mix(p_in, pool_w, pool_scale):
    B, L, _ = p_in.shape
    xg = p_in.reshape(B, L, N_POOL_GROUPS, POOL_GROUP_DIM).astype(jnp.float32)
    cs = jnp.concatenate([jnp.zeros((B, 1, N_POOL_GROUPS, POOL_GROUP_DIM), jnp.float32),
                          jnp.cumsum(xg, axis=1)], axis=1)
    idx = jnp.arange(L)
    outs = []
    for g, w in enumerate(POOL_WINDOWS):
        lo = jnp.clip(idx - w // 2, 0, L)
        hi = jnp.clip(idx + w // 2, 0, L)
        s = cs[:, hi, g] - cs[:, lo, g]
        cnt = (hi - lo).astype(jnp.float32)[None, :, None]
        outs.append(s / cnt - xg[:, :, g])
    y = jnp.stack(outs, axis=2).astype(p_in.dtype)
    y = jnp.einsum('blgc,gcd->blgd', y, pool_w).reshape(B, L, D_POOL)
    return y * pool_scale


def mla_mix(c_q, c_kv, k_r, q_norm_w, w_uq, kv_norm_w, w_ukv, cos, sin):
    B, L, _ = c_q.shape
    dq = QK_NOPE_DIM + QK_ROPE_DIM
    q = (rms_norm(c_q, q_norm_w) @ w_uq).reshape(B, L, MLA_HEADS, dq)
    q = jnp.concatenate([q[..., :QK_NOPE_DIM],
                         apply_rope(q[..., QK_NOPE_DIM:], cos[None, :, None], sin[None, :, None])], axis=-1)
    kv = (rms_norm(c_kv, kv_norm_w) @ w_ukv).reshape(B, L, MLA_HEADS, QK_NOPE_DIM + V_HEAD_DIM)
    k_nope, v = kv[..., :QK_NOPE_DIM], kv[..., QK_NOPE_DIM:]
    k_rope = apply_rope(k_r, cos[None], sin[None])
    k = jnp.concatenate([k_nope, jnp.broadcast_to(k_rope[:, :, None, :], (B, L, MLA_HEADS, QK_ROPE_DIM))], axis=-1)
    scale = dq ** -0.5

    def attend(qb):
        s = jnp.einsum('bqhd,bkhd->bhqk', qb, k).astype(jnp.float32) * scale
        p = jax.nn.softmax(s, axis=-1).astype(v.dtype)
        return jnp.einsum('bhqk,bkhd->bqhd', p, v)

    o_meta = attend(q[:, :N_META])
    n_blk = (L - N_META) // QUERY_BLOCK
    qb = jnp.moveaxis(q[:, N_META:].reshape(B, n_blk, QUERY_BLOCK, MLA_HEADS, dq), 1, 0)
    o = lax.map(attend, qb)
    o = jnp.moveaxis(o, 0, 1).reshape(B, L - N_META, MLA_HEADS, V_HEAD_DIM)
    return jnp.concatenate([o_meta, o], axis=1).reshape(B, L, D_ATTN)


def setup_inputs(seed: int = 0) -> dict:
    key = jax.random.key(seed)
    ks = jax.random.split(key, 13)
    f32 = jnp.float32
    nrm = lambda k, shape, fan: jax.random.normal(k, shape, f32) * (fan ** -0.5)
    return {
        "x": jax.random.normal(ks[0], (BATCH, SEQ, D_MODEL), f32),
        "meta_tokens": jax.random.normal(ks[1], (N_META, D_MODEL), f32),
        "norm_w": 1.0 + 0.02 * jax.random.normal(ks[2], (DEPTH, D_MODEL), f32),
        "w_in": nrm(ks[3], (DEPTH, D_MODEL, D_IN), D_MODEL),
        "fourier_w": nrm(ks[4], (DEPTH, FOURIER_HEADS, FOURIER_HEAD_DIM, FOURIER_HEAD_DIM), FOURIER_HEAD_DIM),
        "pool_w": nrm(ks[5], (DEPTH, N_POOL_GROUPS, POOL_GROUP_DIM, POOL_GROUP_DIM), POOL_GROUP_DIM),
        "pool_scale": 1.0 + 0.02 * jax.random.normal(ks[6], (DEPTH, D_POOL), f32),
        "q_norm_w": 1.0 + 0.02 * jax.random.normal(ks[7], (DEPTH, Q_LORA_RANK), f32),
        "w_uq": nrm(ks[8], (DEPTH, Q_LORA_RANK, MLA_HEADS * (QK_NOPE_DIM + QK_ROPE_DIM)), Q_LORA_RANK),
        "kv_norm_w": 1.0 + 0.02 * jax.random.normal(ks[9], (DEPTH, KV_LORA_RANK), f32),
        "w_ukv": nrm(ks[10], (DEPTH, KV_LORA_RANK, MLA_HEADS * (QK_NOPE_DIM + V_HEAD_DIM)), KV_LORA_RANK),
        "w_out": nrm(ks[11], (DEPTH, D_MIX, D_MODEL), D_MIX),
        "final_norm_w": 1.0 + 0.02 * jax.random.normal(ks[12], (D_MODEL,), f32),
    }


def reference(x, meta_tokens, norm_w, w_in, fourier_w, pool_w, pool_scale, q_norm_w, w_uq,
              kv_norm_w, w_ukv, w_out, final_norm_w):
    B = x.shape[0]
    h = jnp.concatenate([jnp.broadcast_to(meta_tokens[None].astype(x.dtype), (B, N_META, D_MODEL)), x], axis=1)
    L = h.shape[1]
    cos, sin = rope_tables(L)
    split_idx = np.cumsum(IN_SIZES)[:-1].tolist()
    for l in range(DEPTH):
        u = rms_norm(h, norm_w[l]) @ w_in[l]
        f_in, f_gate, p_in, p_gate, c_q, c_kv, k_r, a_gate = jnp.split(u, split_idx, axis=-1)
        y_f = fourier_mix(f_in, fourier_w[l])
        y_p = pool_mix(p_in, pool_w[l], pool_scale[l])
        y_a = mla_mix(c_q, c_kv, k_r, q_norm_w[l], w_uq[l], kv_norm_w[l], w_ukv[l], cos, sin)
        mix = jnp.concatenate([y_f * jax.nn.silu(f_gate),
                               y_p * jax.nn.silu(p_gate),
                               y_a * jax.nn.silu(a_gate)], axis=-1)
        h = h + mix @ w_out[l]
    return rms_norm(h, final_norm_w)[:, N_META:]
```

```python
import contextlib
import numpy as np
import ml_dtypes
import concourse.bass as bass
import concourse.mybir as mybir
from concourse.bass_utils import run_bass_kernel_spmd

F32 = mybir.dt.float32
BF16 = mybir.dt.bfloat16
AF = mybir.ActivationFunctionType
ALU = mybir.AluOpType

NCORES = 8
NS = 2
DM = 1024
SEQ = 4096
NMETA = 16
L = SEQ + NMETA
DEPTH = 4
DIN = 2240
EPS = 1e-6
NT = 33
TM_COLS = 1152
FM_COLS = 1408
SCALE = 192 ** -0.5

STILES = [(i * 512, 512) for i in range(8)] + [(4096, 16)]


def tiles_of(t0, n):
    if n == 512:
        return [(j * 128, 128) for j in range(4)]
    return [(0, n)]


class SemObj:
    def __init__(self, handle, name):
        self.h = handle
        self.name = name
        self.count = 0


class Res:
    __slots__ = ("name", "w", "r")

    def __init__(self, name=""):
        self.name = name
        self.w = None
        self.r = {}


class _Rec:
    def __init__(self):
        self.call = None

    def __getattr__(self, name):
        def f(*a, **kw):
            self.call = (name, a, kw)
            return None
        return f


class K:
    ENGS = ("pe", "act", "dve", "pool", "sp")

    def __init__(self, nc, stack):
        self.nc = nc
        self.stack = stack
        self.ops = {e: [] for e in self.ENGS}
        self.esem = {e: SemObj(stack.enter_context(nc.semaphore("s_" + e)), "s_" + e) for e in self.ENGS}
        self.waited = {e: {} for e in self.ENGS}
        self.dsems = []
        self.free_dsems = []
        self.phase_dsems = []
        self.nops = 0

    def dsem(self, name):
        if self.free_dsems:
            s = self.free_dsems.pop()
        else:
            s = SemObj(self.stack.enter_context(self.nc.semaphore(f"dsem{len(self.dsems)}")), f"dsem{len(self.dsems)}")
            self.dsems.append(s)
        self.phase_dsems.append(s)
        return s

    def op(self, eng, fn, reads=(), writes=(), dma=None):
        rec = _Rec()
        fn(rec)
        call = rec.call
        assert call is not None
        deps = []
        for r in reads:
            if r.w is not None:
                deps.append(r.w)
        for w in writes:
            if w.w is not None:
                deps.append(w.w)
            deps.extend(w.r.values())
        my = self.esem[eng]
        waits = {}
        wd = self.waited[eng]
        for (s, v) in deps:
            if s is my and eng == "pe":
                continue
            if wd.get(s.name, 0) >= v:
                continue
            if waits.get(s.name, (None, 0))[1] < v:
                waits[s.name] = (s, v)
        for (s, v) in waits.values():
            wd[s.name] = v
        if dma is not None:
            dma.count += 16
            mark = (dma, dma.count)
            inc = (dma, 16)
        else:
            my.count += 1
            mark = (my, my.count)
            inc = (my, 1)
        for r in reads:
            r.r[mark[0].name] = mark
        for w in writes:
            w.w = mark
            w.r = {}
        self.ops[eng].append((list(waits.values()), call, inc))
        self.nops += 1
        return mark

    def barrier(self):
        allsems = list(self.esem.values()) + self.dsems
        for eng in self.ENGS:
            waits = []
            wd = self.waited[eng]
            for s in allsems:
                if s is self.esem[eng]:
                    continue
                if s.count > wd.get(s.name, 0):
                    waits.append((s, s.count))
                    wd[s.name] = s.count
            if waits:
                self.ops[eng].append((waits, None, None))

    def end_phase(self):
        self.barrier()
        self.free_dsems.extend(self.phase_dsems)
        self.phase_dsems = []

    def emit(self):
        nc = self.nc
        with nc.Block() as block:
            def run(engname):
                def body(e):
                    for (waits, call, inc) in self.ops[engname]:
                        for (s, v) in waits:
                            e.wait_ge(s.h, v)
                        if call is not None:
                            name, a, kw = call
                            ins = getattr(e, name)(*a, **kw)
                            ins.then_inc(inc[0].h, inc[1])
                return body
            block.tensor(run("pe"))
            block.scalar(run("act"))
            block.vector(run("dve"))
            block.gpsimd(run("pool"))
            block.sync(run("sp"))


_UID = [0]


class Ring:
    def __init__(self, k, nc, st, name, shape, dtype, n, with_dsem=True):
        self.items = []
        for i in range(n):
            _UID[0] += 1
            t = st.enter_context(nc.sbuf_tensor(f"rg_{name}{i}_{_UID[0]}", shape, dtype))
            self.items.append((t, Res(f"{name}{i}"), k.dsem(f"d_{name}{i}") if with_dsem else None))
        self.i = 0

    def next(self):
        it = self.items[self.i % len(self.items)]
        self.i += 1
        return it


_CONSTS = None


def host_consts():
    global _CONSTS
    if _CONSTS is not None:
        return _CONSTS
    bf = ml_dtypes.bfloat16
    c = {}
    c["ident"] = np.eye(128, dtype=np.float32).astype(bf)
    idx = np.arange(L, dtype=np.int64)
    lk = (idx[:, None] * idx[None, :]) % L
    ang = lk.astype(np.float64) * (2.0 * np.pi / L)
    c["dftC"] = (np.cos(ang) / np.sqrt(L)).astype(np.float32).astype(bf)
    c["dftS"] = (np.sin(ang) / np.sqrt(L)).astype(np.float32).astype(bf)
    del lk, ang
    i64 = np.arange(64, dtype=np.int64)
    a64 = ((i64[:, None] * i64[None, :]) % 64).astype(np.float64) * (2.0 * np.pi / 64)
    c64 = np.cos(a64) / 8.0
    s64 = -np.sin(a64) / 8.0
    cb = np.zeros((128, 128), np.float32)
    sb = np.zeros((128, 128), np.float32)
    for b in range(2):
        cb[b * 64:(b + 1) * 64, b * 64:(b + 1) * 64] = c64
        sb[b * 64:(b + 1) * 64, b * 64:(b + 1) * 64] = s64
    c["c64bd"] = cb.astype(bf)
    c["s64bd"] = sb.astype(bf)
    inv = (1.0 / (np.float32(10000.0) ** (np.arange(0, 64, 2, dtype=np.float32) / np.float32(64)))).astype(np.float32)
    angr = (np.arange(L, dtype=np.float32)[:, None] * inv[None, :]).astype(np.float32)
    cosr = np.cos(angr).astype(np.float32).T
    sinr = np.sin(angr).astype(np.float32).T
    c["cos4"] = np.ascontiguousarray(np.tile(cosr, (4, 1)))
    c["sin4"] = np.ascontiguousarray(np.tile(sinr, (4, 1)))
    wins = (2, 4, 8, 16)
    invw = np.zeros((128, 2), np.float32)
    elo = np.zeros((128, 2, 8), np.float32)
    ehi = np.zeros((128, 2, 8), np.float32)
    for a in range(2):
        for p in range(128):
            w = wins[2 * a + p // 64]
            invw[p, a] = 1.0 / w
            for j in range(8):
                i = j
                cnt = min(i + w // 2, L) - max(i - w // 2, 0)
                elo[p, a, j] = 1.0 / cnt
                i = L - 8 + j
                cnt = min(i + w // 2, L) - max(i - w // 2, 0)
                ehi[p, a, j] = 1.0 / cnt
    c["pinvw"] = invw
    c["pelo"] = elo
    c["pehi"] = ehi
    _CONSTS = c
    return c


def build_program(layers, first_from_input=True, final_norm=True, debug=False):
    nc = bass.Bass("TRN2", target_bir_lowering=False)
    dk = "ExternalOutput" if debug else "Internal"

    def din(name, shape, dt=F32):
        return nc.dram_tensor(name, list(shape), dt, kind="ExternalInput").ap()

    h0 = din("h0", [NS, L, DM])
    norm_w = din("norm_w", [DEPTH, DM])
    w_in = din("w_in", [DEPTH, DM, DIN])
    fourier_w = din("fourier_w", [DEPTH, 4, 64, 64])
    pool_w = din("pool_w", [DEPTH, 4, 64, 64])
    pool_scale = din("pool_scale", [DEPTH, 256])
    q_norm_w = din("q_norm_w", [DEPTH, 384])
    w_uq = din("w_uq", [DEPTH, 384, 768])
    kv_norm_w = din("kv_norm_w", [DEPTH, 256])
    w_ukv = din("w_ukv", [DEPTH, 256, 1024])
    w_out = din("w_out", [DEPTH, DM, DM])
    final_norm_w = din("final_norm_w", [DM])
    ident_d = din("ident", [128, 128], BF16)
    dftC = din("dftC", [L, L], BF16)
    dftS = din("dftS", [L, L], BF16)
    c64bd_d = din("c64bd", [128, 128], BF16)
    s64bd_d = din("s64bd", [128, 128], BF16)
    cos4_d = din("cos4", [128, L])
    sin4_d = din("sin4", [128, L])
    pinvw_d = din("pinvw", [128, 2])
    pelo_d = din("pelo", [128, 2, 8])
    pehi_d = din("pehi", [128, 2, 8])

    if final_norm:
        out_d = nc.dram_tensor("out", [NS, SEQ, DM], F32, kind="ExternalOutput").ap()
        hbuf = nc.dram_tensor("hbuf", [NS, L, DM], F32, kind="Internal").ap()
    else:
        out_d = None
        hbuf = nc.dram_tensor("hbuf", [NS, L, DM], F32, kind="ExternalOutput").ap()
    pcs_s = nc.dram_tensor("pcs_s", [NS, L, 512], BF16, kind=dk).ap()
    qT_s = nc.dram_tensor("qT_s", [NS, 768, L], BF16, kind=dk).ap()
    kT_s = nc.dram_tensor("kT_s", [NS, 576, L], BF16, kind=dk).ap()
    v_s = nc.dram_tensor("v_s", [NS, L, 512], BF16, kind=dk).ap()
    gTa_s = nc.dram_tensor("gTa_s", [NS, 512, L], BF16, kind=dk).ap()
    gTf_s = nc.dram_tensor("gTf_s", [NS, 256, L], BF16, kind=dk).ap()
    mixT_s = nc.dram_tensor("mixT_s", [NS, 1024, L], BF16, kind=dk).ap()

    with contextlib.ExitStack() as st:
        k = K(nc, st)

        def sb(name, shape, dt, stack=st):
            _UID[0] += 1
            return stack.enter_context(nc.sbuf_tensor(f"sb_{name}_{_UID[0]}", list(shape), dt))

        ident = sb("ident", [128, 128], BF16)
        ones_f = sb("ones_f", [128, 128], F32)
        eps_t = sb("eps_t", [128, 1], F32)
        Win_tm = sb("Win_tm", [128, 8, TM_COLS], BF16)
        Win_fm = sb("Win_fm", [128, 8, FM_COLS], BF16)
        Wuq = sb("Wuq", [128, 3, 1024], BF16)
        Wukv = sb("Wukv", [128, 2, 1024], BF16)
        Wout = sb("Wout", [128, 8, 1024], BF16)
        Pbd = sb("Pbd", [128, 2, 128], BF16)
        psc = sb("psc", [128, 2], F32)
        c64bd = sb("c64bd", [128, 128], BF16)
        s64bd = sb("s64bd", [128, 128], BF16)
        pinvw = sb("pinvw", [128, 2], F32)
        pelo = sb("pelo", [128, 2, 8], F32)
        pehi = sb("pehi", [128, 2, 8], F32)
        R_w = Res("weights")
        R_c = Res("consts")
        ps = [st.enter_context(nc.psum_tensor(f"ps{i}", [128, 512], F32)) for i in range(8)]
        R_ps = [Res(f"ps{i}") for i in range(8)]
        dconst = k.dsem("d_const")

        def ld_const(dst, src):
            k.op("sp", lambda e: e.dma_start(out=dst, in_=src), writes=[R_c], dma=dconst)

        ld_const(ident[:], ident_d)
        ld_const(c64bd[:], c64bd_d)
        ld_const(s64bd[:], s64bd_d)
        ld_const(pinvw[:], pinvw_d)
        ld_const(pelo[:], pelo_d)
        ld_const(pehi[:], pehi_d)
        k.op("pool", lambda e: e.memset(ones_f[:], 1.0), writes=[R_c])
        k.op("pool", lambda e: e.memset(eps_t[:], EPS), writes=[R_c])
        k.end_phase()

        def phase_P(l):
            with contextlib.ExitStack() as ls:
                stg = Ring(k, nc, ls, "wst", [128, 8, 640], F32, 2)
                nw = sb("nw", [128, 8], F32, ls)
                qnw = sb("qnw", [128, 3], F32, ls)
                kvnw = sb("kvnw", [128, 2], F32, ls)
                wf_st = sb("wf_st", [128, 2, 64], F32, ls)
                pw_st = sb("pw_st", [128, 2, 64], F32, ls)
                wf_b = sb("wf_b", [128, 2, 64], BF16, ls)
                Mbd = sb("Mbd", [128, 2, 2, 128], BF16, ls)
                Wf_bf = sb("Wf_bf", [128, 8, 256], BF16, ls)
                WfT = sb("WfT", [128, 2, 1024], BF16, ls)
                R_small = Res("small")
                R_wfb = Res("wfb")
                R_mbd = Res("mbd")
                R_wfbf = Res("wfbf")
                R_wft = Res("wft")
                dsm = k.dsem(f"d_small{l}")

                def ld_small(dst, src):
                    k.op("sp", lambda e: e.dma_start(out=dst, in_=src, allow_slow_non_contiguous=True),
                         writes=[R_small], dma=dsm)

                ld_small(nw[:], norm_w[l].rearrange("(c p) -> p c", p=128))
                ld_small(qnw[:], q_norm_w[l].rearrange("(c p) -> p c", p=128))
                ld_small(kvnw[:], kv_norm_w[l].rearrange("(c p) -> p c", p=128))
                ld_small(psc[:], pool_scale[l].rearrange("(a p) -> p a", p=128))
                ld_small(wf_st[:], fourier_w[l].rearrange("(a hh) c d -> (hh c) a d", hh=2))
                ld_small(pw_st[:], pool_w[l].rearrange("(a hh) c d -> (hh c) a d", hh=2))

                engs = ["act", "dve"]
                ei = [0]

                def scaled_cast(dst, src, sc_ap, mul2=None, eng=None):
                    if eng is None:
                        eng = engs[ei[0] % 2]
                        ei[0] += 1
                    if eng == "act":
                        if mul2 is None:
                            fn = lambda e: e.activation(out=dst, in_=src, func=AF.Copy, scale=sc_ap)
                        else:
                            eng = "dve"
                    if eng != "act":
                        if mul2 is None:
                            fn = lambda e: e.tensor_scalar(out=dst, in0=src, scalar1=sc_ap, scalar2=None, op0=ALU.mult)
                        else:
                            fn = lambda e: e.tensor_scalar(out=dst, in0=src, scalar1=sc_ap, scalar2=float(mul2),
                                                           op0=ALU.mult, op1=ALU.mult)
                    return eng, fn

                Wv = w_in[l].rearrange("(c p) n -> p c n", p=128)
                pieces = [(0, 512), (512, 1024), (1024, 1664), (1664, 2240)]
                for (c0, c1) in pieces:
                    t, R_t, ds = stg.next()
                    wcols = c1 - c0
                    k.op("sp", lambda e, t=t, c0=c0, c1=c1, wcols=wcols: e.dma_start(out=t[:, :, 0:wcols], in_=Wv[:, :, c0:c1]),
                         writes=[R_t], dma=ds)
                    jobs = []
                    if c0 == 0:
                        jobs.append((Wf_bf, 0, 0, 256, None, R_wfbf))
                        jobs.append((Win_fm, 0, 256, 256, None, R_w))
                    elif c0 == 512:
                        jobs.append((Win_fm, 1024, 512, 256, None, R_w))
                        jobs.append((Win_fm, 256, 768, 256, None, R_w))
                    elif c0 == 1024:
                        jobs.append((Win_tm, 512, 1024, 384, None, R_w))
                        jobs.append((Win_tm, 896, 1408, 256, None, R_w))
                    else:
                        jobs.append((Win_fm, 1280, 1664, 64, None, R_w))
                        jobs.append((Win_fm, 1344, 1696, 32, -1.0, R_w))
                        jobs.append((Win_fm, 1376, 1664, 32, None, R_w))
                        jobs.append((Win_fm, 512, 1728, 512, None, R_w))
                    for (dt_, d0, s0, ncol, mul2, rr) in jobs:
                        for c in range(8):
                            eng, fn = scaled_cast(dt_[:, c, d0:d0 + ncol], t[:, c, s0 - c0:s0 - c0 + ncol], nw[:, c:c + 1], mul2)
                            k.op(eng, fn, reads=[R_t, R_small], writes=[rr])

                t, R_t, ds = stg.next()
                tq = t[:].rearrange("p c n -> p (c n)")[:, 0:3 * 768].rearrange("p (c n) -> p c n", c=3)
                k.op("sp", lambda e: e.dma_start(out=tq, in_=w_uq[l].rearrange("(c p) n -> p c n", p=128)),
                     writes=[R_t], dma=ds)
                for c in range(3):
                    srcv = tq[:, c, :].rearrange("p (h e) -> p h e", h=4)
                    sc = qnw[:, c:c + 1]
                    eng, fn = scaled_cast(Wuq[:, c, 0:512].rearrange("p (h e) -> p h e", h=4), srcv[:, :, 0:128], sc, SCALE, "dve")
                    k.op(eng, fn, reads=[R_t, R_small], writes=[R_w])
                    eng, fn = scaled_cast(Wuq[:, c, 512:768].rearrange("p (h e) -> p h e", h=4), srcv[:, :, 128:192], sc, SCALE, "dve")
                    k.op(eng, fn, reads=[R_t, R_small], writes=[R_w])
                    rot = Wuq[:, c, 768:1024].rearrange("p (h e) -> p h e", h=4)
                    eng, fn = scaled_cast(rot[:, :, 0:32], srcv[:, :, 160:192], sc, -SCALE, "dve")
                    k.op(eng, fn, reads=[R_t, R_small], writes=[R_w])
                    eng, fn = scaled_cast(rot[:, :, 32:64], srcv[:, :, 128:160], sc, SCALE, "dve")
                    k.op(eng, fn, reads=[R_t, R_small], writes=[R_w])
                t, R_t, ds = stg.next()
                tkv = t[:].rearrange("p c n -> p (c n)")[:, 0:2 * 1024].rearrange("p (c n) -> p c n", c=2)
                k.op("sp", lambda e: e.dma_start(out=tkv, in_=w_ukv[l].rearrange("(c p) n -> p c n", p=128)),
                     writes=[R_t], dma=ds)
                for c in range(2):
                    srcv = tkv[:, c, :].rearrange("p (h e) -> p h e", h=4)
                    sc = kvnw[:, c:c + 1]
                    eng, fn = scaled_cast(Wukv[:, c, 0:512].rearrange("p (h e) -> p h e", h=4), srcv[:, :, 0:128], sc, None, "act")
                    k.op(eng, fn, reads=[R_t, R_small], writes=[R_w])
                    eng, fn = scaled_cast(Wukv[:, c, 512:1024].rearrange("p (h e) -> p h e", h=4), srcv[:, :, 128:256], sc, None, "dve")
                    k.op(eng, fn, reads=[R_t, R_small], writes=[R_w])
                Wo = w_out[l].rearrange("(c p) n -> p c n", p=128)
                for half in range(2):
                    t, R_t, ds = stg.next()
                    tv = t[:].rearrange("p c n -> p (c n)")[:, 0:4096].rearrange("p (c n) -> p c n", c=4)
                    k.op("sp", lambda e, tv=tv, half=half: e.dma_start(out=tv, in_=Wo[:, half * 4:(half + 1) * 4, :]),
                         writes=[R_t], dma=ds)
                    for c in range(4):
                        eng = ["act", "dve"][c % 2]
                        if eng == "act":
                            fn = lambda e, tv=tv, c=c, half=half: e.activation(out=Wout[:, half * 4 + c, :], in_=tv[:, c, :], func=AF.Copy)
                        else:
                            fn = lambda e, tv=tv, c=c, half=half: e.tensor_copy(out=Wout[:, half * 4 + c, :], in_=tv[:, c, :])
                        k.op(eng, fn, reads=[R_t], writes=[R_w])
                k.op("pool", lambda e: e.memset(Pbd[:], 0.0), writes=[R_w])
                for a in range(2):
                    k.op("pool", lambda e, a=a: e.tensor_copy(out=Pbd[0:64, a, 0:64], in_=pw_st[0:64, a, :]), reads=[R_small], writes=[R_w])
                    k.op("pool", lambda e, a=a: e.tensor_copy(out=Pbd[64:128, a, 64:128], in_=pw_st[64:128, a, :]), reads=[R_small], writes=[R_w])
                k.op("dve", lambda e: e.tensor_copy(out=wf_b[:], in_=wf_st[:]), reads=[R_small], writes=[R_wfb])
                k.op("pool", lambda e: e.memset(Mbd[:], 0.0), writes=[R_mbd])
                for a in range(2):
                    for cs in range(2):
                        blk = (a * 2 + cs) * 64
                        lh = c64bd if cs == 0 else s64bd
                        k.op("pe", lambda e, a=a, blk=blk, lh=lh: e.matmul(ps[0][:, blk:blk + 64], lhsT=lh[:], rhs=wf_b[:, a, :], start=True, stop=True),
                             reads=[R_c, R_wfb], writes=[R_ps[0]])
                for a in range(2):
                    for cs in range(2):
                        blk = (a * 2 + cs) * 64
                        k.op("dve", lambda e, a=a, cs=cs, blk=blk: e.tensor_copy(out=Mbd[0:64, a, cs, 0:64], in_=ps[0][0:64, blk:blk + 64]),
                             reads=[R_ps[0]], writes=[R_mbd])
                        k.op("dve", lambda e, a=a, cs=cs, blk=blk: e.tensor_copy(out=Mbd[64:128, a, cs, 64:128], in_=ps[0][64:128, blk:blk + 64]),
                             reads=[R_ps[0]], writes=[R_mbd])
                for a in range(2):
                    pt = ps[1 + a][:].bitcast(BF16)
                    for c in range(8):
                        k.op("pe", lambda e, a=a, c=c, pt=pt: e.transpose(out=pt[:, c * 128:(c + 1) * 128], in_=Wf_bf[:, c, a * 128:(a + 1) * 128], identity=ident[:]),
                             reads=[R_wfbf, R_c], writes=[R_ps[1 + a]])
                    k.op("act", lambda e, a=a, pt=pt: e.activation(out=WfT[:, a, :], in_=pt, func=AF.Copy), reads=[R_ps[1 + a]], writes=[R_wft])
                for c in range(8):
                    pb = 3 + (c % 2)
                    for cs in range(2):
                        for a in range(2):
                            col = cs * 256 + a * 128
                            k.op("pe", lambda e, c=c, cs=cs, a=a, col=col, pb=pb: e.matmul(ps[pb][:, col:col + 128], lhsT=WfT[:, a, c * 128:(c + 1) * 128],
                                                                                         rhs=Mbd[:, a, cs, :], start=True, stop=True),
                                 reads=[R_wft, R_mbd], writes=[R_ps[pb]])
                    k.op("dve", lambda e, c=c, pb=pb: e.tensor_copy(out=Win_tm[:, c, 0:512], in_=ps[pb][:, :]), reads=[R_ps[pb]], writes=[R_w])
            k.end_phase()

        def phase_A(l, s, h_src):
            with contextlib.ExitStack() as ls:
                Xp = sb("Xp", [128, 2, L + 16], BF16, ls)
                spg = sb("spg", [128, 2, L], BF16, ls)
                ls2 = ls.enter_context(contextlib.ExitStack())
                hring = Ring(k, nc, ls2, "ht", [128, 1024], F32, 2)
                xn_ring = Ring(k, nc, ls2, "xn", [128, 1024], BF16, 2, with_dsem=False)
                cn_ring = Ring(k, nc, ls2, "cn", [128, 640], BF16, 2, with_dsem=False)
                xnT_ring = [(sb(f"xnT{i}", [128, 8, 512], BF16, ls2), [Res() for _ in range(4)]) for i in range(2)]
                cnT_ring = [(sb(f"cnT{i}", [128, 5, 512], BF16, ls2), [Res() for _ in range(4)]) for i in range(2)]
                pcs_st = Ring(k, nc, ls2, "pcs_st", [128, 4, 512], BF16, 1)
                v_st = Ring(k, nc, ls2, "v_st", [128, 4, 512], BF16, 1)
                gf_st = Ring(k, nc, ls2, "gf_st", [128, 2, 512], BF16, 1)
                ga_st = Ring(k, nc, ls2, "ga_st", [128, 4, 512], BF16, 1)
                q_st = Ring(k, nc, ls2, "q_st", [128, 6, 512], BF16, 1)
                k_st = Ring(k, nc, ls2, "k_st", [128, 4, 512], BF16, 1)
                kr_st = Ring(k, nc, ls2, "kr_st", [64, 512], BF16, 1)
                cos_r = Ring(k, nc, ls2, "cos_r", [128, 512], F32, 2)
                sin_r = Ring(k, nc, ls2, "sin_r", [128, 512], F32, 2)
                t1_r = Ring(k, nc, ls2, "t1_r", [128, 512], F32, 2, with_dsem=False)
                t2_r = Ring(k, nc, ls2, "t2_r", [128, 512], F32, 2, with_dsem=False)
                junk_a = sb("junk_a", [128, 1024], BF16, ls2)
                ss_r = Ring(k, nc, ls2, "ss_r", [128, 4], F32, 4, with_dsem=False)
                R_xp = [Res("xp0"), Res("xp1")]
                R_spg = [Res("spg0"), Res("spg1")]
                PT = [0, 1]
                TMB = [2, 3, 4]
                VB = 5
                FMB = [6, 7]
                fmi = [0]
                pti = [0]

                k.op("pool", lambda e: e.memset(Xp[:, :, 0:8], 0.0), writes=R_xp)
                k.op("pool", lambda e: e.memset(Xp[:, :, L + 8:L + 16], 0.0), writes=R_xp)

                for sti, (t0, n) in enumerate(STILES):
                    xnT, R_xnT = xnT_ring[sti % 2]
                    cnT, R_cnT = cnT_ring[sti % 2]
                    pcs_t, R_pcs, d_pcs = pcs_st.next()
                    v_t, R_v, d_v = v_st.next()
                    cos_t, R_cos, d_cos = cos_r.next()
                    sin_t, R_sin, d_sin = sin_r.next()
                    k.op("sp", lambda e, cos_t=cos_t, t0=t0, n=n: e.dma_start(out=cos_t[:, 0:n], in_=cos4_d[:, t0:t0 + n]), writes=[R_cos], dma=d_cos)
                    k.op("sp", lambda e, sin_t=sin_t, t0=t0, n=n: e.dma_start(out=sin_t[:, 0:n], in_=sin4_d[:, t0:t0 + n]), writes=[R_sin], dma=d_sin)
                    tl = tiles_of(t0, n)
                    for j, (r0, nr) in enumerate(tl):
                        ht, R_ht, d_ht = hring.next()
                        k.op("sp", lambda e, ht=ht, nr=nr, r0=r0, t0=t0: e.dma_start(out=ht[0:nr, :], in_=h_src[s, t0 + r0:t0 + r0 + nr, :]), writes=[R_ht], dma=d_ht)
                        ss, R_ss, _ = ss_r.next()
                        k.op("act", lambda e, ht=ht, nr=nr, ss=ss: e.activation(out=junk_a[0:nr, :], in_=ht[0:nr, :], func=AF.Square, scale=float(DM ** -0.5), accum_out=ss[0:nr, 0:1]),
                             reads=[R_ht], writes=[R_ss])
                        k.op("act", lambda e, nr=nr, ss=ss: e.activation(out=ss[0:nr, 0:1], in_=ss[0:nr, 0:1], func=AF.Sqrt, bias=eps_t[0:nr, 0:1]),
                             reads=[R_ss], writes=[R_ss])
                        k.op("dve", lambda e, nr=nr, ss=ss: e.reciprocal(out=ss[0:nr, 0:1], in_=ss[0:nr, 0:1]), reads=[R_ss], writes=[R_ss])
                        xn, R_xn, _ = xn_ring.next()
                        k.op("dve", lambda e, nr=nr, ss=ss, xn=xn, ht=ht: e.tensor_scalar(out=xn[0:nr, :], in0=ht[0:nr, :], scalar1=ss[0:nr, 0:1], scalar2=None, op0=ALU.mult),
                             reads=[R_ht, R_ss], writes=[R_xn])
                        pb = PT[pti[0] % 2]
                        pti[0] += 1
                        ptv = ps[pb][:].bitcast(BF16)
                        for c in range(8):
                            k.op("pe", lambda e, c=c, nr=nr, xn=xn, ptv=ptv: e.transpose(out=ptv[:, c * 128:c * 128 + nr], in_=xn[0:nr, c * 128:(c + 1) * 128], identity=ident[0:nr, 0:nr]),
                                 reads=[R_xn, R_c], writes=[R_ps[pb]])
                        k.op("act", lambda e, nr=nr, r0=r0, xnT=xnT, ptv=ptv: e.activation(out=xnT[:, :, r0:r0 + nr], in_=ptv.rearrange("p (c t) -> p c t", c=8)[:, :, 0:nr], func=AF.Copy),
                             reads=[R_ps[pb]], writes=[R_xnT[j]])
                        tm_specs = [(TMB[0], 0, 512), (TMB[1], 512, 384), (TMB[2], 896, 256)]
                        for (bk, c0, ncol) in tm_specs:
                            for c in range(8):
                                k.op("pe", lambda e, bk=bk, c0=c0, ncol=ncol, c=c, nr=nr, r0=r0, xnT=xnT: e.matmul(ps[bk][0:nr, 0:ncol], lhsT=xnT[:, c, r0:r0 + nr], rhs=Win_tm[:, c, c0:c0 + ncol],
                                                                                                                  start=(c == 0), stop=(c == 7)),
                                     reads=[R_xnT[j], R_w], writes=[R_ps[bk]])
                        k.op("act", lambda e, nr=nr, j=j, pcs_t=pcs_t: e.activation(out=pcs_t[0:nr, j, :], in_=ps[TMB[0]][0:nr, :], func=AF.Copy),
                             reads=[R_ps[TMB[0]]], writes=[R_pcs])
                        k.op("act", lambda e, nr=nr, ss=ss: e.activation(out=junk_a[0:nr, 0:384], in_=ps[TMB[1]][0:nr, 0:384], func=AF.Square, scale=float(384 ** -0.5), accum_out=ss[0:nr, 1:2]),
                             reads=[R_ps[TMB[1]]], writes=[R_ss])
                        k.op("act", lambda e, nr=nr, ss=ss: e.activation(out=junk_a[0:nr, 0:256], in_=ps[TMB[2]][0:nr, 0:256], func=AF.Square, scale=float(256 ** -0.5), accum_out=ss[0:nr, 2:3]),
                             reads=[R_ps[TMB[2]]], writes=[R_ss])
                        k.op("act", lambda e, nr=nr, ss=ss: e.activation(out=ss[0:nr, 1:3], in_=ss[0:nr, 1:3], func=AF.Sqrt, bias=eps_t[0:nr, 0:1]),
                             reads=[R_ss], writes=[R_ss])
                        k.op("dve", lambda e, nr=nr, ss=ss: e.reciprocal(out=ss[0:nr, 1:3], in_=ss[0:nr, 1:3]), reads=[R_ss], writes=[R_ss])
                        cn, R_cn, _ = cn_ring.next()
                        k.op("dve", lambda e, nr=nr, ss=ss, cn=cn: e.tensor_scalar(out=cn[0:nr, 0:384], in0=ps[TMB[1]][0:nr, 0:384], scalar1=ss[0:nr, 1:2], scalar2=None, op0=ALU.mult),
                             reads=[R_ps[TMB[1]], R_ss], writes=[R_cn])
                        k.op("dve", lambda e, nr=nr, ss=ss, cn=cn: e.tensor_scalar(out=cn[0:nr, 384:640], in0=ps[TMB[2]][0:nr, 0:256], scalar1=ss[0:nr, 2:3], scalar2=None, op0=ALU.mult),
                             reads=[R_ps[TMB[2]], R_ss], writes=[R_cn])
                        pb = PT[pti[0] % 2]
                        pti[0] += 1
                        ptv = ps[pb][:].bitcast(BF16)
                        for c in range(5):
                            k.op("pe", lambda e, c=c, nr=nr, cn=cn, ptv=ptv: e.transpose(out=ptv[:, c * 128:c * 128 + nr], in_=cn[0:nr, c * 128:(c + 1) * 128], identity=ident[0:nr, 0:nr]),
                                 reads=[R_cn, R_c], writes=[R_ps[pb]])
                        k.op("dve", lambda e, nr=nr, r0=r0, cnT=cnT, ptv=ptv: e.tensor_copy(out=cnT[:, :, r0:r0 + nr], in_=ptv[:, 0:640].rearrange("p (c t) -> p c t", c=5)[:, :, 0:nr]),
                             reads=[R_ps[pb]], writes=[R_cnT[j]])
                        for c in range(2):
                            k.op("pe", lambda e, c=c, nr=nr, r0=r0, cnT=cnT: e.matmul(ps[VB][0:nr, :], lhsT=cnT[:, 3 + c, r0:r0 + nr], rhs=Wukv[:, c, 512:1024], start=(c == 0), stop=(c == 1)),
                                 reads=[R_cnT[j], R_w], writes=[R_ps[VB]])
                        k.op("dve", lambda e, nr=nr, j=j, v_t=v_t: e.tensor_copy(out=v_t[0:nr, j, :], in_=ps[VB][0:nr, :]), reads=[R_ps[VB]], writes=[R_v])
                    if n == 512:
                        k.op("sp", lambda e, pcs_t=pcs_t, t0=t0: e.dma_start(out=pcs_s[s, t0:t0 + 512, :].rearrange("(j p) n -> p j n", p=128), in_=pcs_t[:]), reads=[R_pcs], dma=d_pcs)
                        k.op("sp", lambda e, v_t=v_t, t0=t0: e.dma_start(out=v_s[s, t0:t0 + 512, :].rearrange("(j p) n -> p j n", p=128), in_=v_t[:]), reads=[R_v], dma=d_v)
                    else:
                        k.op("sp", lambda e, pcs_t=pcs_t, t0=t0, n=n: e.dma_start(out=pcs_s[s, t0:t0 + n, :], in_=pcs_t[0:n, 0, :]), reads=[R_pcs], dma=d_pcs)
                        k.op("sp", lambda e, v_t=v_t, t0=t0, n=n: e.dma_start(out=v_s[s, t0:t0 + n, :], in_=v_t[0:n, 0, :]), reads=[R_v], dma=d_v)

                    def fm_group(col0, m):
                        bk = FMB[fmi[0] % 2]
                        fmi[0] += 1
                        for c in range(8):
                            k.op("pe", lambda e, bk=bk, c=c, col0=col0, m=m: e.matmul(ps[bk][0:m, 0:n], lhsT=Win_fm[:, c, col0:col0 + m], rhs=xnT[:, c, 0:n], start=(c == 0), stop=(c == 7)),
                                 reads=R_xnT[0:len(tl)] + [R_w], writes=[R_ps[bk]])
                        return bk

                    gf_t, R_gf, d_gf = gf_st.next()
                    ga_t, R_ga, d_ga = ga_st.next()
                    for oc in range(2):
                        bk = fm_group(oc * 128, 128)
                        k.op("act", lambda e, bk=bk, oc=oc, gf_t=gf_t: e.activation(out=gf_t[:, oc, 0:n], in_=ps[bk][:, 0:n], func=AF.Silu), reads=[R_ps[bk]], writes=[R_gf])
                    k.op("sp", lambda e, gf_t=gf_t: e.dma_start(out=gTf_s[s, :, t0:t0 + n].rearrange("(c p) t -> p c t", p=128), in_=gf_t[:, :, 0:n]), reads=[R_gf], dma=d_gf)
                    for oc in range(2):
                        bk = fm_group(256 + oc * 128, 128)
                        k.op("act", lambda e, bk=bk, oc=oc: e.activation(out=spg[:, oc, t0:t0 + n], in_=ps[bk][:, 0:n], func=AF.Silu), reads=[R_ps[bk]], writes=[R_spg[oc]])
                    for oc in range(4):
                        bk = fm_group(512 + oc * 128, 128)
                        k.op("act", lambda e, bk=bk, oc=oc, ga_t=ga_t: e.activation(out=ga_t[:, oc, 0:n], in_=ps[bk][:, 0:n], func=AF.Silu), reads=[R_ps[bk]], writes=[R_ga])
                    k.op("sp", lambda e, ga_t=ga_t: e.dma_start(out=gTa_s[s, :, t0:t0 + n].rearrange("(c p) t -> p c t", p=128), in_=ga_t[:, :, 0:n]), reads=[R_ga], dma=d_ga)
                    for oc in range(2):
                        bk = fm_group(1024 + oc * 128, 128)
                        k.op("dve", lambda e, bk=bk, oc=oc: e.tensor_copy(out=Xp[:, oc, 8 + t0:8 + t0 + n], in_=ps[bk][:, 0:n]), reads=[R_ps[bk]], writes=[R_xp[oc]])
                    bkA = fm_group(1280, 64)
                    bkB = fm_group(1344, 64)
                    t1, R_t1, _ = t1_r.next()
                    t2, R_t2, _ = t2_r.next()
                    kr_t, R_kr, d_kr = kr_st.next()
                    k.op("dve", lambda e, t1=t1, bkA=bkA, cos_t=cos_t: e.tensor_tensor(out=t1[0:64, 0:n], in0=ps[bkA][0:64, 0:n], in1=cos_t[0:64, 0:n], op=ALU.mult),
                         reads=[R_ps[bkA], R_cos], writes=[R_t1])
                    k.op("dve", lambda e, t2=t2, bkB=bkB, sin_t=sin_t: e.tensor_tensor(out=t2[0:64, 0:n], in0=ps[bkB][0:64, 0:n], in1=sin_t[0:64, 0:n], op=ALU.mult),
                         reads=[R_ps[bkB], R_sin], writes=[R_t2])
                    k.op("pool", lambda e, t1=t1, t2=t2, kr_t=kr_t: e.tensor_tensor(out=kr_t[:, 0:n], in0=t1[0:64, 0:n], in1=t2[0:64, 0:n], op=ALU.add),
                         reads=[R_t1, R_t2], writes=[R_kr])
                    k.op("sp", lambda e, kr_t=kr_t: e.dma_start(out=kT_s[s, 512:576, t0:t0 + n], in_=kr_t[:, 0:n]), reads=[R_kr], dma=d_kr)

                    def cn_group(Wt, nk, kc0, col0):
                        bk = FMB[fmi[0] % 2]
                        fmi[0] += 1
                        for c in range(nk):
                            k.op("pe", lambda e, bk=bk, c=c: e.matmul(ps[bk][:, 0:n], lhsT=Wt[:, c, col0:col0 + 128], rhs=cnT[:, kc0 + c, 0:n], start=(c == 0), stop=(c == nk - 1)),
                                 reads=R_cnT[0:len(tl)] + [R_w], writes=[R_ps[bk]])
                        return bk

                    q_t, R_q, d_q = q_st.next()
                    for oc in range(4):
                        bk = cn_group(Wuq, 3, 0, oc * 128)
                        if oc % 2 == 0:
                            k.op("act", lambda e, bk=bk, oc=oc, q_t=q_t: e.activation(out=q_t[:, oc, 0:n], in_=ps[bk][:, 0:n], func=AF.Copy), reads=[R_ps[bk]], writes=[R_q])
                        else:
                            k.op("dve", lambda e, bk=bk, oc=oc, q_t=q_t: e.tensor_copy(out=q_t[:, oc, 0:n], in_=ps[bk][:, 0:n]), reads=[R_ps[bk]], writes=[R_q])
                    for a in range(2):
                        bkA = cn_group(Wuq, 3, 0, 512 + a * 128)
                        bkB = cn_group(Wuq, 3, 0, 768 + a * 128)
                        t1, R_t1, _ = t1_r.next()
                        t2, R_t2, _ = t2_r.next()
                        k.op("dve", lambda e, t1=t1, bkA=bkA: e.tensor_tensor(out=t1[:, 0:n], in0=ps[bkA][:, 0:n], in1=cos_t[:, 0:n], op=ALU.mult),
                             reads=[R_ps[bkA], R_cos], writes=[R_t1])
                        k.op("dve", lambda e, t2=t2, bkB=bkB: e.tensor_tensor(out=t2[:, 0:n], in0=ps[bkB][:, 0:n], in1=sin_t[:, 0:n], op=ALU.mult),
                             reads=[R_ps[bkB], R_sin], writes=[R_t2])
                        k.op("pool", lambda e, t1=t1, t2=t2, a=a, q_t=q_t: e.tensor_tensor(out=q_t[:, 4 + a, 0:n], in0=t1[:, 0:n], in1=t2[:, 0:n], op=ALU.add),
                             reads=[R_t1, R_t2], writes=[R_q])
                    k.op("sp", lambda e, q_t=q_t: e.dma_start(out=qT_s[s, :, t0:t0 + n].rearrange("(c p) t -> p c t", p=128), in_=q_t[:, :, 0:n]), reads=[R_q], dma=d_q)
                    k_t, R_k, d_k = k_st.next()
                    for oc in range(4):
                        bk = cn_group(Wukv, 2, 3, oc * 128)
                        if oc % 2 == 0:
                            k.op("act", lambda e, bk=bk, oc=oc, k_t=k_t: e.activation(out=k_t[:, oc, 0:n], in_=ps[bk][:, 0:n], func=AF.Copy), reads=[R_ps[bk]], writes=[R_k])
                        else:
                            k.op("dve", lambda e, bk=bk, oc=oc, k_t=k_t: e.tensor_copy(out=k_t[:, oc, 0:n], in_=ps[bk][:, 0:n]), reads=[R_ps[bk]], writes=[R_k])
                    k.op("sp", lambda e, k_t=k_t: e.dma_start(out=kT_s[s, 0:512, t0:t0 + n].rearrange("(c p) t -> p c t", p=128), in_=k_t[:, :, 0:n]), reads=[R_k], dma=d_k)

                k.barrier()
                ls2.close()
                CB = 1028
                sA = Ring(k, nc, ls, "poolA", [128, CB + 16], F32, 2, with_dsem=False)
                sB = Ring(k, nc, ls, "poolB", [128, CB + 16], F32, 2, with_dsem=False)
                pm_st = Ring(k, nc, ls, "pm_st", [128, CB], BF16, 2)
                ym = Ring(k, nc, ls, "ym", [128, CB], BF16, 2, with_dsem=False)
                for a in range(2):
                    for b in range(4):
                        o = b * CB
                        A_, R_A, _ = sA.next()
                        B_, R_B, _ = sB.next()
                        X = Xp[:, a, o:o + CB + 16]
                        W_ = CB + 16
                        eng1 = "pool" if (b % 2 == 0) else "dve"
                        eng2 = "dve"
                        k.op(eng1, lambda e, A_=A_, X=X, W_=W_: e.tensor_tensor(out=A_[:, 1:W_], in0=X[:, 0:W_ - 1], in1=X[:, 1:W_], op=ALU.add), reads=[R_xp[a]], writes=[R_A])
                        k.op(eng1, lambda e, A_=A_, B_=B_, W_=W_: e.tensor_tensor(out=B_[64:128, 2:W_ - 1] if a == 0 else B_[:, 2:W_ - 1],
                                                                               in0=A_[64:128, 1:W_ - 2] if a == 0 else A_[:, 1:W_ - 2],
                                                                               in1=A_[64:128, 3:W_] if a == 0 else A_[:, 3:W_], op=ALU.add), reads=[R_A], writes=[R_B])
                        if a == 0:
                            k.op(eng1, lambda e, A_=A_, B_=B_, W_=W_: e.tensor_copy(out=B_[0:64, 2:W_ - 1], in_=A_[0:64, 2:W_ - 1]), reads=[R_A], writes=[R_B])
                            Sfin, R_S = B_, R_B
                        else:
                            k.op(eng1, lambda e, A_=A_, B_=B_, W_=W_: e.tensor_tensor(out=A_[:, 4:W_ - 3], in0=B_[:, 2:W_ - 5], in1=B_[:, 6:W_ - 1], op=ALU.add), reads=[R_B], writes=[R_A])
                            k.op(eng1, lambda e, A_=A_, B_=B_, W_=W_: e.tensor_tensor(out=B_[64:128, 8:W_ - 7], in0=A_[64:128, 4:W_ - 11], in1=A_[64:128, 12:W_ - 3], op=ALU.add), reads=[R_A], writes=[R_B])
                            k.op(eng1, lambda e, A_=A_, B_=B_, W_=W_: e.tensor_copy(out=B_[0:64, 8:W_ - 7], in_=A_[0:64, 8:W_ - 7]), reads=[R_A], writes=[R_B])
                            Sfin, R_S = B_, R_B
                        y_, R_y, _ = ym.next()
                        k.op(eng2, lambda e, Sfin=Sfin, X=X, y_=y_, a=a: e.scalar_tensor_tensor(out=y_[:, 0:CB], in0=Sfin[:, 8:8 + CB], scalar=pinvw[:, a:a + 1], in1=X[:, 8:8 + CB],
                                                                                              op0=ALU.mult, op1=ALU.subtract), reads=[R_S, R_xp[a], R_c], writes=[R_y])
                        if b == 0:
                            k.op(eng2, lambda e, Sfin=Sfin, a=a: e.tensor_tensor(out=Sfin[:, 8:16], in0=Sfin[:, 8:16], in1=pelo[:, a, :], op=ALU.mult), reads=[R_S, R_c, R_y], writes=[R_S])
                            k.op(eng2, lambda e, Sfin=Sfin, X=X, y_=y_: e.tensor_tensor(out=y_[:, 0:8], in0=Sfin[:, 8:16], in1=X[:, 8:16], op=ALU.subtract), reads=[R_S, R_xp[a]], writes=[R_y])
                        if b == 3:
                            k.op(eng2, lambda e, Sfin=Sfin, a=a: e.tensor_tensor(out=Sfin[:, CB:CB + 8], in0=Sfin[:, CB:CB + 8], in1=pehi[:, a, :], op=ALU.mult), reads=[R_S, R_c, R_y], writes=[R_S])
                            k.op(eng2, lambda e, Sfin=Sfin, X=X, y_=y_: e.tensor_tensor(out=y_[:, CB - 8:CB], in0=Sfin[:, CB:CB + 8], in1=X[:, CB:CB + 8], op=ALU.subtract), reads=[R_S, R_xp[a]], writes=[R_y])
                        pm_t, R_pm, d_pm = pm_st.next()
                        for (c0, cw) in [(0, 512), (512, 512), (1024, CB - 1024)]:
                            bk = FMB[fmi[0] % 2]
                            fmi[0] += 1
                            k.op("pe", lambda e, bk=bk, y_=y_, c0=c0, cw=cw, a=a: e.matmul(ps[bk][:, 0:cw], lhsT=Pbd[:, a, :], rhs=y_[:, c0:c0 + cw], start=True, stop=True),
                                 reads=[R_y, R_w], writes=[R_ps[bk]])
                            k.op("dve", lambda e, bk=bk, c0=c0, cw=cw, a=a, pm_t=pm_t, o=o: e.scalar_tensor_tensor(out=pm_t[:, c0:c0 + cw], in0=ps[bk][:, 0:cw], scalar=psc[:, a:a + 1],
                                                                                                             in1=spg[:, a, o + c0:o + c0 + cw], op0=ALU.mult, op1=ALU.mult),
                                 reads=[R_ps[bk], R_spg[a], R_w], writes=[R_pm])
                        k.op("sp", lambda e, pm_t=pm_t, a=a, o=o: e.dma_start(out=mixT_s[s, 256 + a * 128:256 + (a + 1) * 128, o:o + CB], in_=pm_t[:, :]), reads=[R_pm], dma=d_pm)
            k.end_phase()

        def phase_B(l, s):
            with contextlib.ExitStack() as ls:
                kT = sb("kT", [128, 4, L], BF16, ls)
                krT = sb("krT", [64, L], BF16, ls)
                V = sb("V", [128, NT, 512], BF16, ls)
                R_kv = Res("kv")
                dkv = k.dsem(f"d_kv{l}_{s}")
                k.op("sp", lambda e: e.dma_start(out=kT[:], in_=kT_s[s, 0:512, :].rearrange("(c p) t -> p c t", p=128)), writes=[R_kv], dma=dkv)
                k.op("sp", lambda e: e.dma_start(out=krT[:], in_=kT_s[s, 512:576, :]), writes=[R_kv], dma=dkv)
                k.op("sp", lambda e: e.dma_start(out=V[:, 0:32, :], in_=v_s[s, 0:4096, :].rearrange("(j p) n -> p j n", p=128)), writes=[R_kv], dma=dkv)
                k.op("sp", lambda e: e.dma_start(out=V[0:16, 32, :], in_=v_s[s, 4096:L, :]), writes=[R_kv], dma=dkv)
                q_r = Ring(k, nc, ls, "qb", [128, 4, 512], BF16, 2)
                qr_r = Ring(k, nc, ls, "qrb", [64, 4, 512], BF16, 2)
                ga_r = Ring(k, nc, ls, "gab", [128, 4, 512], BF16, 2)
                mx_r = Ring(k, nc, ls, "mxb", [128, 4, 512], BF16, 2)
                pt_r = Ring(k, nc, ls, "ptb", [128, 512], BF16, 4, with_dsem=False)
                acc0_r = Ring(k, nc, ls, "acc0", [128, 512], F32, 2, with_dsem=False)
                acc1_r = Ring(k, nc, ls, "acc1", [128, 512], F32, 2, with_dsem=False)
                rinv_r = Ring(k, nc, ls, "rinv", [128, 512], F32, 2, with_dsem=False)
                o1_r = Ring(k, nc, ls, "o1", [128, 512], F32, 2, with_dsem=False)
                SB_ = [0, 1, 2, 3]
                OB = [4, 5]
                SUMB = 6
                sbi = [0]
                obi = [0]
                LOOK = 2
                for (t0, n) in STILES:
                    q_t, R_q, d_q = q_r.next()
                    qr_t, R_qr, d_qr = qr_r.next()
                    ga_t, R_ga, d_ga = ga_r.next()
                    mx_t, R_mx, d_mx = mx_r.next()
                    k.op("sp", lambda e, q_t=q_t: e.dma_start(out=q_t[:, :, 0:n], in_=qT_s[s, 0:512, t0:t0 + n].rearrange("(c p) t -> p c t", p=128)), writes=[R_q], dma=d_q)
                    k.op("sp", lambda e, qr_t=qr_t: e.dma_start(out=qr_t[:, :, 0:n], in_=qT_s[s, 512:768, t0:t0 + n].rearrange("(c p) t -> p c t", p=64)), writes=[R_qr], dma=d_qr)
                    k.op("sp", lambda e, ga_t=ga_t: e.dma_start(out=ga_t[:, :, 0:n], in_=gTa_s[s, :, t0:t0 + n].rearrange("(c p) t -> p c t", p=128)), writes=[R_ga], dma=d_ga)
                    for h in range(4):
                        ob = OB[obi[0] % 2]
                        obi[0] += 1
                        acc0, R_a0, _ = acc0_r.next()
                        acc1, R_a1, _ = acc1_r.next()
                        pend = []

                        def qk(kt):
                            kn = 128 if kt < 32 else 16
                            bk = SB_[sbi[0] % 4]
                            sbi[0] += 1
                            k.op("pe", lambda e, bk=bk, kt=kt, kn=kn: e.matmul(ps[bk][0:kn, 0:n], lhsT=kT[:, h, kt * 128:kt * 128 + kn], rhs=q_t[:, h, 0:n], start=True, stop=False),
                                 reads=[R_kv, R_q], writes=[R_ps[bk]])
                            k.op("pe", lambda e, bk=bk, kt=kt, kn=kn: e.matmul(ps[bk][0:kn, 0:n], lhsT=krT[0:64, kt * 128:kt * 128 + kn], rhs=qr_t[0:64, h, 0:n], start=False, stop=True),
                                 reads=[R_kv, R_qr], writes=[R_ps[bk]])
                            p_t, R_p, _ = pt_r.next()
                            k.op("act", lambda e, bk=bk, kn=kn, p_t=p_t: e.activation(out=p_t[0:kn, 0:n], in_=ps[bk][0:kn, 0:n], func=AF.Exp), reads=[R_ps[bk]], writes=[R_p])
                            return (kt, kn, p_t, R_p)

                        def pv(item):
                            kt, kn, p_t, R_p = item
                            k.op("pe", lambda e, kt=kt, kn=kn, p_t=p_t: e.matmul(ps[ob][:, 0:n], lhsT=V[0:kn, kt, h * 128:(h + 1) * 128], rhs=p_t[0:kn, 0:n], start=(kt == 0), stop=(kt == NT - 1)),
                                 reads=[R_kv, R_p], writes=[R_ps[ob]])
                            eng = "dve" if kt % 2 == 0 else "pool"
                            acc, R_acc = (acc0, R_a0) if kt % 2 == 0 else (acc1, R_a1)
                            if kt < 2:
                                k.op(eng, lambda e, acc=acc, p_t=p_t: e.tensor_copy(out=acc[:, 0:n], in_=p_t[:, 0:n]), reads=[R_p], writes=[R_acc])
                            else:
                                k.op(eng, lambda e, acc=acc, p_t=p_t, kn=kn: e.tensor_tensor(out=acc[0:kn, 0:n], in0=acc[0:kn, 0:n], in1=p_t[0:kn, 0:n], op=ALU.add), reads=[R_p, R_acc], writes=[R_acc])

                        for kt in range(NT):
                            pend.append(qk(kt))
                            if len(pend) > LOOK:
                                pv(pend.pop(0))
                        while pend:
                            pv(pend.pop(0))
                        k.op("pool", lambda e, acc0=acc0, acc1=acc1: e.tensor_tensor(out=acc0[:, 0:n], in0=acc0[:, 0:n], in1=acc1[:, 0:n], op=ALU.add), reads=[R_a1, R_a0], writes=[R_a0])
                        k.op("pe", lambda e, acc0=acc0: e.matmul(ps[SUMB][:, 0:n], lhsT=ones_f[:], rhs=acc0[:, 0:n], start=True, stop=True), reads=[R_a0, R_c], writes=[R_ps[SUMB]])
                        rinv, R_ri, _ = rinv_r.next()
                        k.op("dve", lambda e, rinv=rinv: e.reciprocal(out=rinv[:, 0:n], in_=ps[SUMB][:, 0:n]), reads=[R_ps[SUMB]], writes=[R_ri])
                        o1, R_o1, _ = o1_r.next()
                        k.op("dve", lambda e, o1=o1, rinv=rinv, ob=ob: e.tensor_tensor(out=o1[:, 0:n], in0=ps[ob][:, 0:n], in1=rinv[:, 0:n], op=ALU.mult), reads=[R_ps[ob], R_ri], writes=[R_o1])
                        k.op("pool", lambda e, o1=o1, h=h, mx_t=mx_t, ga_t=ga_t: e.tensor_tensor(out=mx_t[:, h, 0:n], in0=o1[:, 0:n], in1=ga_t[:, h, 0:n], op=ALU.mult), reads=[R_o1, R_ga], writes=[R_mx])
                    k.op("sp", lambda e, mx_t=mx_t: e.dma_start(out=mixT_s[s, 512:1024, t0:t0 + n].rearrange("(c p) t -> p c t", p=128), in_=mx_t[:, :, 0:n]), reads=[R_mx], dma=d_mx)
            k.end_phase()

        def phase_C(l):
            with contextlib.ExitStack() as ls:
                Pc = sb("Pc", [128, NS, NT, 512], BF16, ls)
                R_pc = Res("pc")
                dpc = k.dsem(f"d_pc{l}")
                for s in range(NS):
                    k.op("sp", lambda e, s=s: e.dma_start(out=Pc[:, s, 0:32, :], in_=pcs_s[s, 0:4096, :].rearrange("(j p) n -> p j n", p=128)), writes=[R_pc], dma=dpc)
                    k.op("sp", lambda e, s=s: e.dma_start(out=Pc[0:16, s, 32, :], in_=pcs_s[s, 4096:L, :]), writes=[R_pc], dma=dpc)
                GC = 8
                c_r = Ring(k, nc, ls, "dC", [128, GC, 512], BF16, 3)
                s_r = Ring(k, nc, ls, "dS", [128, GC, 512], BF16, 3)
                gf_r = Ring(k, nc, ls, "gfl", [128, 512], BF16, 4)
                mf_r = Ring(k, nc, ls, "mfl", [128, 512], BF16, 4)
                groups = [(0, 8), (8, 8), (16, 8), (24, 8), (32, 1)]
                for ki, (k0, n) in enumerate(STILES):
                    banks = [[(ki % 2) * 4 + s2 * 2 + a for a in range(2)] for s2 in range(NS)]
                    first = True
                    for (g0, gn) in groups:
                        ct, R_ct, d_ct = c_r.next()
                        st_, R_st, d_st = s_r.next()
                        if gn == 8:
                            k.op("sp", lambda e, ct=ct, g0=g0: e.dma_start(out=ct[:, :, 0:n], in_=dftC[g0 * 128:(g0 + 8) * 128, k0:k0 + n].rearrange("(c p) k -> p c k", p=128)), writes=[R_ct], dma=d_ct)
                            k.op("sp", lambda e, st_=st_, g0=g0: e.dma_start(out=st_[:, :, 0:n], in_=dftS[g0 * 128:(g0 + 8) * 128, k0:k0 + n].rearrange("(c p) k -> p c k", p=128)), writes=[R_st], dma=d_st)
                        else:
                            k.op("sp", lambda e, ct=ct: e.dma_start(out=ct[0:16, 0, 0:n], in_=dftC[4096:L, k0:k0 + n]), writes=[R_ct], dma=d_ct)
                            k.op("sp", lambda e, st_=st_: e.dma_start(out=st_[0:16, 0, 0:n], in_=dftS[4096:L, k0:k0 + n]), writes=[R_st], dma=d_st)
                        for ci in range(gn):
                            ch = g0 + ci
                            ln = 128 if ch < 32 else 16
                            for cs, (xt, R_x) in enumerate([(ct, R_ct), (st_, R_st)]):
                                last = (ch == NT - 1) and (cs == 1)
                                for s2 in range(NS):
                                    for a in range(2):
                                        bk = banks[s2][a]
                                        k.op("pe", lambda e, bk=bk, s2=s2, a=a, ch=ch, ln=ln, cs=cs, xt=xt, ci=ci, first=first, last=last:
                                             e.matmul(ps[bk][:, 0:n], lhsT=Pc[0:ln, s2, ch, cs * 256 + a * 128:cs * 256 + (a + 1) * 128], rhs=xt[0:ln, ci, 0:n], start=first, stop=last),
                                             reads=[R_pc, R_x], writes=[R_ps[bk]])
                                first = False
                    for s2 in range(NS):
                        for a in range(2):
                            bk = banks[s2][a]
                            gf, R_gf, d_gf = gf_r.next()
                            mf, R_mf, d_mf = mf_r.next()
                            k.op("sp", lambda e, gf=gf, s2=s2, a=a: e.dma_start(out=gf[:, 0:n], in_=gTf_s[s2, a * 128:(a + 1) * 128, k0:k0 + n]), writes=[R_gf], dma=d_gf)
                            k.op("dve", lambda e, gf=gf, mf=mf, bk=bk: e.tensor_tensor(out=mf[:, 0:n], in0=ps[bk][:, 0:n], in1=gf[:, 0:n], op=ALU.mult), reads=[R_ps[bk], R_gf], writes=[R_mf])
                            k.op("sp", lambda e, mf=mf, s2=s2, a=a: e.dma_start(out=mixT_s[s2, a * 128:(a + 1) * 128, k0:k0 + n], in_=mf[:, 0:n]), reads=[R_mf], dma=d_mf)
            k.end_phase()

        def phase_D(l, s, h_src, last):
            with contextlib.ExitStack() as ls:
                mx_r = Ring(k, nc, ls, "dmx", [128, 8, 512], BF16, 2)
                h_r = Ring(k, nc, ls, "dh", [128, 1024], F32, 3)
                o_r = Ring(k, nc, ls, "do", [128, 1024], F32, 3)
                ss_r = Ring(k, nc, ls, "dss", [128, 1], F32, 4, with_dsem=False)
                junk = sb("djunk", [128, 1024], BF16, ls)
                R_fn = Res("fnw")
                if last:
                    fnw = sb("fnw", [128, 1024], F32, ls)
                    dfn = k.dsem(f"d_fn{s}")
                    k.op("sp", lambda e: e.dma_start(out=fnw[:], in_=final_norm_w.partition_broadcast(128)), writes=[R_fn], dma=dfn)
                OBK = [0, 1, 2, 3]
                obi = [0]
                for (t0, n) in STILES:
                    mx, R_mx, d_mx = mx_r.next()
                    k.op("sp", lambda e, mx=mx: e.dma_start(out=mx[:, :, 0:n], in_=mixT_s[s, :, t0:t0 + n].rearrange("(c p) t -> p c t", p=128)), writes=[R_mx], dma=d_mx)
                    for j, (r0, nr) in enumerate(tiles_of(t0, n)):
                        ht, R_ht, d_ht = h_r.next()
                        ot, R_ot, d_ot = o_r.next()
                        k.op("sp", lambda e, ht=ht, nr=nr, r0=r0: e.dma_start(out=ht[0:nr, :], in_=h_src[s, t0 + r0:t0 + r0 + nr, :]), writes=[R_ht], dma=d_ht)
                        for half in range(2):
                            bk = OBK[obi[0] % 4]
                            obi[0] += 1
                            for c in range(8):
                                k.op("pe", lambda e, bk=bk, c=c, half=half, mx=mx, r0=r0, nr=nr: e.matmul(ps[bk][0:nr, :], lhsT=mx[:, c, r0:r0 + nr], rhs=Wout[:, c, half * 512:(half + 1) * 512],
                                                                                                     start=(c == 0), stop=(c == 7)), reads=[R_mx, R_w], writes=[R_ps[bk]])
                            k.op("dve", lambda e, bk=bk, half=half, ht=ht, ot=ot, nr=nr: e.tensor_tensor(out=ot[0:nr, half * 512:(half + 1) * 512], in0=ps[bk][0:nr, :], in1=ht[0:nr, half * 512:(half + 1) * 512], op=ALU.add),
                                 reads=[R_ps[bk], R_ht], writes=[R_ot])
                        if not last:
                            k.op("sp", lambda e, ot=ot, nr=nr, r0=r0: e.dma_start(out=hbuf[s, t0 + r0:t0 + r0 + nr, :], in_=ot[0:nr, :]), reads=[R_ot], dma=d_ot)
                        else:
                            ss, R_ss, _ = ss_r.next()
                            k.op("act", lambda e, ot=ot, nr=nr, ss=ss: e.activation(out=junk[0:nr, :], in_=ot[0:nr, :], func=AF.Square, scale=float(DM ** -0.5), accum_out=ss[0:nr, 0:1]), reads=[R_ot], writes=[R_ss])
                            k.op("act", lambda e, nr=nr, ss=ss: e.activation(out=ss[0:nr, 0:1], in_=ss[0:nr, 0:1], func=AF.Sqrt, bias=eps_t[0:nr, 0:1]), reads=[R_ss], writes=[R_ss])
                            k.op("dve", lambda e, nr=nr, ss=ss: e.reciprocal(out=ss[0:nr, 0:1], in_=ss[0:nr, 0:1]), reads=[R_ss], writes=[R_ss])
                            k.op("dve", lambda e, ot=ot, nr=nr, ss=ss: e.scalar_tensor_tensor(out=ot[0:nr, :], in0=ot[0:nr, :], scalar=ss[0:nr, 0:1], in1=fnw[0:nr, :], op0=ALU.mult, op1=ALU.mult),
                                 reads=[R_ot, R_ss, R_fn], writes=[R_ot])
                            g0 = t0 + r0
                            lo = max(g0, NMETA)
                            hi = g0 + nr
                            if hi > lo:
                                k.op("sp", lambda e, ot=ot, lo=lo, hi=hi, g0=g0: e.dma_start(out=out_d[s, lo - NMETA:hi - NMETA, :], in_=ot[lo - g0:hi - g0, :]), reads=[R_ot], dma=d_ot)
            k.end_phase()

        for li, l in enumerate(layers):
            h_src = h0 if (li == 0 and first_from_input) else hbuf
            phase_P(l)
            for s in range(NS):
                phase_A(l, s, h_src)
            for s in range(NS):
                phase_B(l, s)
            phase_C(l)
            last = final_norm and (li == len(layers) - 1)
            for s in range(NS):
                phase_D(l, s, h_src, last)
        k.emit()
        build_program.nops = k.nops
    return nc


_PROG = {}


def get_prog(key, **kw):
    if key not in _PROG:
        _PROG[key] = build_program(**kw)
    return _PROG[key]


def make_in_maps(inputs, h0_full):
    c = host_consts()
    shared = {
        "norm_w": inputs["norm_w"], "w_in": inputs["w_in"], "fourier_w": inputs["fourier_w"], "pool_w": inputs["pool_w"],
        "pool_scale": inputs["pool_scale"], "q_norm_w": inputs["q_norm_w"], "w_uq": inputs["w_uq"], "kv_norm_w": inputs["kv_norm_w"],
        "w_ukv": inputs["w_ukv"], "w_out": inputs["w_out"], "final_norm_w": inputs["final_norm_w"],
    }
    shared = {k_: np.ascontiguousarray(np.asarray(v, dtype=np.float32)) for k_, v in shared.items()}
    shared.update(c)
    maps = []
    for i in range(NCORES):
        m = dict(shared)
        m["h0"] = h0_full[i * NS:(i + 1) * NS]
        maps.append(m)
    return maps


def kernel(x, meta_tokens, norm_w, w_in, fourier_w, pool_w, pool_scale, q_norm_w, w_uq, kv_norm_w, w_ukv, w_out, final_norm_w):
    inputs = dict(norm_w=norm_w, w_in=w_in, fourier_w=fourier_w, pool_w=pool_w, pool_scale=pool_scale, q_norm_w=q_norm_w,
                  w_uq=w_uq, kv_norm_w=kv_norm_w, w_ukv=w_ukv, w_out=w_out, final_norm_w=final_norm_w)
    x = np.asarray(x, dtype=np.float32)
    B = x.shape[0]
    meta = np.asarray(meta_tokens, dtype=np.float32)
    h0 = np.concatenate([np.broadcast_to(meta[None], (B, NMETA, DM)), x], axis=1)
    h0 = np.ascontiguousarray(h0)
    nc = get_prog("full", layers=list(range(DEPTH)))
    maps = make_in_maps(inputs, h0)
    res = run_bass_kernel_spmd(nc, maps, core_ids=list(range(NCORES)))
    out = np.concatenate([np.asarray(r["out"]) for r in res.results], axis=0)
    return out.astype(np.float32)
```

```python
import contextlib
import numpy as np
import ml_dtypes
import concourse.bass as bass
import concourse.mybir as mybir
from concourse.bass_utils import run_bass_kernel_spmd

F32 = mybir.dt.float32
BF16 = mybir.dt.bfloat16
AF = mybir.ActivationFunctionType
ALU = mybir.AluOpType

NCORES = 8
NS = 2
DM = 1024
SEQ = 4096
NMETA = 16
L = SEQ + NMETA
DEPTH = 4
DIN = 2240
EPS = 1e-6
NT = 33
TM_COLS = 1152
FM_COLS = 1408
SCALE = 192 ** -0.5

STILES = [(i * 512, 512) for i in range(8)] + [(4096, 16)]


def tiles_of(t0, n):
    if n == 512:
        return [(j * 128, 128) for j in range(4)]
    return [(0, n)]


class SemObj:
    def __init__(self, handle, name):
        self.h = handle
        self.name = name
        self.count = 0


class Res:
    __slots__ = ("name", "w", "r")

    def __init__(self, name=""):
        self.name = name
        self.w = None
        self.r = {}


class _Rec:
    def __init__(self):
        self.call = None

    def __getattr__(self, name):
        def f(*a, **kw):
            self.call = (name, a, kw)
            return None
        return f


class K:
    ENGS = ("pe", "act", "dve", "pool", "sp")

    def __init__(self, nc, stack):
        self.nc = nc
        self.stack = stack
        self.ops = {e: [] for e in self.ENGS}
        self.esem = {e: SemObj(stack.enter_context(nc.semaphore("s_" + e)), "s_" + e) for e in self.ENGS}
        self.waited = {e: {} for e in self.ENGS}
        self.dsems = []
        self.free_dsems = []
        self.phase_dsems = []
        self.nops = 0

    def dsem(self, name):
        if self.free_dsems:
            s = self.free_dsems.pop()
        else:
            s = SemObj(self.stack.enter_context(self.nc.semaphore(f"dsem{len(self.dsems)}")), f"dsem{len(self.dsems)}")
            self.dsems.append(s)
        self.phase_dsems.append(s)
        return s

    def op(self, eng, fn, reads=(), writes=(), dma=None):
        rec = _Rec()
        fn(rec)
        call = rec.call
        assert call is not None
        deps = []
        for r in reads:
            if r.w is not None:
                deps.append(r.w)
        for w in writes:
            if w.w is not None:
                deps.append(w.w)
            deps.extend(w.r.values())
        my = self.esem[eng]
        waits = {}
        wd = self.waited[eng]
        for (s, v) in deps:
            if s is my and eng == "pe":
                continue
            if wd.get(s.name, 0) >= v:
                continue
            if waits.get(s.name, (None, 0))[1] < v:
                waits[s.name] = (s, v)
        for (s, v) in waits.values():
            wd[s.name] = v
        if dma is not None:
            dma.count += 16
            mark = (dma, dma.count)
            inc = (dma, 16)
        else:
            my.count += 1
            mark = (my, my.count)
            inc = (my, 1)
        for r in reads:
            r.r[mark[0].name] = mark
        for w in writes:
            w.w = mark
            w.r = {}
        self.ops[eng].append((list(waits.values()), call, inc))
        self.nops += 1
        return mark

    def barrier(self):
        allsems = list(self.esem.values()) + self.dsems
        for eng in self.ENGS:
            waits = []
            wd = self.waited[eng]
            for s in allsems:
                if s is self.esem[eng]:
                    continue
                if s.count > wd.get(s.name, 0):
                    waits.append((s, s.count))
                    wd[s.name] = s.count
            if waits:
                self.ops[eng].append((waits, None, None))

    def end_phase(self):
        self.barrier()
        self.free_dsems.extend(self.phase_dsems)
        self.phase_dsems = []

    def emit(self):
        nc = self.nc
        with nc.Block() as block:
            def run(engname):
                def body(e):
                    for (waits, call, inc) in self.ops[engname]:
                        for (s, v) in waits:
                            e.wait_ge(s.h, v)
                        if call is not None:
                            name, a, kw = call
                            ins = getattr(e, name)(*a, **kw)
                            ins.then_inc(inc[0].h, inc[1])
                return body
            block.tensor(run("pe"))
            block.scalar(run("act"))
            block.vector(run("dve"))
            block.gpsimd(run("pool"))
            block.sync(run("sp"))


_UID = [0]


class Ring:
    def __init__(self, k, nc, st, name, shape, dtype, n, with_dsem=True):
        self.items = []
        for i in range(n):
            _UID[0] += 1
            t = st.enter_context(nc.sbuf_tensor(f"rg_{name}{i}_{_UID[0]}", shape, dtype))
            self.items.append((t, Res(f"{name}{i}"), k.dsem(f"d_{name}{i}") if with_dsem else None))
        self.i = 0

    def next(self):
        it = self.items[self.i % len(self.items)]
        self.i += 1
        return it


_CONSTS = None


def host_consts():
    global _CONSTS
    if _CONSTS is not None:
        return _CONSTS
    bf = ml_dtypes.bfloat16
    c = {}
    c["ident"] = np.eye(128, dtype=np.float32).astype(bf)
    idx = np.arange(L, dtype=np.int64)
    lk = (idx[:, None] * idx[None, :]) % L
    ang = lk.astype(np.float64) * (2.0 * np.pi / L)
    c["dftC"] = (np.cos(ang) / np.sqrt(L)).astype(np.float32).astype(bf)
    c["dftS"] = (np.sin(ang) / np.sqrt(L)).astype(np.float32).astype(bf)
    del lk, ang
    i64 = np.arange(64, dtype=np.int64)
    a64 = ((i64[:, None] * i64[None, :]) % 64).astype(np.float64) * (2.0 * np.pi / 64)
    c64 = np.cos(a64) / 8.0
    s64 = -np.sin(a64) / 8.0
    cb = np.zeros((128, 128), np.float32)
    sb = np.zeros((128, 128), np.float32)
    for b in range(2):
        cb[b * 64:(b + 1) * 64, b * 64:(b + 1) * 64] = c64
        sb[b * 64:(b + 1) * 64, b * 64:(b + 1) * 64] = s64
    c["c64bd"] = cb.astype(bf)
    c["s64bd"] = sb.astype(bf)
    inv = (1.0 / (np.float32(10000.0) ** (np.arange(0, 64, 2, dtype=np.float32) / np.float32(64)))).astype(np.float32)
    angr = (np.arange(L, dtype=np.float32)[:, None] * inv[None, :]).astype(np.float32)
    cosr = np.cos(angr).astype(np.float32).T
    sinr = np.sin(angr).astype(np.float32).T
    c["cos4"] = np.ascontiguousarray(np.tile(cosr, (4, 1)))
    c["sin4"] = np.ascontiguousarray(np.tile(sinr, (4, 1)))
    wins = (2, 4, 8, 16)
    invw = np.zeros((128, 2), np.float32)
    elo = np.zeros((128, 2, 8), np.float32)
    ehi = np.zeros((128, 2, 8), np.float32)
    for a in range(2):
        for p in range(128):
            w = wins[2 * a + p // 64]
            invw[p, a] = 1.0 / w
            for j in range(8):
                i = j
                cnt = min(i + w // 2, L) - max(i - w // 2, 0)
                elo[p, a, j] = 1.0 / cnt
                i = L - 8 + j
                cnt = min(i + w // 2, L) - max(i - w // 2, 0)
                ehi[p, a, j] = 1.0 / cnt
    c["pinvw"] = invw
    c["pelo"] = elo
    c["pehi"] = ehi
    _CONSTS = c
    return c


def build_program(layers, first_from_input=True, final_norm=True, debug=False, phases="PABCD"):
    nc = bass.Bass("TRN2", target_bir_lowering=False)
    dk = "ExternalOutput" if debug else "Internal"

    def din(name, shape, dt=F32):
        return nc.dram_tensor(name, list(shape), dt, kind="ExternalInput").ap()

    h0 = din("h0", [NS, L, DM])
    norm_w = din("norm_w", [DEPTH, DM])
    w_in = din("w_in", [DEPTH, DM, DIN])
    fourier_w = din("fourier_w", [DEPTH, 4, 64, 64])
    pool_w = din("pool_w", [DEPTH, 4, 64, 64])
    pool_scale = din("pool_scale", [DEPTH, 256])
    q_norm_w = din("q_norm_w", [DEPTH, 384])
    w_uq = din("w_uq", [DEPTH, 384, 768])
    kv_norm_w = din("kv_norm_w", [DEPTH, 256])
    w_ukv = din("w_ukv", [DEPTH, 256, 1024])
    w_out = din("w_out", [DEPTH, DM, DM])
    final_norm_w = din("final_norm_w", [DM])
    ident_d = din("ident", [128, 128], BF16)
    dftC = din("dftC", [L, L], BF16)
    dftS = din("dftS", [L, L], BF16)
    c64bd_d = din("c64bd", [128, 128], BF16)
    s64bd_d = din("s64bd", [128, 128], BF16)
    cos4_d = din("cos4", [128, L])
    sin4_d = din("sin4", [128, L])
    pinvw_d = din("pinvw", [128, 2])
    pelo_d = din("pelo", [128, 2, 8])
    pehi_d = din("pehi", [128, 2, 8])

    if final_norm:
        out_d = nc.dram_tensor("out", [NS, SEQ, DM], F32, kind="ExternalOutput").ap()
        hbuf = nc.dram_tensor("hbuf", [NS, L, DM], F32, kind="Internal").ap()
    else:
        out_d = None
        hbuf = nc.dram_tensor("hbuf", [NS, L, DM], F32, kind="ExternalOutput").ap()
    pcs_s = nc.dram_tensor("pcs_s", [NS, L, 512], BF16, kind=dk).ap()
    qT_s = nc.dram_tensor("qT_s", [NS, 768, L], BF16, kind=dk).ap()
    kT_s = nc.dram_tensor("kT_s", [NS, 576, L], BF16, kind=dk).ap()
    v_s = nc.dram_tensor("v_s", [NS, L, 512], BF16, kind=dk).ap()
    gTa_s = nc.dram_tensor("gTa_s", [NS, 512, L], BF16, kind=dk).ap()
    gTf_s = nc.dram_tensor("gTf_s", [NS, 256, L], BF16, kind=dk).ap()
    mixT_s = nc.dram_tensor("mixT_s", [NS, 1024, L], BF16, kind=dk).ap()

    with contextlib.ExitStack() as st:
        k = K(nc, st)

        def sb(name, shape, dt, stack=st):
            _UID[0] += 1
            return stack.enter_context(nc.sbuf_tensor(f"sb_{name}_{_UID[0]}", list(shape), dt))

        ident = sb("ident", [128, 128], BF16)
        ones_f = sb("ones_f", [128, 128], F32)
        ones_b = sb("ones_b", [128, 128], BF16)
        eps_t = sb("eps_t", [128, 1], F32)
        Win_tm = sb("Win_tm", [128, 8, TM_COLS], BF16)
        Win_fm = sb("Win_fm", [128, 8, FM_COLS], BF16)
        Wuq = sb("Wuq", [128, 3, 1024], BF16)
        Wukv = sb("Wukv", [128, 2, 1024], BF16)
        Wout = sb("Wout", [128, 8, 1024], BF16)
        Pbd = sb("Pbd", [128, 2, 128], BF16)
        psc = sb("psc", [128, 2], F32)
        c64bd = sb("c64bd", [128, 128], BF16)
        s64bd = sb("s64bd", [128, 128], BF16)
        pinvw = sb("pinvw", [128, 2], F32)
        pelo = sb("pelo", [128, 2, 8], F32)
        pehi = sb("pehi", [128, 2, 8], F32)
        R_w = Res("weights")
        R_c = Res("consts")
        psd = [st.enter_context(nc.psum_tensor(f"psd{i}", [128, 1024], F32)) for i in range(4)]
        ps = [psd[i // 2][:, (i % 2) * 512:(i % 2 + 1) * 512] for i in range(8)]
        R_ps = [Res(f"ps{i}") for i in range(8)]
        dconst = k.dsem("d_const")

        def ld_const(dst, src):
            k.op("sp", lambda e: e.dma_start(out=dst, in_=src), writes=[R_c], dma=dconst)

        ld_const(ident[:], ident_d)
        ld_const(c64bd[:], c64bd_d)
        ld_const(s64bd[:], s64bd_d)
        ld_const(pinvw[:], pinvw_d)
        ld_const(pelo[:], pelo_d)
        ld_const(pehi[:], pehi_d)
        k.op("pool", lambda e: e.memset(ones_f[:], 1.0), writes=[R_c])
        k.op("pool", lambda e: e.memset(ones_b[:], 1.0), writes=[R_c])
        k.op("pool", lambda e: e.memset(eps_t[:], EPS), writes=[R_c])
        k.end_phase()

        def phase_P(l):
            with contextlib.ExitStack() as ls:
                stg = Ring(k, nc, ls, "wst", [128, 8, 640], F32, 2)
                nw = sb("nw", [128, 8], F32, ls)
                qnw = sb("qnw", [128, 3], F32, ls)
                kvnw = sb("kvnw", [128, 2], F32, ls)
                wf_st = sb("wf_st", [128, 2, 64], F32, ls)
                pw_st = sb("pw_st", [128, 2, 64], F32, ls)
                wf_b = sb("wf_b", [128, 2, 64], BF16, ls)
                Mbd = sb("Mbd", [128, 2, 2, 128], BF16, ls)
                Wf_bf = sb("Wf_bf", [128, 8, 256], BF16, ls)
                WfT = sb("WfT", [128, 2, 1024], BF16, ls)
                R_small = Res("small")
                R_wfb = Res("wfb")
                R_mbd = Res("mbd")
                R_wfbf = Res("wfbf")
                R_wft = Res("wft")
                dsm = k.dsem(f"d_small{l}")

                def ld_small(dst, src):
                    k.op("sp", lambda e: e.dma_start(out=dst, in_=src, allow_slow_non_contiguous=True),
                         writes=[R_small], dma=dsm)

                ld_small(nw[:], norm_w[l].rearrange("(c p) -> p c", p=128))
                ld_small(qnw[:], q_norm_w[l].rearrange("(c p) -> p c", p=128))
                ld_small(kvnw[:], kv_norm_w[l].rearrange("(c p) -> p c", p=128))
                ld_small(psc[:], pool_scale[l].rearrange("(a p) -> p a", p=128))
                ld_small(wf_st[:], fourier_w[l].rearrange("(a hh) c d -> (hh c) a d", hh=2))
                ld_small(pw_st[:], pool_w[l].rearrange("(a hh) c d -> (hh c) a d", hh=2))

                engs = ["act", "dve"]
                ei = [0]

                def scaled_cast(dst, src, sc_ap, mul2=None, eng=None):
                    if eng is None:
                        eng = engs[ei[0] % 2]
                        ei[0] += 1
                    if eng == "act":
                        if mul2 is None:
                            fn = lambda e: e.activation(out=dst, in_=src, func=AF.Copy, scale=sc_ap)
                        else:
                            eng = "dve"
                    if eng != "act":
                        if mul2 is None:
                            fn = lambda e: e.tensor_scalar(out=dst, in0=src, scalar1=sc_ap, scalar2=None, op0=ALU.mult)
                        else:
                            fn = lambda e: e.tensor_scalar(out=dst, in0=src, scalar1=sc_ap, scalar2=float(mul2),
                                                           op0=ALU.mult, op1=ALU.mult)
                    return eng, fn

                Wv = w_in[l].rearrange("(c p) n -> p c n", p=128)
                pieces = [(0, 512), (512, 1024), (1024, 1664), (1664, 2240)]
                for (c0, c1) in pieces:
                    t, R_t, ds = stg.next()
                    wcols = c1 - c0
                    k.op("sp", lambda e, t=t, c0=c0, c1=c1, wcols=wcols: e.dma_start(out=t[:, :, 0:wcols], in_=Wv[:, :, c0:c1]),
                         writes=[R_t], dma=ds)
                    jobs = []
                    if c0 == 0:
                        jobs.append((Wf_bf, 0, 0, 256, None, R_wfbf))
                        jobs.append((Win_fm, 0, 256, 256, None, R_w))
                    elif c0 == 512:
                        jobs.append((Win_fm, 1024, 512, 256, None, R_w))
                        jobs.append((Win_fm, 256, 768, 256, None, R_w))
                    elif c0 == 1024:
                        jobs.append((Win_tm, 512, 1024, 384, None, R_w))
                        jobs.append((Win_tm, 896, 1408, 256, None, R_w))
                    else:
                        jobs.append((Win_fm, 1280, 1664, 64, None, R_w))
                        jobs.append((Win_fm, 1344, 1696, 32, -1.0, R_w))
                        jobs.append((Win_fm, 1376, 1664, 32, None, R_w))
                        jobs.append((Win_fm, 512, 1728, 512, None, R_w))
                    for (dt_, d0, s0, ncol, mul2, rr) in jobs:
                        for c in range(8):
                            eng, fn = scaled_cast(dt_[:, c, d0:d0 + ncol], t[:, c, s0 - c0:s0 - c0 + ncol], nw[:, c:c + 1], mul2)
                            k.op(eng, fn, reads=[R_t, R_small], writes=[rr])

                t, R_t, ds = stg.next()
                tq = t[:].rearrange("p c n -> p (c n)")[:, 0:3 * 768].rearrange("p (c n) -> p c n", c=3)
                k.op("sp", lambda e: e.dma_start(out=tq, in_=w_uq[l].rearrange("(c p) n -> p c n", p=128)),
                     writes=[R_t], dma=ds)
                for c in range(3):
                    srcv = tq[:, c, :].rearrange("p (h e) -> p h e", h=4)
                    sc = qnw[:, c:c + 1]
                    eng, fn = scaled_cast(Wuq[:, c, 0:512].rearrange("p (h e) -> p h e", h=4), srcv[:, :, 0:128], sc, SCALE, "dve")
                    k.op(eng, fn, reads=[R_t, R_small], writes=[R_w])
                    eng, fn = scaled_cast(Wuq[:, c, 512:768].rearrange("p (h e) -> p h e", h=4), srcv[:, :, 128:192], sc, SCALE, "dve")
                    k.op(eng, fn, reads=[R_t, R_small], writes=[R_w])
                    rot = Wuq[:, c, 768:1024].rearrange("p (h e) -> p h e", h=4)
                    eng, fn = scaled_cast(rot[:, :, 0:32], srcv[:, :, 160:192], sc, -SCALE, "dve")
                    k.op(eng, fn, reads=[R_t, R_small], writes=[R_w])
                    eng, fn = scaled_cast(rot[:, :, 32:64], srcv[:, :, 128:160], sc, SCALE, "dve")
                    k.op(eng, fn, reads=[R_t, R_small], writes=[R_w])
                t, R_t, ds = stg.next()
                tkv = t[:].rearrange("p c n -> p (c n)")[:, 0:2 * 1024].rearrange("p (c n) -> p c n", c=2)
                k.op("sp", lambda e: e.dma_start(out=tkv, in_=w_ukv[l].rearrange("(c p) n -> p c n", p=128)),
                     writes=[R_t], dma=ds)
                for c in range(2):
                    srcv = tkv[:, c, :].rearrange("p (h e) -> p h e", h=4)
                    sc = kvnw[:, c:c + 1]
                    eng, fn = scaled_cast(Wukv[:, c, 0:512].rearrange("p (h e) -> p h e", h=4), srcv[:, :, 0:128], sc, None, "act")
                    k.op(eng, fn, reads=[R_t, R_small], writes=[R_w])
                    eng, fn = scaled_cast(Wukv[:, c, 512:1024].rearrange("p (h e) -> p h e", h=4), srcv[:, :, 128:256], sc, None, "dve")
                    k.op(eng, fn, reads=[R_t, R_small], writes=[R_w])
                Wo = w_out[l].rearrange("(c p) n -> p c n", p=128)
                for half in range(2):
                    t, R_t, ds = stg.next()
                    tv = t[:].rearrange("p c n -> p (c n)")[:, 0:4096].rearrange("p (c n) -> p c n", c=4)
                    k.op("sp", lambda e, tv=tv, half=half: e.dma_start(out=tv, in_=Wo[:, half * 4:(half + 1) * 4, :]),
                         writes=[R_t], dma=ds)
                    for c in range(4):
                        eng = ["act", "dve"][c % 2]
                        if eng == "act":
                            fn = lambda e, tv=tv, c=c, half=half: e.activation(out=Wout[:, half * 4 + c, :], in_=tv[:, c, :], func=AF.Copy)
                        else:
                            fn = lambda e, tv=tv, c=c, half=half: e.tensor_copy(out=Wout[:, half * 4 + c, :], in_=tv[:, c, :])
                        k.op(eng, fn, reads=[R_t], writes=[R_w])
                k.op("pool", lambda e: e.memset(Pbd[:], 0.0), writes=[R_w])
                for a in range(2):
                    k.op("pool", lambda e, a=a: e.tensor_copy(out=Pbd[0:64, a, 0:64], in_=pw_st[0:64, a, :]), reads=[R_small], writes=[R_w])
                    k.op("pool", lambda e, a=a: e.tensor_copy(out=Pbd[64:128, a, 64:128], in_=pw_st[64:128, a, :]), reads=[R_small], writes=[R_w])
                k.op("dve", lambda e: e.tensor_copy(out=wf_b[:], in_=wf_st[:]), reads=[R_small], writes=[R_wfb])
                k.op("pool", lambda e: e.memset(Mbd[:], 0.0), writes=[R_mbd])
                for a in range(2):
                    for cs in range(2):
                        blk = (a * 2 + cs) * 64
                        lh = c64bd if cs == 0 else s64bd
                        k.op("pe", lambda e, a=a, blk=blk, lh=lh: e.matmul(ps[0][:, blk:blk + 64], lhsT=lh[:], rhs=wf_b[:, a, :], start=True, stop=True),
                             reads=[R_c, R_wfb], writes=[R_ps[0]])
                for a in range(2):
                    for cs in range(2):
                        blk = (a * 2 + cs) * 64
                        k.op("dve", lambda e, a=a, cs=cs, blk=blk: e.tensor_copy(out=Mbd[0:64, a, cs, 0:64], in_=ps[0][0:64, blk:blk + 64]),
                             reads=[R_ps[0]], writes=[R_mbd])
                        k.op("dve", lambda e, a=a, cs=cs, blk=blk: e.tensor_copy(out=Mbd[64:128, a, cs, 64:128], in_=ps[0][64:128, blk:blk + 64]),
                             reads=[R_ps[0]], writes=[R_mbd])
                for a in range(2):
                    pt = ps[1 + a][:].bitcast(BF16)
                    for c in range(8):
                        k.op("pe", lambda e, a=a, c=c, pt=pt: e.transpose(out=pt[:, c * 128:(c + 1) * 128], in_=Wf_bf[:, c, a * 128:(a + 1) * 128], identity=ident[:]),
                             reads=[R_wfbf, R_c], writes=[R_ps[1 + a]])
                    k.op("act", lambda e, a=a, pt=pt: e.activation(out=WfT[:, a, :], in_=pt, func=AF.Copy), reads=[R_ps[1 + a]], writes=[R_wft])
                for c in range(8):
                    pb = 3 + (c % 2)
                    for cs in range(2):
                        for a in range(2):
                            col = cs * 256 + a * 128
                            k.op("pe", lambda e, c=c, cs=cs, a=a, col=col, pb=pb: e.matmul(ps[pb][:, col:col + 128], lhsT=WfT[:, a, c * 128:(c + 1) * 128],
                                                                                         rhs=Mbd[:, a, cs, :], start=True, stop=True),
                                 reads=[R_wft, R_mbd], writes=[R_ps[pb]])
                    k.op("dve", lambda e, c=c, pb=pb: e.tensor_copy(out=Win_tm[:, c, 0:512], in_=ps[pb][:, :]), reads=[R_ps[pb]], writes=[R_w])
            k.end_phase()

        def phase_A(l, s, h_src):
            with contextlib.ExitStack() as ls:
                Xp = sb("Xp", [128, 2, L + 16], BF16, ls)
                spg = sb("spg", [128, 2, L], BF16, ls)
                ls2 = ls.enter_context(contextlib.ExitStack())
                hring = Ring(k, nc, ls2, "ht", [128, 1024], F32, 2)
                xn_ring = Ring(k, nc, ls2, "xn", [128, 1024], BF16, 2, with_dsem=False)
                cn_ring = Ring(k, nc, ls2, "cn", [128, 640], BF16, 2, with_dsem=False)
                xnT_ring = [(sb(f"xnT{i}", [128, 8, 512], BF16, ls2), [Res() for _ in range(4)]) for i in range(2)]
                cnT_ring = [(sb(f"cnT{i}", [128, 5, 512], BF16, ls2), [Res() for _ in range(4)]) for i in range(2)]
                pcs_st = Ring(k, nc, ls2, "pcs_st", [128, 4, 512], BF16, 1)
                v_st = Ring(k, nc, ls2, "v_st", [128, 4, 512], BF16, 1)
                gf_st = Ring(k, nc, ls2, "gf_st", [128, 2, 512], BF16, 1)
                ga_st = Ring(k, nc, ls2, "ga_st", [128, 4, 512], BF16, 1)
                q_st = Ring(k, nc, ls2, "q_st", [128, 6, 512], BF16, 1)
                k_st = Ring(k, nc, ls2, "k_st", [128, 4, 512], BF16, 1)
                kr_st = Ring(k, nc, ls2, "kr_st", [64, 512], BF16, 1)
                cos_r = Ring(k, nc, ls2, "cos_r", [128, 512], F32, 2)
                sin_r = Ring(k, nc, ls2, "sin_r", [128, 512], F32, 2)
                t1_r = Ring(k, nc, ls2, "t1_r", [128, 512], F32, 2, with_dsem=False)
                t2_r = Ring(k, nc, ls2, "t2_r", [128, 512], F32, 2, with_dsem=False)
                junk_a = sb("junk_a", [128, 1024], BF16, ls2)
                ss_r = Ring(k, nc, ls2, "ss_r", [128, 4], F32, 4, with_dsem=False)
                R_xp = [Res("xp0"), Res("xp1")]
                R_spg = [Res("spg0"), Res("spg1")]
                PT = [0, 1]
                TMB = [2, 3, 4]
                VB = 5
                FMB = [6, 7]
                fmi = [0]
                pti = [0]

                k.op("pool", lambda e: e.memset(Xp[:, :, 0:8], 0.0), writes=R_xp)
                k.op("pool", lambda e: e.memset(Xp[:, :, L + 8:L + 16], 0.0), writes=R_xp)

                for sti, (t0, n) in enumerate(STILES):
                    xnT, R_xnT = xnT_ring[sti % 2]
                    cnT, R_cnT = cnT_ring[sti % 2]
                    pcs_t, R_pcs, d_pcs = pcs_st.next()
                    v_t, R_v, d_v = v_st.next()
                    cos_t, R_cos, d_cos = cos_r.next()
                    sin_t, R_sin, d_sin = sin_r.next()
                    k.op("sp", lambda e, cos_t=cos_t, t0=t0, n=n: e.dma_start(out=cos_t[:, 0:n], in_=cos4_d[:, t0:t0 + n]), writes=[R_cos], dma=d_cos)
                    k.op("sp", lambda e, sin_t=sin_t, t0=t0, n=n: e.dma_start(out=sin_t[:, 0:n], in_=sin4_d[:, t0:t0 + n]), writes=[R_sin], dma=d_sin)
                    tl = tiles_of(t0, n)
                    for j, (r0, nr) in enumerate(tl):
                        ht, R_ht, d_ht = hring.next()
                        k.op("sp", lambda e, ht=ht, nr=nr, r0=r0, t0=t0: e.dma_start(out=ht[0:nr, :], in_=h_src[s, t0 + r0:t0 + r0 + nr, :]), writes=[R_ht], dma=d_ht)
                        ss, R_ss, _ = ss_r.next()
                        k.op("act", lambda e, ht=ht, nr=nr, ss=ss: e.activation(out=junk_a[0:nr, :], in_=ht[0:nr, :], func=AF.Square, scale=float(DM ** -0.5), accum_out=ss[0:nr, 0:1]),
                             reads=[R_ht], writes=[R_ss])
                        k.op("act", lambda e, nr=nr, ss=ss: e.activation(out=ss[0:nr, 0:1], in_=ss[0:nr, 0:1], func=AF.Sqrt, bias=eps_t[0:nr, 0:1]),
                             reads=[R_ss], writes=[R_ss])
                        k.op("dve", lambda e, nr=nr, ss=ss: e.reciprocal(out=ss[0:nr, 0:1], in_=ss[0:nr, 0:1]), reads=[R_ss], writes=[R_ss])
                        xn, R_xn, _ = xn_ring.next()
                        k.op("dve", lambda e, nr=nr, ss=ss, xn=xn, ht=ht: e.tensor_scalar(out=xn[0:nr, :], in0=ht[0:nr, :], scalar1=ss[0:nr, 0:1], scalar2=None, op0=ALU.mult),
                             reads=[R_ht, R_ss], writes=[R_xn])
                        pb = PT[pti[0] % 2]
                        pti[0] += 1
                        ptv = ps[pb][:].bitcast(BF16)
                        for c in range(8):
                            k.op("pe", lambda e, c=c, nr=nr, xn=xn, ptv=ptv: e.transpose(out=ptv[:, c * 128:c * 128 + nr], in_=xn[0:nr, c * 128:(c + 1) * 128], identity=ident[0:nr, 0:nr]),
                                 reads=[R_xn, R_c], writes=[R_ps[pb]])
                        k.op("act", lambda e, nr=nr, r0=r0, xnT=xnT, ptv=ptv: e.activation(out=xnT[:, :, r0:r0 + nr], in_=ptv.rearrange("p (c t) -> p c t", c=8)[:, :, 0:nr], func=AF.Copy),
                             reads=[R_ps[pb]], writes=[R_xnT[j]])
                        tm_specs = [(TMB[0], 0, 512), (TMB[1], 512, 384), (TMB[2], 896, 256)]
                        for (bk, c0, ncol) in tm_specs:
                            for c in range(8):
                                k.op("pe", lambda e, bk=bk, c0=c0, ncol=ncol, c=c, nr=nr, r0=r0, xnT=xnT: e.matmul(ps[bk][0:nr, 0:ncol], lhsT=xnT[:, c, r0:r0 + nr], rhs=Win_tm[:, c, c0:c0 + ncol],
                                                                                                                  start=(c == 0), stop=(c == 7)),
                                     reads=[R_xnT[j], R_w], writes=[R_ps[bk]])
                        k.op("act", lambda e, nr=nr, j=j, pcs_t=pcs_t: e.activation(out=pcs_t[0:nr, j, :], in_=ps[TMB[0]][0:nr, :], func=AF.Copy),
                             reads=[R_ps[TMB[0]]], writes=[R_pcs])
                        k.op("act", lambda e, nr=nr, ss=ss: e.activation(out=junk_a[0:nr, 0:384], in_=ps[TMB[1]][0:nr, 0:384], func=AF.Square, scale=float(384 ** -0.5), accum_out=ss[0:nr, 1:2]),
                             reads=[R_ps[TMB[1]]], writes=[R_ss])
                        k.op("act", lambda e, nr=nr, ss=ss: e.activation(out=junk_a[0:nr, 0:256], in_=ps[TMB[2]][0:nr, 0:256], func=AF.Square, scale=float(256 ** -0.5), accum_out=ss[0:nr, 2:3]),
                             reads=[R_ps[TMB[2]]], writes=[R_ss])
                        k.op("act", lambda e, nr=nr, ss=ss: e.activation(out=ss[0:nr, 1:3], in_=ss[0:nr, 1:3], func=AF.Sqrt, bias=eps_t[0:nr, 0:1]),
                             reads=[R_ss], writes=[R_ss])
                        k.op("dve", lambda e, nr=nr, ss=ss: e.reciprocal(out=ss[0:nr, 1:3], in_=ss[0:nr, 1:3]), reads=[R_ss], writes=[R_ss])
                        cn, R_cn, _ = cn_ring.next()
                        k.op("dve", lambda e, nr=nr, ss=ss, cn=cn: e.tensor_scalar(out=cn[0:nr, 0:384], in0=ps[TMB[1]][0:nr, 0:384], scalar1=ss[0:nr, 1:2], scalar2=None, op0=ALU.mult),
                             reads=[R_ps[TMB[1]], R_ss], writes=[R_cn])
                        k.op("dve", lambda e, nr=nr, ss=ss, cn=cn: e.tensor_scalar(out=cn[0:nr, 384:640], in0=ps[TMB[2]][0:nr, 0:256], scalar1=ss[0:nr, 2:3], scalar2=None, op0=ALU.mult),
                             reads=[R_ps[TMB[2]], R_ss], writes=[R_cn])
                        pb = PT[pti[0] % 2]
                        pti[0] += 1
                        ptv = ps[pb][:].bitcast(BF16)
                        for c in range(5):
                            k.op("pe", lambda e, c=c, nr=nr, cn=cn, ptv=ptv: e.transpose(out=ptv[:, c * 128:c * 128 + nr], in_=cn[0:nr, c * 128:(c + 1) * 128], identity=ident[0:nr, 0:nr]),
                                 reads=[R_cn, R_c], writes=[R_ps[pb]])
                        k.op("dve", lambda e, nr=nr, r0=r0, cnT=cnT, ptv=ptv: e.tensor_copy(out=cnT[:, :, r0:r0 + nr], in_=ptv[:, 0:640].rearrange("p (c t) -> p c t", c=5)[:, :, 0:nr]),
                             reads=[R_ps[pb]], writes=[R_cnT[j]])
                        for c in range(2):
                            k.op("pe", lambda e, c=c, nr=nr, r0=r0, cnT=cnT: e.matmul(ps[VB][0:nr, :], lhsT=cnT[:, 3 + c, r0:r0 + nr], rhs=Wukv[:, c, 512:1024], start=(c == 0), stop=(c == 1)),
                                 reads=[R_cnT[j], R_w], writes=[R_ps[VB]])
                        k.op("dve", lambda e, nr=nr, j=j, v_t=v_t: e.tensor_copy(out=v_t[0:nr, j, :], in_=ps[VB][0:nr, :]), reads=[R_ps[VB]], writes=[R_v])
                    if n == 512:
                        k.op("sp", lambda e, pcs_t=pcs_t, t0=t0: e.dma_start(out=pcs_s[s, t0:t0 + 512, :].rearrange("(j p) n -> p j n", p=128), in_=pcs_t[:]), reads=[R_pcs], dma=d_pcs)
                        k.op("sp", lambda e, v_t=v_t, t0=t0: e.dma_start(out=v_s[s, t0:t0 + 512, :].rearrange("(j p) n -> p j n", p=128), in_=v_t[:]), reads=[R_v], dma=d_v)
                    else:
                        k.op("sp", lambda e, pcs_t=pcs_t, t0=t0, n=n: e.dma_start(out=pcs_s[s, t0:t0 + n, :], in_=pcs_t[0:n, 0, :]), reads=[R_pcs], dma=d_pcs)
                        k.op("sp", lambda e, v_t=v_t, t0=t0, n=n: e.dma_start(out=v_s[s, t0:t0 + n, :], in_=v_t[0:n, 0, :]), reads=[R_v], dma=d_v)

                    def fm_group(col0, m):
                        bk = FMB[fmi[0] % 2]
                        fmi[0] += 1
                        for c in range(8):
                            k.op("pe", lambda e, bk=bk, c=c, col0=col0, m=m: e.matmul(ps[bk][0:m, 0:n], lhsT=Win_fm[:, c, col0:col0 + m], rhs=xnT[:, c, 0:n], start=(c == 0), stop=(c == 7)),
                                 reads=R_xnT[0:len(tl)] + [R_w], writes=[R_ps[bk]])
                        return bk

                    gf_t, R_gf, d_gf = gf_st.next()
                    ga_t, R_ga, d_ga = ga_st.next()
                    for oc in range(2):
                        bk = fm_group(oc * 128, 128)
                        k.op("act", lambda e, bk=bk, oc=oc, gf_t=gf_t: e.activation(out=gf_t[:, oc, 0:n], in_=ps[bk][:, 0:n], func=AF.Silu), reads=[R_ps[bk]], writes=[R_gf])
                    k.op("sp", lambda e, gf_t=gf_t: e.dma_start(out=gTf_s[s, :, t0:t0 + n].rearrange("(c p) t -> p c t", p=128), in_=gf_t[:, :, 0:n]), reads=[R_gf], dma=d_gf)
                    for oc in range(2):
                        bk = fm_group(256 + oc * 128, 128)
                        k.op("act", lambda e, bk=bk, oc=oc: e.activation(out=spg[:, oc, t0:t0 + n], in_=ps[bk][:, 0:n], func=AF.Silu), reads=[R_ps[bk]], writes=[R_spg[oc]])
                    for oc in range(4):
                        bk = fm_group(512 + oc * 128, 128)
                        k.op("act", lambda e, bk=bk, oc=oc, ga_t=ga_t: e.activation(out=ga_t[:, oc, 0:n], in_=ps[bk][:, 0:n], func=AF.Silu), reads=[R_ps[bk]], writes=[R_ga])
                    k.op("sp", lambda e, ga_t=ga_t: e.dma_start(out=gTa_s[s, :, t0:t0 + n].rearrange("(c p) t -> p c t", p=128), in_=ga_t[:, :, 0:n]), reads=[R_ga], dma=d_ga)
                    for oc in range(2):
                        bk = fm_group(1024 + oc * 128, 128)
                        k.op("dve", lambda e, bk=bk, oc=oc: e.tensor_copy(out=Xp[:, oc, 8 + t0:8 + t0 + n], in_=ps[bk][:, 0:n]), reads=[R_ps[bk]], writes=[R_xp[oc]])
                    bkA = fm_group(1280, 64)
                    bkB = fm_group(1344, 64)
                    t1, R_t1, _ = t1_r.next()
                    t2, R_t2, _ = t2_r.next()
                    kr_t, R_kr, d_kr = kr_st.next()
                    k.op("dve", lambda e, t1=t1, bkA=bkA, cos_t=cos_t: e.tensor_tensor(out=t1[0:64, 0:n], in0=ps[bkA][0:64, 0:n], in1=cos_t[0:64, 0:n], op=ALU.mult),
                         reads=[R_ps[bkA], R_cos], writes=[R_t1])
                    k.op("dve", lambda e, t2=t2, bkB=bkB, sin_t=sin_t: e.tensor_tensor(out=t2[0:64, 0:n], in0=ps[bkB][0:64, 0:n], in1=sin_t[0:64, 0:n], op=ALU.mult),
                         reads=[R_ps[bkB], R_sin], writes=[R_t2])
                    k.op("pool", lambda e, t1=t1, t2=t2, kr_t=kr_t: e.tensor_tensor(out=kr_t[:, 0:n], in0=t1[0:64, 0:n], in1=t2[0:64, 0:n], op=ALU.add),
                         reads=[R_t1, R_t2], writes=[R_kr])
                    k.op("sp", lambda e, kr_t=kr_t: e.dma_start(out=kT_s[s, 512:576, t0:t0 + n], in_=kr_t[:, 0:n]), reads=[R_kr], dma=d_kr)

                    def cn_group(Wt, nk, kc0, col0):
                        bk = FMB[fmi[0] % 2]
                        fmi[0] += 1
                        for c in range(nk):
                            k.op("pe", lambda e, bk=bk, c=c: e.matmul(ps[bk][:, 0:n], lhsT=Wt[:, c, col0:col0 + 128], rhs=cnT[:, kc0 + c, 0:n], start=(c == 0), stop=(c == nk - 1)),
                                 reads=R_cnT[0:len(tl)] + [R_w], writes=[R_ps[bk]])
                        return bk

                    q_t, R_q, d_q = q_st.next()
                    for oc in range(4):
                        bk = cn_group(Wuq, 3, 0, oc * 128)
                        if oc % 2 == 0:
                            k.op("act", lambda e, bk=bk, oc=oc, q_t=q_t: e.activation(out=q_t[:, oc, 0:n], in_=ps[bk][:, 0:n], func=AF.Copy), reads=[R_ps[bk]], writes=[R_q])
                        else:
                            k.op("dve", lambda e, bk=bk, oc=oc, q_t=q_t: e.tensor_copy(out=q_t[:, oc, 0:n], in_=ps[bk][:, 0:n]), reads=[R_ps[bk]], writes=[R_q])
                    for a in range(2):
                        bkA = cn_group(Wuq, 3, 0, 512 + a * 128)
                        bkB = cn_group(Wuq, 3, 0, 768 + a * 128)
                        t1, R_t1, _ = t1_r.next()
                        t2, R_t2, _ = t2_r.next()
                        k.op("dve", lambda e, t1=t1, bkA=bkA: e.tensor_tensor(out=t1[:, 0:n], in0=ps[bkA][:, 0:n], in1=cos_t[:, 0:n], op=ALU.mult),
                             reads=[R_ps[bkA], R_cos], writes=[R_t1])
                        k.op("dve", lambda e, t2=t2, bkB=bkB: e.tensor_tensor(out=t2[:, 0:n], in0=ps[bkB][:, 0:n], in1=sin_t[:, 0:n], op=ALU.mult),
                             reads=[R_ps[bkB], R_sin], writes=[R_t2])
                        k.op("pool", lambda e, t1=t1, t2=t2, a=a, q_t=q_t: e.tensor_tensor(out=q_t[:, 4 + a, 0:n], in0=t1[:, 0:n], in1=t2[:, 0:n], op=ALU.add),
                             reads=[R_t1, R_t2], writes=[R_q])
                    k.op("sp", lambda e, q_t=q_t: e.dma_start(out=qT_s[s, :, t0:t0 + n].rearrange("(c p) t -> p c t", p=128), in_=q_t[:, :, 0:n]), reads=[R_q], dma=d_q)
                    k_t, R_k, d_k = k_st.next()
                    for oc in range(4):
                        bk = cn_group(Wukv, 2, 3, oc * 128)
                        if oc % 2 == 0:
                            k.op("act", lambda e, bk=bk, oc=oc, k_t=k_t: e.activation(out=k_t[:, oc, 0:n], in_=ps[bk][:, 0:n], func=AF.Copy), reads=[R_ps[bk]], writes=[R_k])
                        else:
                            k.op("dve", lambda e, bk=bk, oc=oc, k_t=k_t: e.tensor_copy(out=k_t[:, oc, 0:n], in_=ps[bk][:, 0:n]), reads=[R_ps[bk]], writes=[R_k])
                    k.op("sp", lambda e, k_t=k_t: e.dma_start(out=kT_s[s, 0:512, t0:t0 + n].rearrange("(c p) t -> p c t", p=128), in_=k_t[:, :, 0:n]), reads=[R_k], dma=d_k)

                k.barrier()
                ls2.close()
                CB = 1028
                sA = Ring(k, nc, ls, "poolA", [128, CB + 16], F32, 2, with_dsem=False)
                sB = Ring(k, nc, ls, "poolB", [128, CB + 16], F32, 2, with_dsem=False)
                pm_st = Ring(k, nc, ls, "pm_st", [128, CB], BF16, 2)
                ym = Ring(k, nc, ls, "ym", [128, CB], BF16, 2, with_dsem=False)
                for a in range(2):
                    for b in range(4):
                        o = b * CB
                        A_, R_A, _ = sA.next()
                        B_, R_B, _ = sB.next()
                        X = Xp[:, a, o:o + CB + 16]
                        W_ = CB + 16
                        eng1 = "pool" if (b % 2 == 0) else "dve"
                        eng2 = "dve"
                        k.op(eng1, lambda e, A_=A_, X=X, W_=W_: e.tensor_tensor(out=A_[:, 1:W_], in0=X[:, 0:W_ - 1], in1=X[:, 1:W_], op=ALU.add), reads=[R_xp[a]], writes=[R_A])
                        k.op(eng1, lambda e, A_=A_, B_=B_, W_=W_: e.tensor_tensor(out=B_[64:128, 2:W_ - 1] if a == 0 else B_[:, 2:W_ - 1],
                                                                               in0=A_[64:128, 1:W_ - 2] if a == 0 else A_[:, 1:W_ - 2],
                                                                               in1=A_[64:128, 3:W_] if a == 0 else A_[:, 3:W_], op=ALU.add), reads=[R_A], writes=[R_B])
                        if a == 0:
                            k.op(eng1, lambda e, A_=A_, B_=B_, W_=W_: e.tensor_copy(out=B_[0:64, 2:W_ - 1], in_=A_[0:64, 2:W_ - 1]), reads=[R_A], writes=[R_B])
                            Sfin, R_S = B_, R_B
                        else:
                            k.op(eng1, lambda e, A_=A_, B_=B_, W_=W_: e.tensor_tensor(out=A_[:, 4:W_ - 3], in0=B_[:, 2:W_ - 5], in1=B_[:, 6:W_ - 1], op=ALU.add), reads=[R_B], writes=[R_A])
                            k.op(eng1, lambda e, A_=A_, B_=B_, W_=W_: e.tensor_tensor(out=B_[64:128, 8:W_ - 7], in0=A_[64:128, 4:W_ - 11], in1=A_[64:128, 12:W_ - 3], op=ALU.add), reads=[R_A], writes=[R_B])
                            k.op(eng1, lambda e, A_=A_, B_=B_, W_=W_: e.tensor_copy(out=B_[0:64, 8:W_ - 7], in_=A_[0:64, 8:W_ - 7]), reads=[R_A], writes=[R_B])
                            Sfin, R_S = B_, R_B
                        y_, R_y, _ = ym.next()
                        k.op(eng2, lambda e, Sfin=Sfin, X=X, y_=y_, a=a: e.scalar_tensor_tensor(out=y_[:, 0:CB], in0=Sfin[:, 8:8 + CB], scalar=pinvw[:, a:a + 1], in1=X[:, 8:8 + CB],
                                                                                              op0=ALU.mult, op1=ALU.subtract), reads=[R_S, R_xp[a], R_c], writes=[R_y])
                        if b == 0:
                            k.op(eng2, lambda e, Sfin=Sfin, a=a: e.tensor_tensor(out=Sfin[:, 8:16], in0=Sfin[:, 8:16], in1=pelo[:, a, :], op=ALU.mult), reads=[R_S, R_c, R_y], writes=[R_S])
                            k.op(eng2, lambda e, Sfin=Sfin, X=X, y_=y_: e.tensor_tensor(out=y_[:, 0:8], in0=Sfin[:, 8:16], in1=X[:, 8:16], op=ALU.subtract), reads=[R_S, R_xp[a]], writes=[R_y])
                        if b == 3:
                            k.op(eng2, lambda e, Sfin=Sfin, a=a: e.tensor_tensor(out=Sfin[:, CB:CB + 8], in0=Sfin[:, CB:CB + 8], in1=pehi[:, a, :], op=ALU.mult), reads=[R_S, R_c, R_y], writes=[R_S])
                            k.op(eng2, lambda e, Sfin=Sfin, X=X, y_=y_: e.tensor_tensor(out=y_[:, CB - 8:CB], in0=Sfin[:, CB:CB + 8], in1=X[:, CB:CB + 8], op=ALU.subtract), reads=[R_S, R_xp[a]], writes=[R_y])
                        pm_t, R_pm, d_pm = pm_st.next()
                        for (c0, cw) in [(0, 512), (512, 512), (1024, CB - 1024)]:
                            bk = FMB[fmi[0] % 2]
                            fmi[0] += 1
                            k.op("pe", lambda e, bk=bk, y_=y_, c0=c0, cw=cw, a=a: e.matmul(ps[bk][:, 0:cw], lhsT=Pbd[:, a, :], rhs=y_[:, c0:c0 + cw], start=True, stop=True),
                                 reads=[R_y, R_w], writes=[R_ps[bk]])
                            k.op("dve", lambda e, bk=bk, c0=c0, cw=cw, a=a, pm_t=pm_t, o=o: e.scalar_tensor_tensor(out=pm_t[:, c0:c0 + cw], in0=ps[bk][:, 0:cw], scalar=psc[:, a:a + 1],
                                                                                                             in1=spg[:, a, o + c0:o + c0 + cw], op0=ALU.mult, op1=ALU.mult),
                                 reads=[R_ps[bk], R_spg[a], R_w], writes=[R_pm])
                        k.op("sp", lambda e, pm_t=pm_t, a=a, o=o: e.dma_start(out=mixT_s[s, 256 + a * 128:256 + (a + 1) * 128, o:o + CB], in_=pm_t[:, :]), reads=[R_pm], dma=d_pm)
            k.end_phase()

        def phase_B(l, s):
            with contextlib.ExitStack() as ls:
                kT = sb("kT", [128, 4, L], BF16, ls)
                krT = sb("krT", [128, L], BF16, ls)
                V = sb("V", [128, NT, 512], BF16, ls)
                R_kv = Res("kv")
                dkv = k.dsem(f"d_kv{l}_{s}")
                k.op("sp", lambda e: e.dma_start(out=kT[:], in_=kT_s[s, 0:512, :].rearrange("(c p) t -> p c t", p=128)), writes=[R_kv], dma=dkv)
                k.op("sp", lambda e: e.dma_start(out=krT[0:64, :], in_=kT_s[s, 512:576, :]), writes=[R_kv], dma=dkv)
                k.op("pool", lambda e: e.memset(krT[64:128, :], 0.0), writes=[R_kv])
                k.op("sp", lambda e: e.dma_start(out=V[:, 0:32, :], in_=v_s[s, 0:4096, :].rearrange("(j p) n -> p j n", p=128)), writes=[R_kv], dma=dkv)
                k.op("sp", lambda e: e.dma_start(out=V[0:16, 32, :], in_=v_s[s, 4096:L, :]), writes=[R_kv], dma=dkv)
                q_r = Ring(k, nc, ls, "qb", [128, 4, 512], BF16, 2)
                qr_r = Ring(k, nc, ls, "qrb", [128, 4, 512], BF16, 2)
                for (qt_, Rq_, _) in qr_r.items:
                    k.op("pool", lambda e: e.memset(qt_[64:128, :, :], 0.0), writes=[Rq_])
                ga_r = Ring(k, nc, ls, "gab", [128, 4, 512], BF16, 2)
                mx_r = Ring(k, nc, ls, "mxb", [128, 4, 512], BF16, 2)
                pt_r = Ring(k, nc, ls, "ptb", [128, 2, 512], BF16, 3, with_dsem=False)
                acc0_r = Ring(k, nc, ls, "acc0", [128, 2, 512], F32, 2, with_dsem=False)
                rinv_r = Ring(k, nc, ls, "rinv", [128, 512], F32, 1, with_dsem=False)
                o1_r = Ring(k, nc, ls, "o1", [128, 512], F32, 1, with_dsem=False)
                SPAIR = [0, 1]
                OB = [4, 5]
                SUMBS = [6, 7]
                spi = [0]
                obi = [0]
                NPAIR = 17
                for (t0, n) in STILES:
                    q_t, R_q, d_q = q_r.next()
                    qr_t, R_qr, d_qr = qr_r.next()
                    ga_t, R_ga, d_ga = ga_r.next()
                    mx_t, R_mx, d_mx = mx_r.next()
                    k.op("sp", lambda e: e.dma_start(out=q_t[:, :, 0:n], in_=qT_s[s, 0:512, t0:t0 + n].rearrange("(c p) t -> p c t", p=128)), writes=[R_q], dma=d_q)
                    k.op("sp", lambda e: e.dma_start(out=qr_t[0:64, :, 0:n], in_=qT_s[s, 512:768, t0:t0 + n].rearrange("(c p) t -> p c t", p=64)), writes=[R_qr], dma=d_qr)
                    k.op("sp", lambda e: e.dma_start(out=ga_t[:, :, 0:n], in_=gTa_s[s, :, t0:t0 + n].rearrange("(c p) t -> p c t", p=128)), writes=[R_ga], dma=d_ga)
                    for h in range(4):
                        ob = OB[obi[0] % 2]
                        obi[0] += 1
                        acc0, R_a0, _ = acc0_r.next()
                        sumb = SUMBS[(obi[0] - 1) % 2]

                        def qk(p):
                            bp = SPAIR[spi[0] % 2]
                            spi[0] += 1
                            kts = [2 * p, 2 * p + 1] if p < NPAIR - 1 else [2 * p]
                            for i, kt in enumerate(kts):
                                kn = 128 if kt < 32 else 16
                                bk = 2 * bp + i
                                k.op("pe", lambda e: e.matmul(ps[bk][0:kn, 0:n], lhsT=kT[:, h, kt * 128:kt * 128 + kn], rhs=q_t[:, h, 0:n], start=True, stop=False),
                                     reads=[R_kv, R_q], writes=[R_ps[bk]])
                                k.op("pe", lambda e: e.matmul(ps[bk][0:kn, 0:n], lhsT=krT[:, kt * 128:kt * 128 + kn], rhs=qr_t[:, h, 0:n], start=False, stop=True),
                                     reads=[R_kv, R_qr], writes=[R_ps[bk]])
                            p_t, R_p, _ = pt_r.next()
                            nh = len(kts)
                            kn = 128 if kts[-1] < 32 else 16
                            src = psd[bp][0:kn, :].rearrange("p (b m) -> p b m", b=2)[:, 0:nh, 0:n]
                            k.op("act", lambda e: e.activation(out=p_t[0:kn, 0:nh, 0:n], in_=src, func=AF.Exp), reads=[R_ps[2 * bp], R_ps[2 * bp + 1]], writes=[R_p])
                            return (p, kts, p_t, R_p)

                        def pv(item):
                            p, kts, p_t, R_p = item
                            for i, kt in enumerate(kts):
                                kn = 128 if kt < 32 else 16
                                k.op("pe", lambda e: e.matmul(ps[ob][:, 0:n], lhsT=V[0:kn, kt, h * 128:(h + 1) * 128], rhs=p_t[0:kn, i, 0:n], start=(kt == 0), stop=(kt == NT - 1)),
                                     reads=[R_kv, R_p], writes=[R_ps[ob]])
                            nh = len(kts)
                            kn = 128 if kts[-1] < 32 else 16
                            if p % 3 == 2:
                                for i, kt in enumerate(kts):
                                    k.op("pe", lambda e: e.matmul(ps[sumb][:, 0:n], lhsT=ones_b[0:kn, :], rhs=p_t[0:kn, i, 0:n], start=(p == 2 and i == 0), stop=False),
                                         reads=[R_c, R_p], writes=[R_ps[sumb]])
                            elif p == 0:
                                k.op("dve", lambda e: e.tensor_copy(out=acc0[:, :, 0:n], in_=p_t[:, :, 0:n]), reads=[R_p], writes=[R_a0])
                            else:
                                k.op("dve", lambda e: e.tensor_tensor(out=acc0[0:kn, 0:nh, 0:n], in0=acc0[0:kn, 0:nh, 0:n], in1=p_t[0:kn, 0:nh, 0:n], op=ALU.add), reads=[R_p, R_a0], writes=[R_a0])

                        prev = None
                        for p in range(NPAIR):
                            cur = qk(p)
                            if prev is not None:
                                pv(prev)
                            prev = cur
                        pv(prev)
                        k.op("dve", lambda e: e.tensor_tensor(out=acc0[:, 0, 0:n], in0=acc0[:, 0, 0:n], in1=acc0[:, 1, 0:n], op=ALU.add), reads=[R_a0], writes=[R_a0])
                        k.op("pe", lambda e: e.matmul(ps[sumb][:, 0:n], lhsT=ones_f[:], rhs=acc0[:, 0, 0:n], start=False, stop=True), reads=[R_a0, R_c], writes=[R_ps[sumb]])
                        rinv, R_ri, _ = rinv_r.next()
                        k.op("dve", lambda e: e.reciprocal(out=rinv[:, 0:n], in_=ps[sumb][:, 0:n]), reads=[R_ps[sumb]], writes=[R_ri])
                        o1, R_o1, _ = o1_r.next()
                        k.op("dve", lambda e: e.tensor_tensor(out=o1[:, 0:n], in0=ps[ob][:, 0:n], in1=rinv[:, 0:n], op=ALU.mult), reads=[R_ps[ob], R_ri], writes=[R_o1])
                        k.op("pool", lambda e: e.tensor_tensor(out=mx_t[:, h, 0:n], in0=o1[:, 0:n], in1=ga_t[:, h, 0:n], op=ALU.mult), reads=[R_o1, R_ga], writes=[R_mx])
                    k.op("sp", lambda e: e.dma_start(out=mixT_s[s, 512:1024, t0:t0 + n].rearrange("(c p) t -> p c t", p=128), in_=mx_t[:, :, 0:n]), reads=[R_mx], dma=d_mx)
            k.end_phase()

        def phase_C(l):
            with contextlib.ExitStack() as ls:
                Pc = sb("Pc", [128, NS, NT, 512], BF16, ls)
                R_pc = Res("pc")
                dpc = k.dsem(f"d_pc{l}")
                for s in range(NS):
                    k.op("sp", lambda e, s=s: e.dma_start(out=Pc[:, s, 0:32, :], in_=pcs_s[s, 0:4096, :].rearrange("(j p) n -> p j n", p=128)), writes=[R_pc], dma=dpc)
                    k.op("sp", lambda e, s=s: e.dma_start(out=Pc[0:16, s, 32, :], in_=pcs_s[s, 4096:L, :]), writes=[R_pc], dma=dpc)
                GC = 8
                c_r = Ring(k, nc, ls, "dC", [128, GC, 512], BF16, 3)
                s_r = Ring(k, nc, ls, "dS", [128, GC, 512], BF16, 3)
                gf_r = Ring(k, nc, ls, "gfl", [128, 512], BF16, 4)
                mf_r = Ring(k, nc, ls, "mfl", [128, 512], BF16, 4)
                groups = [(0, 8), (8, 8), (16, 8), (24, 8), (32, 1)]
                for ki, (k0, n) in enumerate(STILES):
                    banks = [[(ki % 2) * 4 + s2 * 2 + a for a in range(2)] for s2 in range(NS)]
                    first = True
                    for (g0, gn) in groups:
                        ct, R_ct, d_ct = c_r.next()
                        st_, R_st, d_st = s_r.next()
                        if gn == 8:
                            k.op("sp", lambda e, ct=ct, g0=g0: e.dma_start(out=ct[:, :, 0:n], in_=dftC[g0 * 128:(g0 + 8) * 128, k0:k0 + n].rearrange("(c p) k -> p c k", p=128)), writes=[R_ct], dma=d_ct)
                            k.op("sp", lambda e, st_=st_, g0=g0: e.dma_start(out=st_[:, :, 0:n], in_=dftS[g0 * 128:(g0 + 8) * 128, k0:k0 + n].rearrange("(c p) k -> p c k", p=128)), writes=[R_st], dma=d_st)
                        else:
                            k.op("sp", lambda e, ct=ct: e.dma_start(out=ct[0:16, 0, 0:n], in_=dftC[4096:L, k0:k0 + n]), writes=[R_ct], dma=d_ct)
                            k.op("sp", lambda e, st_=st_: e.dma_start(out=st_[0:16, 0, 0:n], in_=dftS[4096:L, k0:k0 + n]), writes=[R_st], dma=d_st)
                        for ci in range(gn):
                            ch = g0 + ci
                            ln = 128 if ch < 32 else 16
                            for cs, (xt, R_x) in enumerate([(ct, R_ct), (st_, R_st)]):
                                last = (ch == NT - 1) and (cs == 1)
                                for s2 in range(NS):
                                    for a in range(2):
                                        bk = banks[s2][a]
                                        k.op("pe", lambda e, bk=bk, s2=s2, a=a, ch=ch, ln=ln, cs=cs, xt=xt, ci=ci, first=first, last=last:
                                             e.matmul(ps[bk][:, 0:n], lhsT=Pc[0:ln, s2, ch, cs * 256 + a * 128:cs * 256 + (a + 1) * 128], rhs=xt[0:ln, ci, 0:n], start=first, stop=last),
                                             reads=[R_pc, R_x], writes=[R_ps[bk]])
                                first = False
                    for s2 in range(NS):
                        for a in range(2):
                            bk = banks[s2][a]
                            gf, R_gf, d_gf = gf_r.next()
                            mf, R_mf, d_mf = mf_r.next()
                            k.op("sp", lambda e, gf=gf, s2=s2, a=a: e.dma_start(out=gf[:, 0:n], in_=gTf_s[s2, a * 128:(a + 1) * 128, k0:k0 + n]), writes=[R_gf], dma=d_gf)
                            k.op("dve", lambda e, gf=gf, mf=mf, bk=bk: e.tensor_tensor(out=mf[:, 0:n], in0=ps[bk][:, 0:n], in1=gf[:, 0:n], op=ALU.mult), reads=[R_ps[bk], R_gf], writes=[R_mf])
                            k.op("sp", lambda e, mf=mf, s2=s2, a=a: e.dma_start(out=mixT_s[s2, a * 128:(a + 1) * 128, k0:k0 + n], in_=mf[:, 0:n]), reads=[R_mf], dma=d_mf)
            k.end_phase()

        def phase_D(l, s, h_src, last):
            with contextlib.ExitStack() as ls:
                mx_r = Ring(k, nc, ls, "dmx", [128, 8, 512], BF16, 2)
                h_r = Ring(k, nc, ls, "dh", [128, 1024], F32, 3)
                o_r = Ring(k, nc, ls, "do", [128, 1024], F32, 3)
                ss_r = Ring(k, nc, ls, "dss", [128, 1], F32, 4, with_dsem=False)
                junk = sb("djunk", [128, 1024], BF16, ls)
                R_fn = Res("fnw")
                if last:
                    fnw = sb("fnw", [128, 1024], F32, ls)
                    dfn = k.dsem(f"d_fn{s}")
                    k.op("sp", lambda e: e.dma_start(out=fnw[:], in_=final_norm_w.partition_broadcast(128)), writes=[R_fn], dma=dfn)
                OBK = [0, 1, 2, 3]
                obi = [0]
                for (t0, n) in STILES:
                    mx, R_mx, d_mx = mx_r.next()
                    k.op("sp", lambda e, mx=mx: e.dma_start(out=mx[:, :, 0:n], in_=mixT_s[s, :, t0:t0 + n].rearrange("(c p) t -> p c t", p=128)), writes=[R_mx], dma=d_mx)
                    for j, (r0, nr) in enumerate(tiles_of(t0, n)):
                        ht, R_ht, d_ht = h_r.next()
                        ot, R_ot, d_ot = o_r.next()
                        k.op("sp", lambda e, ht=ht, nr=nr, r0=r0: e.dma_start(out=ht[0:nr, :], in_=h_src[s, t0 + r0:t0 + r0 + nr, :]), writes=[R_ht], dma=d_ht)
                        for half in range(2):
                            bk = OBK[obi[0] % 4]
                            obi[0] += 1
                            for c in range(8):
                                k.op("pe", lambda e, bk=bk, c=c, half=half, mx=mx, r0=r0, nr=nr: e.matmul(ps[bk][0:nr, :], lhsT=mx[:, c, r0:r0 + nr], rhs=Wout[:, c, half * 512:(half + 1) * 512],
                                                                                                     start=(c == 0), stop=(c == 7)), reads=[R_mx, R_w], writes=[R_ps[bk]])
                            k.op("dve", lambda e, bk=bk, half=half, ht=ht, ot=ot, nr=nr: e.tensor_tensor(out=ot[0:nr, half * 512:(half + 1) * 512], in0=ps[bk][0:nr, :], in1=ht[0:nr, half * 512:(half + 1) * 512], op=ALU.add),
                                 reads=[R_ps[bk], R_ht], writes=[R_ot])
                        if not last:
                            k.op("sp", lambda e, ot=ot, nr=nr, r0=r0: e.dma_start(out=hbuf[s, t0 + r0:t0 + r0 + nr, :], in_=ot[0:nr, :]), reads=[R_ot], dma=d_ot)
                        else:
                            ss, R_ss, _ = ss_r.next()
                            k.op("act", lambda e, ot=ot, nr=nr, ss=ss: e.activation(out=junk[0:nr, :], in_=ot[0:nr, :], func=AF.Square, scale=float(DM ** -0.5), accum_out=ss[0:nr, 0:1]), reads=[R_ot], writes=[R_ss])
                            k.op("act", lambda e, nr=nr, ss=ss: e.activation(out=ss[0:nr, 0:1], in_=ss[0:nr, 0:1], func=AF.Sqrt, bias=eps_t[0:nr, 0:1]), reads=[R_ss], writes=[R_ss])
                            k.op("dve", lambda e, nr=nr, ss=ss: e.reciprocal(out=ss[0:nr, 0:1], in_=ss[0:nr, 0:1]), reads=[R_ss], writes=[R_ss])
                            k.op("dve", lambda e, ot=ot, nr=nr, ss=ss: e.scalar_tensor_tensor(out=ot[0:nr, :], in0=ot[0:nr, :], scalar=ss[0:nr, 0:1], in1=fnw[0:nr, :], op0=ALU.mult, op1=ALU.mult),
                                 reads=[R_ot, R_ss, R_fn], writes=[R_ot])
                            g0 = t0 + r0
                            lo = max(g0, NMETA)
                            hi = g0 + nr
                            if hi > lo:
                                k.op("sp", lambda e, ot=ot, lo=lo, hi=hi, g0=g0: e.dma_start(out=out_d[s, lo - NMETA:hi - NMETA, :], in_=ot[lo - g0:hi - g0, :]), reads=[R_ot], dma=d_ot)
            k.end_phase()

        for li, l in enumerate(layers):
            h_src = h0 if (li == 0 and first_from_input) else hbuf
            if "P" in phases:
                phase_P(l)
            for s in range(NS):
                if "A" in phases:
                    phase_A(l, s, h_src)
            for s in range(NS):
                if "B" in phases:
                    phase_B(l, s)
            if "C" in phases:
                phase_C(l)
            last = final_norm and (li == len(layers) - 1)
            for s in range(NS):
                if "D" in phases:
                    phase_D(l, s, h_src, last)
        k.emit()
        build_program.nops = k.nops
    return nc


_PROG = {}


def get_prog(key, **kw):
    if key not in _PROG:
        _PROG[key] = build_program(**kw)
    return _PROG[key]


def make_in_maps(inputs, h0_full):
    c = host_consts()
    shared = {
        "norm_w": inputs["norm_w"], "w_in": inputs["w_in"], "fourier_w": inputs["fourier_w"], "pool_w": inputs["pool_w"],
        "pool_scale": inputs["pool_scale"], "q_norm_w": inputs["q_norm_w"], "w_uq": inputs["w_uq"], "kv_norm_w": inputs["kv_norm_w"],
        "w_ukv": inputs["w_ukv"], "w_out": inputs["w_out"], "final_norm_w": inputs["final_norm_w"],
    }
    shared = {k_: np.ascontiguousarray(np.asarray(v, dtype=np.float32)) for k_, v in shared.items()}
    shared.update(c)
    maps = []
    for i in range(NCORES):
        m = dict(shared)
        m["h0"] = h0_full[i * NS:(i + 1) * NS]
        maps.append(m)
    return maps


def kernel(x, meta_tokens, norm_w, w_in, fourier_w, pool_w, pool_scale, q_norm_w, w_uq, kv_norm_w, w_ukv, w_out, final_norm_w):
    inputs = dict(norm_w=norm_w, w_in=w_in, fourier_w=fourier_w, pool_w=pool_w, pool_scale=pool_scale, q_norm_w=q_norm_w,
                  w_uq=w_uq, kv_norm_w=kv_norm_w, w_ukv=w_ukv, w_out=w_out, final_norm_w=final_norm_w)
    x = np.asarray(x, dtype=np.float32)
    B = x.shape[0]
    meta = np.asarray(meta_tokens, dtype=np.float32)
    h0 = np.concatenate([np.broadcast_to(meta[None], (B, NMETA, DM)), x], axis=1)
    h0 = np.ascontiguousarray(h0)
    nc = get_prog("full", layers=list(range(DEPTH)))
    maps = make_in_maps(inputs, h0)
    res = run_bass_kernel_spmd(nc, maps, core_ids=list(range(NCORES)))
    out = np.concatenate([np.asarray(r["out"]) for r in res.results], axis=0)
    return out.astype(np.float32)
```

```python
import contextlib
import numpy as np
import ml_dtypes
import concourse.bass as bass
import concourse.mybir as mybir
from concourse.bass_utils import run_bass_kernel_spmd

F32 = mybir.dt.float32
BF16 = mybir.dt.bfloat16
AF = mybir.ActivationFunctionType
ALU = mybir.AluOpType

NCORES = 8
NS = 2
DM = 1024
SEQ = 4096
NMETA = 16
L = SEQ + NMETA
DEPTH = 4
DIN = 2240
EPS = 1e-6
NT = 33
TM_COLS = 1152
FM_COLS = 1408
SCALE = 192 ** -0.5

STILES = [(i * 512, 512) for i in range(8)] + [(4096, 16)]


def tiles_of(t0, n):
    if n == 512:
        return [(j * 128, 128) for j in range(4)]
    return [(0, n)]


class SemObj:
    def __init__(self, handle, name):
        self.h = handle
        self.name = name
        self.count = 0


class Res:
    __slots__ = ("name", "w", "r")

    def __init__(self, name=""):
        self.name = name
        self.w = None
        self.r = {}


class _Rec:
    def __init__(self):
        self.call = None

    def __getattr__(self, name):
        def f(*a, **kw):
            self.call = (name, a, kw)
            return None
        return f


class K:
    ENGS = ("pe", "act", "dve", "pool", "sp")

    def __init__(self, nc, stack):
        self.nc = nc
        self.stack = stack
        self.ops = {e: [] for e in self.ENGS}
        self.esem = {e: SemObj(stack.enter_context(nc.semaphore("s_" + e)), "s_" + e) for e in self.ENGS}
        self.waited = {e: {} for e in self.ENGS}
        self.dsems = []
        self.free_dsems = []
        self.phase_dsems = []
        self.nops = 0

    def dsem(self, name):
        if self.free_dsems:
            s = self.free_dsems.pop()
        else:
            s = SemObj(self.stack.enter_context(self.nc.semaphore(f"dsem{len(self.dsems)}")), f"dsem{len(self.dsems)}")
            self.dsems.append(s)
        self.phase_dsems.append(s)
        return s

    def op(self, eng, fn, reads=(), writes=(), dma=None):
        rec = _Rec()
        fn(rec)
        call = rec.call
        assert call is not None
        deps = []
        for r in reads:
            if r.w is not None:
                deps.append(r.w)
        for w in writes:
            if w.w is not None:
                deps.append(w.w)
            deps.extend(w.r.values())
        my = self.esem[eng]
        waits = {}
        wd = self.waited[eng]
        for (s, v) in deps:
            if s is my and eng == "pe":
                continue
            if wd.get(s.name, 0) >= v:
                continue
            if waits.get(s.name, (None, 0))[1] < v:
                waits[s.name] = (s, v)
        for (s, v) in waits.values():
            wd[s.name] = v
        if dma is not None:
            dma.count += 16
            mark = (dma, dma.count)
            inc = (dma, 16)
        else:
            my.count += 1
            mark = (my, my.count)
            inc = (my, 1)
        for r in reads:
            r.r[mark[0].name] = mark
        for w in writes:
            w.w = mark
            w.r = {}
        self.ops[eng].append((list(waits.values()), call, inc))
        self.nops += 1
        return mark

    def barrier(self):
        allsems = list(self.esem.values()) + self.dsems
        for eng in self.ENGS:
            waits = []
            wd = self.waited[eng]
            for s in allsems:
                if s is self.esem[eng]:
                    continue
                if s.count > wd.get(s.name, 0):
                    waits.append((s, s.count))
                    wd[s.name] = s.count
            if waits:
                self.ops[eng].append((waits, None, None))

    def end_phase(self):
        self.barrier()
        self.free_dsems.extend(self.phase_dsems)
        self.phase_dsems = []

    def emit(self):
        nc = self.nc
        with nc.Block() as block:
            def run(engname):
                def body(e):
                    for (waits, call, inc) in self.ops[engname]:
                        for (s, v) in waits:
                            e.wait_ge(s.h, v)
                        if call is not None:
                            name, a, kw = call
                            ins = getattr(e, name)(*a, **kw)
                            ins.then_inc(inc[0].h, inc[1])
                return body
            block.tensor(run("pe"))
            block.scalar(run("act"))
            block.vector(run("dve"))
            block.gpsimd(run("pool"))
            block.sync(run("sp"))


_UID = [0]


class Ring:
    def __init__(self, k, nc, st, name, shape, dtype, n, with_dsem=True):
        self.items = []
        for i in range(n):
            _UID[0] += 1
            t = st.enter_context(nc.sbuf_tensor(f"rg_{name}{i}_{_UID[0]}", shape, dtype))
            self.items.append((t, Res(f"{name}{i}"), k.dsem(f"d_{name}{i}") if with_dsem else None))
        self.i = 0

    def next(self):
        it = self.items[self.i % len(self.items)]
        self.i += 1
        return it


_CONSTS = None


def host_consts():
    global _CONSTS
    if _CONSTS is not None:
        return _CONSTS
    bf = ml_dtypes.bfloat16
    c = {}
    c["ident"] = np.eye(128, dtype=np.float32).astype(bf)
    idx = np.arange(L, dtype=np.int64)
    lk = (idx[:, None] * idx[None, :]) % L
    ang = lk.astype(np.float64) * (2.0 * np.pi / L)
    c["dftC"] = (np.cos(ang) / np.sqrt(L)).astype(np.float32).astype(bf)
    c["dftS"] = (np.sin(ang) / np.sqrt(L)).astype(np.float32).astype(bf)
    del lk, ang
    i64 = np.arange(64, dtype=np.int64)
    a64 = ((i64[:, None] * i64[None, :]) % 64).astype(np.float64) * (2.0 * np.pi / 64)
    c64 = np.cos(a64) / 8.0
    s64 = -np.sin(a64) / 8.0
    cb = np.zeros((128, 128), np.float32)
    sb = np.zeros((128, 128), np.float32)
    for b in range(2):
        cb[b * 64:(b + 1) * 64, b * 64:(b + 1) * 64] = c64
        sb[b * 64:(b + 1) * 64, b * 64:(b + 1) * 64] = s64
    c["c64bd"] = cb.astype(bf)
    c["s64bd"] = sb.astype(bf)
    inv = (1.0 / (np.float32(10000.0) ** (np.arange(0, 64, 2, dtype=np.float32) / np.float32(64)))).astype(np.float32)
    angr = (np.arange(L, dtype=np.float32)[:, None] * inv[None, :]).astype(np.float32)
    cosr = np.cos(angr).astype(np.float32).T
    sinr = np.sin(angr).astype(np.float32).T
    c["cos4"] = np.ascontiguousarray(np.tile(cosr, (4, 1)))
    c["sin4"] = np.ascontiguousarray(np.tile(sinr, (4, 1)))
    wins = (2, 4, 8, 16)
    invw = np.zeros((128, 2), np.float32)
    elo = np.zeros((128, 2, 8), np.float32)
    ehi = np.zeros((128, 2, 8), np.float32)
    for a in range(2):
        for p in range(128):
            w = wins[2 * a + p // 64]
            invw[p, a] = 1.0 / w
            for j in range(8):
                i = j
                cnt = min(i + w // 2, L) - max(i - w // 2, 0)
                elo[p, a, j] = 1.0 / cnt
                i = L - 8 + j
                cnt = min(i + w // 2, L) - max(i - w // 2, 0)
                ehi[p, a, j] = 1.0 / cnt
    c["pinvw"] = invw
    c["pelo"] = elo
    c["pehi"] = ehi
    _CONSTS = c
    return c


def build_program(layers, first_from_input=True, final_norm=True, debug=False, phases="PABCD"):
    nc = bass.Bass("TRN2", target_bir_lowering=False)
    dk = "ExternalOutput" if debug else "Internal"

    def din(name, shape, dt=F32):
        return nc.dram_tensor(name, list(shape), dt, kind="ExternalInput").ap()

    h0 = din("h0", [NS, L, DM])
    norm_w = din("norm_w", [DEPTH, DM])
    w_in = din("w_in", [DEPTH, DM, DIN])
    fourier_w = din("fourier_w", [DEPTH, 4, 64, 64])
    pool_w = din("pool_w", [DEPTH, 4, 64, 64])
    pool_scale = din("pool_scale", [DEPTH, 256])
    q_norm_w = din("q_norm_w", [DEPTH, 384])
    w_uq = din("w_uq", [DEPTH, 384, 768])
    kv_norm_w = din("kv_norm_w", [DEPTH, 256])
    w_ukv = din("w_ukv", [DEPTH, 256, 1024])
    w_out = din("w_out", [DEPTH, DM, DM])
    final_norm_w = din("final_norm_w", [DM])
    ident_d = din("ident", [128, 128], BF16)
    dftC = din("dftC", [L, L], BF16)
    dftS = din("dftS", [L, L], BF16)
    c64bd_d = din("c64bd", [128, 128], BF16)
    s64bd_d = din("s64bd", [128, 128], BF16)
    cos4_d = din("cos4", [128, L])
    sin4_d = din("sin4", [128, L])
    pinvw_d = din("pinvw", [128, 2])
    pelo_d = din("pelo", [128, 2, 8])
    pehi_d = din("pehi", [128, 2, 8])

    if final_norm:
        out_d = nc.dram_tensor("out", [NS, SEQ, DM], F32, kind="ExternalOutput").ap()
        hbuf = nc.dram_tensor("hbuf", [NS, L, DM], F32, kind="Internal").ap()
    else:
        out_d = None
        hbuf = nc.dram_tensor("hbuf", [NS, L, DM], F32, kind="ExternalOutput").ap()
    pcs_s = nc.dram_tensor("pcs_s", [NS, L, 512], BF16, kind=dk).ap()
    qT_s = nc.dram_tensor("qT_s", [NS, 768, L], BF16, kind=dk).ap()
    kT_s = nc.dram_tensor("kT_s", [NS, 576, L], BF16, kind=dk).ap()
    v_s = nc.dram_tensor("v_s", [NS, L, 512], BF16, kind=dk).ap()
    gTa_s = nc.dram_tensor("gTa_s", [NS, 512, L], BF16, kind=dk).ap()
    gTf_s = nc.dram_tensor("gTf_s", [NS, 256, L], BF16, kind=dk).ap()
    mixT_s = nc.dram_tensor("mixT_s", [NS, 1024, L], BF16, kind=dk).ap()

    with contextlib.ExitStack() as st:
        k = K(nc, st)

        def sb(name, shape, dt, stack=st):
            _UID[0] += 1
            return stack.enter_context(nc.sbuf_tensor(f"sb_{name}_{_UID[0]}", list(shape), dt))

        ident = sb("ident", [128, 128], BF16)
        ones_f = sb("ones_f", [128, 128], F32)
        ones_b = sb("ones_b", [128, 128], BF16)
        eps_t = sb("eps_t", [128, 1], F32)
        Win_tm = sb("Win_tm", [128, 8, TM_COLS], BF16)
        Win_fm = sb("Win_fm", [128, 8, FM_COLS], BF16)
        Wuq = sb("Wuq", [128, 3, 1024], BF16)
        Wukv = sb("Wukv", [128, 2, 1024], BF16)
        Wout = sb("Wout", [128, 8, 1024], BF16)
        Pbd = sb("Pbd", [128, 2, 128], BF16)
        psc = sb("psc", [128, 2], F32)
        c64bd = sb("c64bd", [128, 128], BF16)
        s64bd = sb("s64bd", [128, 128], BF16)
        pinvw = sb("pinvw", [128, 2], F32)
        pelo = sb("pelo", [128, 2, 8], F32)
        pehi = sb("pehi", [128, 2, 8], F32)
        R_w = Res("weights")
        R_c = Res("consts")
        psd = [st.enter_context(nc.psum_tensor(f"psd{i}", [128, 1024], F32)) for i in range(4)]
        ps = [psd[i // 2][:, (i % 2) * 512:(i % 2 + 1) * 512] for i in range(8)]
        R_ps = [Res(f"ps{i}") for i in range(8)]
        dconst = k.dsem("d_const")

        def ld_const(dst, src):
            k.op("sp", lambda e: e.dma_start(out=dst, in_=src), writes=[R_c], dma=dconst)

        ld_const(ident[:], ident_d)
        ld_const(c64bd[:], c64bd_d)
        ld_const(s64bd[:], s64bd_d)
        ld_const(pinvw[:], pinvw_d)
        ld_const(pelo[:], pelo_d)
        ld_const(pehi[:], pehi_d)
        k.op("pool", lambda e: e.memset(ones_f[:], 1.0), writes=[R_c])
        k.op("pool", lambda e: e.memset(ones_b[:], 1.0), writes=[R_c])
        k.op("pool", lambda e: e.memset(eps_t[:], EPS), writes=[R_c])
        k.end_phase()

        def phase_P(l):
            with contextlib.ExitStack() as ls:
                stg = Ring(k, nc, ls, "wst", [128, 8, 640], F32, 2)
                nw = sb("nw", [128, 8], F32, ls)
                qnw = sb("qnw", [128, 3], F32, ls)
                kvnw = sb("kvnw", [128, 2], F32, ls)
                wf_st = sb("wf_st", [128, 2, 64], F32, ls)
                pw_st = sb("pw_st", [128, 2, 64], F32, ls)
                wf_b = sb("wf_b", [128, 2, 64], BF16, ls)
                Mbd = sb("Mbd", [128, 2, 2, 128], BF16, ls)
                Wf_bf = sb("Wf_bf", [128, 8, 256], BF16, ls)
                WfT = sb("WfT", [128, 2, 1024], BF16, ls)
                R_small = Res("small")
                R_wfb = Res("wfb")
                R_mbd = Res("mbd")
                R_wfbf = Res("wfbf")
                R_wft = Res("wft")
                dsm = k.dsem(f"d_small{l}")

                def ld_small(dst, src):
                    k.op("sp", lambda e: e.dma_start(out=dst, in_=src, allow_slow_non_contiguous=True),
                         writes=[R_small], dma=dsm)

                ld_small(nw[:], norm_w[l].rearrange("(c p) -> p c", p=128))
                ld_small(qnw[:], q_norm_w[l].rearrange("(c p) -> p c", p=128))
                ld_small(kvnw[:], kv_norm_w[l].rearrange("(c p) -> p c", p=128))
                ld_small(psc[:], pool_scale[l].rearrange("(a p) -> p a", p=128))
                ld_small(wf_st[:], fourier_w[l].rearrange("(a hh) c d -> (hh c) a d", hh=2))
                ld_small(pw_st[:], pool_w[l].rearrange("(a hh) c d -> (hh c) a d", hh=2))

                engs = ["act", "dve"]
                ei = [0]

                def scaled_cast(dst, src, sc_ap, mul2=None, eng=None):
                    if eng is None:
                        eng = engs[ei[0] % 2]
                        ei[0] += 1
                    if eng == "act":
                        if mul2 is None:
                            fn = lambda e: e.activation(out=dst, in_=src, func=AF.Copy, scale=sc_ap)
                        else:
                            eng = "dve"
                    if eng != "act":
                        if mul2 is None:
                            fn = lambda e: e.tensor_scalar(out=dst, in0=src, scalar1=sc_ap, scalar2=None, op0=ALU.mult)
                        else:
                            fn = lambda e: e.tensor_scalar(out=dst, in0=src, scalar1=sc_ap, scalar2=float(mul2),
                                                           op0=ALU.mult, op1=ALU.mult)
                    return eng, fn

                Wv = w_in[l].rearrange("(c p) n -> p c n", p=128)
                pieces = [(0, 512), (512, 1024), (1024, 1664), (1664, 2240)]
                for (c0, c1) in pieces:
                    t, R_t, ds = stg.next()
                    wcols = c1 - c0
                    k.op("sp", lambda e, t=t, c0=c0, c1=c1, wcols=wcols: e.dma_start(out=t[:, :, 0:wcols], in_=Wv[:, :, c0:c1]),
                         writes=[R_t], dma=ds)
                    jobs = []
                    if c0 == 0:
                        jobs.append((Wf_bf, 0, 0, 256, None, R_wfbf))
                        jobs.append((Win_fm, 0, 256, 256, None, R_w))
                    elif c0 == 512:
                        jobs.append((Win_fm, 1024, 512, 256, None, R_w))
                        jobs.append((Win_fm, 256, 768, 256, None, R_w))
                    elif c0 == 1024:
                        jobs.append((Win_tm, 512, 1024, 384, None, R_w))
                        jobs.append((Win_tm, 896, 1408, 256, None, R_w))
                    else:
                        jobs.append((Win_fm, 1280, 1664, 64, None, R_w))
                        jobs.append((Win_fm, 1344, 1696, 32, -1.0, R_w))
                        jobs.append((Win_fm, 1376, 1664, 32, None, R_w))
                        jobs.append((Win_fm, 512, 1728, 512, None, R_w))
                    for (dt_, d0, s0, ncol, mul2, rr) in jobs:
                        for c in range(8):
                            eng, fn = scaled_cast(dt_[:, c, d0:d0 + ncol], t[:, c, s0 - c0:s0 - c0 + ncol], nw[:, c:c + 1], mul2)
                            k.op(eng, fn, reads=[R_t, R_small], writes=[rr])

                t, R_t, ds = stg.next()
                tq = t[:].rearrange("p c n -> p (c n)")[:, 0:3 * 768].rearrange("p (c n) -> p c n", c=3)
                k.op("sp", lambda e: e.dma_start(out=tq, in_=w_uq[l].rearrange("(c p) n -> p c n", p=128)),
                     writes=[R_t], dma=ds)
                for c in range(3):
                    srcv = tq[:, c, :].rearrange("p (h e) -> p h e", h=4)
                    sc = qnw[:, c:c + 1]
                    eng, fn = scaled_cast(Wuq[:, c, 0:512].rearrange("p (h e) -> p h e", h=4), srcv[:, :, 0:128], sc, SCALE, "dve")
                    k.op(eng, fn, reads=[R_t, R_small], writes=[R_w])
                    eng, fn = scaled_cast(Wuq[:, c, 512:768].rearrange("p (h e) -> p h e", h=4), srcv[:, :, 128:192], sc, SCALE, "dve")
                    k.op(eng, fn, reads=[R_t, R_small], writes=[R_w])
                    rot = Wuq[:, c, 768:1024].rearrange("p (h e) -> p h e", h=4)
                    eng, fn = scaled_cast(rot[:, :, 0:32], srcv[:, :, 160:192], sc, -SCALE, "dve")
                    k.op(eng, fn, reads=[R_t, R_small], writes=[R_w])
                    eng, fn = scaled_cast(rot[:, :, 32:64], srcv[:, :, 128:160], sc, SCALE, "dve")
                    k.op(eng, fn, reads=[R_t, R_small], writes=[R_w])
                t, R_t, ds = stg.next()
                tkv = t[:].rearrange("p c n -> p (c n)")[:, 0:2 * 1024].rearrange("p (c n) -> p c n", c=2)
                k.op("sp", lambda e: e.dma_start(out=tkv, in_=w_ukv[l].rearrange("(c p) n -> p c n", p=128)),
                     writes=[R_t], dma=ds)
                for c in range(2):
                    srcv = tkv[:, c, :].rearrange("p (h e) -> p h e", h=4)
                    sc = kvnw[:, c:c + 1]
                    eng, fn = scaled_cast(Wukv[:, c, 0:512].rearrange("p (h e) -> p h e", h=4), srcv[:, :, 0:128], sc, None, "act")
                    k.op(eng, fn, reads=[R_t, R_small], writes=[R_w])
                    eng, fn = scaled_cast(Wukv[:, c, 512:1024].rearrange("p (h e) -> p h e", h=4), srcv[:, :, 128:256], sc, None, "dve")
                    k.op(eng, fn, reads=[R_t, R_small], writes=[R_w])
                Wo = w_out[l].rearrange("(c p) n -> p c n", p=128)
                for half in range(2):
                    t, R_t, ds = stg.next()
                    tv = t[:].rearrange("p c n -> p (c n)")[:, 0:4096].rearrange("p (c n) -> p c n", c=4)
                    k.op("sp", lambda e, tv=tv, half=half: e.dma_start(out=tv, in_=Wo[:, half * 4:(half + 1) * 4, :]),
                         writes=[R_t], dma=ds)
                    for c in range(4):
                        eng = ["act", "dve"][c % 2]
                        if eng == "act":
                            fn = lambda e, tv=tv, c=c, half=half: e.activation(out=Wout[:, half * 4 + c, :], in_=tv[:, c, :], func=AF.Copy)
                        else:
                            fn = lambda e, tv=tv, c=c, half=half: e.tensor_copy(out=Wout[:, half * 4 + c, :], in_=tv[:, c, :])
                        k.op(eng, fn, reads=[R_t], writes=[R_w])
                k.op("pool", lambda e: e.memset(Pbd[:], 0.0), writes=[R_w])
                for a in range(2):
                    k.op("pool", lambda e, a=a: e.tensor_copy(out=Pbd[0:64, a, 0:64], in_=pw_st[0:64, a, :]), reads=[R_small], writes=[R_w])
                    k.op("pool", lambda e, a=a: e.tensor_copy(out=Pbd[64:128, a, 64:128], in_=pw_st[64:128, a, :]), reads=[R_small], writes=[R_w])
                k.op("dve", lambda e: e.tensor_copy(out=wf_b[:], in_=wf_st[:]), reads=[R_small], writes=[R_wfb])
                k.op("pool", lambda e: e.memset(Mbd[:], 0.0), writes=[R_mbd])
                for a in range(2):
                    for cs in range(2):
                        blk = (a * 2 + cs) * 64
                        lh = c64bd if cs == 0 else s64bd
                        k.op("pe", lambda e, a=a, blk=blk, lh=lh: e.matmul(ps[0][:, blk:blk + 64], lhsT=lh[:], rhs=wf_b[:, a, :], start=True, stop=True),
                             reads=[R_c, R_wfb], writes=[R_ps[0]])
                for a in range(2):
                    for cs in range(2):
                        blk = (a * 2 + cs) * 64
                        k.op("dve", lambda e, a=a, cs=cs, blk=blk: e.tensor_copy(out=Mbd[0:64, a, cs, 0:64], in_=ps[0][0:64, blk:blk + 64]),
                             reads=[R_ps[0]], writes=[R_mbd])
                        k.op("dve", lambda e, a=a, cs=cs, blk=blk: e.tensor_copy(out=Mbd[64:128, a, cs, 64:128], in_=ps[0][64:128, blk:blk + 64]),
                             reads=[R_ps[0]], writes=[R_mbd])
                for a in range(2):
                    pt = ps[1 + a][:].bitcast(BF16)
                    for c in range(8):
                        k.op("pe", lambda e, a=a, c=c, pt=pt: e.transpose(out=pt[:, c * 128:(c + 1) * 128], in_=Wf_bf[:, c, a * 128:(a + 1) * 128], identity=ident[:]),
                             reads=[R_wfbf, R_c], writes=[R_ps[1 + a]])
                    k.op("act", lambda e, a=a, pt=pt: e.activation(out=WfT[:, a, :], in_=pt, func=AF.Copy), reads=[R_ps[1 + a]], writes=[R_wft])
                for c in range(8):
                    pb = 3 + (c % 2)
                    for cs in range(2):
                        for a in range(2):
                            col = cs * 256 + a * 128
                            k.op("pe", lambda e, c=c, cs=cs, a=a, col=col, pb=pb: e.matmul(ps[pb][:, col:col + 128], lhsT=WfT[:, a, c * 128:(c + 1) * 128],
                                                                                         rhs=Mbd[:, a, cs, :], start=True, stop=True),
                                 reads=[R_wft, R_mbd], writes=[R_ps[pb]])
                    k.op("dve", lambda e, c=c, pb=pb: e.tensor_copy(out=Win_tm[:, c, 0:512], in_=ps[pb][:, :]), reads=[R_ps[pb]], writes=[R_w])
            k.end_phase()

        def phase_A(l, s, h_src):
            with contextlib.ExitStack() as ls:
                Xp = sb("Xp", [128, 2, L + 16], BF16, ls)
                spg = sb("spg", [128, 2, L], BF16, ls)
                ls2 = ls.enter_context(contextlib.ExitStack())
                hring = Ring(k, nc, ls2, "ht", [128, 1024], F32, 4)
                xn_ring = Ring(k, nc, ls2, "xn", [128, 1024], BF16, 4, with_dsem=False)
                cn_ring = Ring(k, nc, ls2, "cn", [128, 640], BF16, 4, with_dsem=False)
                xnT_ring = [(sb(f"xnT{i}", [128, 8, 512], BF16, ls2), [Res() for _ in range(4)]) for i in range(2)]
                cnT_ring = [(sb(f"cnT{i}", [128, 5, 512], BF16, ls2), [Res() for _ in range(4)]) for i in range(2)]
                pcs_st = Ring(k, nc, ls2, "pcs_st", [128, 4, 512], BF16, 1)
                v_st = Ring(k, nc, ls2, "v_st", [128, 4, 512], BF16, 1)
                gf_st = Ring(k, nc, ls2, "gf_st", [128, 2, 512], BF16, 1)
                ga_st = Ring(k, nc, ls2, "ga_st", [128, 4, 512], BF16, 1)
                q_st = Ring(k, nc, ls2, "q_st", [128, 6, 512], BF16, 1)
                k_st = Ring(k, nc, ls2, "k_st", [128, 4, 512], BF16, 1)
                kr_st = Ring(k, nc, ls2, "kr_st", [64, 512], BF16, 1)
                cos_r = Ring(k, nc, ls2, "cos_r", [128, 512], F32, 2)
                sin_r = Ring(k, nc, ls2, "sin_r", [128, 512], F32, 2)
                t1_r = Ring(k, nc, ls2, "t1_r", [128, 512], F32, 2, with_dsem=False)
                t2_r = Ring(k, nc, ls2, "t2_r", [128, 512], F32, 2, with_dsem=False)
                junk_a = sb("junk_a", [128, 1024], BF16, ls2)
                ss_r = Ring(k, nc, ls2, "ss_r", [128, 4], F32, 8, with_dsem=False)
                R_xp = [Res("xp0"), Res("xp1")]
                R_spg = [Res("spg0"), Res("spg1")]
                PT = [0, 1]
                TMB = [2, 3, 4]
                VB = 5
                FMB = [2, 3, 6, 7]
                fmi = [0]
                pti = [0]

                k.op("pool", lambda e: e.memset(Xp[:, :, 0:8], 0.0), writes=R_xp)
                k.op("pool", lambda e: e.memset(Xp[:, :, L + 8:L + 16], 0.0), writes=R_xp)

                TMSETS = [[2, 3, 4], [5, 6, 7]]
                tmi = [0]
                st_state = {}

                def S1a(u):
                    t0, n = STILES[u]
                    xnT, R_xnT = xnT_ring[u % 2]
                    stt = {"tiles": []}
                    st_state[u] = stt
                    for j, (r0, nr) in enumerate(tiles_of(t0, n)):
                        ht, R_ht, d_ht = hring.next()
                        k.op("sp", lambda e: e.dma_start(out=ht[0:nr, :], in_=h_src[s, t0 + r0:t0 + r0 + nr, :]), writes=[R_ht], dma=d_ht)
                        ss, R_ss, _ = ss_r.next()
                        k.op("act", lambda e: e.activation(out=junk_a[0:nr, :], in_=ht[0:nr, :], func=AF.Square, scale=float(DM ** -0.5), accum_out=ss[0:nr, 0:1]),
                             reads=[R_ht], writes=[R_ss])
                        k.op("act", lambda e: e.activation(out=ss[0:nr, 0:1], in_=ss[0:nr, 0:1], func=AF.Sqrt, bias=eps_t[0:nr, 0:1]),
                             reads=[R_ss], writes=[R_ss])
                        k.op("dve", lambda e: e.reciprocal(out=ss[0:nr, 0:1], in_=ss[0:nr, 0:1]), reads=[R_ss], writes=[R_ss])
                        xn, R_xn, _ = xn_ring.next()
                        k.op("dve", lambda e: e.tensor_scalar(out=xn[0:nr, :], in0=ht[0:nr, :], scalar1=ss[0:nr, 0:1], scalar2=None, op0=ALU.mult),
                             reads=[R_ht, R_ss], writes=[R_xn])
                        stt["tiles"].append({"ss": ss, "R_ss": R_ss, "xn": (xn, R_xn)})

                def S1b(u):
                    t0, n = STILES[u]
                    xnT, R_xnT = xnT_ring[u % 2]
                    stt = st_state[u]
                    for j, (r0, nr) in enumerate(tiles_of(t0, n)):
                        xn, R_xn = stt["tiles"][j]["xn"]
                        pb = PT[pti[0] % 2]
                        pti[0] += 1
                        ptv = ps[pb][:].bitcast(BF16)
                        for c in range(8):
                            k.op("pe", lambda e: e.transpose(out=ptv[:, c * 128:c * 128 + nr], in_=xn[0:nr, c * 128:(c + 1) * 128], identity=ident[0:nr, 0:nr]),
                                 reads=[R_xn, R_c], writes=[R_ps[pb]])
                        k.op("act", lambda e: e.activation(out=xnT[:, :, r0:r0 + nr], in_=ptv.rearrange("p (c t) -> p c t", c=8)[:, :, 0:nr], func=AF.Copy),
                             reads=[R_ps[pb]], writes=[R_xnT[j]])

                def S2(u):
                    t0, n = STILES[u]
                    xnT, R_xnT = xnT_ring[u % 2]
                    stt = st_state[u]
                    pcs_t, R_pcs, d_pcs = pcs_st.next()
                    stt["pcs"] = (pcs_t, R_pcs, d_pcs)
                    for j, (r0, nr) in enumerate(tiles_of(t0, n)):
                        ss, R_ss = stt["tiles"][j]["ss"], stt["tiles"][j]["R_ss"]
                        TMB = TMSETS[tmi[0] % 2]
                        tmi[0] += 1
                        tm_specs = [(TMB[0], 0, 512), (TMB[1], 512, 384), (TMB[2], 896, 256)]
                        for (bk, c0, ncol) in tm_specs:
                            for c in range(8):
                                k.op("pe", lambda e: e.matmul(ps[bk][0:nr, 0:ncol], lhsT=xnT[:, c, r0:r0 + nr], rhs=Win_tm[:, c, c0:c0 + ncol], start=(c == 0), stop=(c == 7)),
                                     reads=[R_xnT[j], R_w], writes=[R_ps[bk]])
                        k.op("act", lambda e: e.activation(out=pcs_t[0:nr, j, :], in_=ps[TMB[0]][0:nr, :], func=AF.Copy),
                             reads=[R_ps[TMB[0]]], writes=[R_pcs])
                        k.op("act", lambda e: e.activation(out=junk_a[0:nr, 0:384], in_=ps[TMB[1]][0:nr, 0:384], func=AF.Square, scale=float(384 ** -0.5), accum_out=ss[0:nr, 1:2]),
                             reads=[R_ps[TMB[1]]], writes=[R_ss])
                        k.op("act", lambda e: e.activation(out=junk_a[0:nr, 0:256], in_=ps[TMB[2]][0:nr, 0:256], func=AF.Square, scale=float(256 ** -0.5), accum_out=ss[0:nr, 2:3]),
                             reads=[R_ps[TMB[2]]], writes=[R_ss])
                        k.op("act", lambda e: e.activation(out=ss[0:nr, 1:3], in_=ss[0:nr, 1:3], func=AF.Sqrt, bias=eps_t[0:nr, 0:1]),
                             reads=[R_ss], writes=[R_ss])
                        k.op("dve", lambda e: e.reciprocal(out=ss[0:nr, 1:3], in_=ss[0:nr, 1:3]), reads=[R_ss], writes=[R_ss])
                        cn, R_cn, _ = cn_ring.next()
                        k.op("dve", lambda e: e.tensor_scalar(out=cn[0:nr, 0:384], in0=ps[TMB[1]][0:nr, 0:384], scalar1=ss[0:nr, 1:2], scalar2=None, op0=ALU.mult),
                             reads=[R_ps[TMB[1]], R_ss], writes=[R_cn])
                        k.op("dve", lambda e: e.tensor_scalar(out=cn[0:nr, 384:640], in0=ps[TMB[2]][0:nr, 0:256], scalar1=ss[0:nr, 2:3], scalar2=None, op0=ALU.mult),
                             reads=[R_ps[TMB[2]], R_ss], writes=[R_cn])
                        stt["tiles"][j]["cn"] = (cn, R_cn)
                    if n == 512:
                        k.op("sp", lambda e: e.dma_start(out=pcs_s[s, t0:t0 + 512, :].rearrange("(j p) n -> p j n", p=128), in_=pcs_t[:]), reads=[R_pcs], dma=d_pcs)
                    else:
                        k.op("sp", lambda e: e.dma_start(out=pcs_s[s, t0:t0 + n, :], in_=pcs_t[0:n, 0, :]), reads=[R_pcs], dma=d_pcs)

                def S3t(u):
                    t0, n = STILES[u]
                    cnT, R_cnT = cnT_ring[u % 2]
                    stt = st_state[u]
                    v_t, R_v, d_v = v_st.next()
                    for j, (r0, nr) in enumerate(tiles_of(t0, n)):
                        cn, R_cn = stt["tiles"][j]["cn"]
                        pb = PT[pti[0] % 2]
                        pti[0] += 1
                        ptv = ps[pb][:].bitcast(BF16)
                        for c in range(5):
                            k.op("pe", lambda e: e.transpose(out=ptv[:, c * 128:c * 128 + nr], in_=cn[0:nr, c * 128:(c + 1) * 128], identity=ident[0:nr, 0:nr]),
                                 reads=[R_cn, R_c], writes=[R_ps[pb]])
                        k.op("dve", lambda e: e.tensor_copy(out=cnT[:, :, r0:r0 + nr], in_=ptv[:, 0:640].rearrange("p (c t) -> p c t", c=5)[:, :, 0:nr]),
                             reads=[R_ps[pb]], writes=[R_cnT[j]])
                    stt["v"] = (v_t, R_v, d_v)

                def S3v(u):
                    t0, n = STILES[u]
                    cnT, R_cnT = cnT_ring[u % 2]
                    stt = st_state[u]
                    v_t, R_v, d_v = stt["v"]
                    for j, (r0, nr) in enumerate(tiles_of(t0, n)):
                        vb = [5, 4][j % 2]
                        for c in range(2):
                            k.op("pe", lambda e: e.matmul(ps[vb][0:nr, :], lhsT=cnT[:, 3 + c, r0:r0 + nr], rhs=Wukv[:, c, 512:1024], start=(c == 0), stop=(c == 1)),
                                 reads=[R_cnT[j], R_w], writes=[R_ps[vb]])
                        k.op("dve", lambda e: e.tensor_copy(out=v_t[0:nr, j, :], in_=ps[vb][0:nr, :]), reads=[R_ps[vb]], writes=[R_v])
                    if n == 512:
                        k.op("sp", lambda e: e.dma_start(out=v_s[s, t0:t0 + 512, :].rearrange("(j p) n -> p j n", p=128), in_=v_t[:]), reads=[R_v], dma=d_v)
                    else:
                        k.op("sp", lambda e: e.dma_start(out=v_s[s, t0:t0 + n, :], in_=v_t[0:n, 0, :]), reads=[R_v], dma=d_v)

                def load_tables(u):
                    t0, n = STILES[u]
                    cos_t, R_cos, d_cos = cos_r.next()
                    sin_t, R_sin, d_sin = sin_r.next()
                    k.op("sp", lambda e: e.dma_start(out=cos_t[:, 0:n], in_=cos4_d[:, t0:t0 + n]), writes=[R_cos], dma=d_cos)
                    k.op("sp", lambda e: e.dma_start(out=sin_t[:, 0:n], in_=sin4_d[:, t0:t0 + n]), writes=[R_sin], dma=d_sin)
                    st_state[u]["tab"] = (cos_t, R_cos, sin_t, R_sin)

                def FM(u):
                    t0, n = STILES[u]
                    xnT, R_xnT = xnT_ring[u % 2]
                    ntl = len(tiles_of(t0, n))
                    cos_t, R_cos, sin_t, R_sin = st_state[u]["tab"]

                    def fm_group(col0, m):
                        bk = FMB[fmi[0] % len(FMB)]
                        fmi[0] += 1
                        for c in range(8):
                            k.op("pe", lambda e: e.matmul(ps[bk][0:m, 0:n], lhsT=Win_fm[:, c, col0:col0 + m], rhs=xnT[:, c, 0:n], start=(c == 0), stop=(c == 7)),
                                 reads=R_xnT[0:ntl] + [R_w], writes=[R_ps[bk]])
                        return bk

                    gf_t, R_gf, d_gf = gf_st.next()
                    ga_t, R_ga, d_ga = ga_st.next()
                    for oc in range(2):
                        bk = fm_group(oc * 128, 128)
                        k.op("act", lambda e: e.activation(out=gf_t[:, oc, 0:n], in_=ps[bk][:, 0:n], func=AF.Silu), reads=[R_ps[bk]], writes=[R_gf])
                    k.op("sp", lambda e: e.dma_start(out=gTf_s[s, :, t0:t0 + n].rearrange("(c p) t -> p c t", p=128), in_=gf_t[:, :, 0:n]), reads=[R_gf], dma=d_gf)
                    for oc in range(2):
                        bk = fm_group(256 + oc * 128, 128)
                        k.op("act", lambda e: e.activation(out=spg[:, oc, t0:t0 + n], in_=ps[bk][:, 0:n], func=AF.Silu), reads=[R_ps[bk]], writes=[R_spg[oc]])
                    for oc in range(4):
                        bk = fm_group(512 + oc * 128, 128)
                        k.op("act", lambda e: e.activation(out=ga_t[:, oc, 0:n], in_=ps[bk][:, 0:n], func=AF.Silu), reads=[R_ps[bk]], writes=[R_ga])
                    k.op("sp", lambda e: e.dma_start(out=gTa_s[s, :, t0:t0 + n].rearrange("(c p) t -> p c t", p=128), in_=ga_t[:, :, 0:n]), reads=[R_ga], dma=d_ga)
                    for oc in range(2):
                        bk = fm_group(1024 + oc * 128, 128)
                        k.op("dve", lambda e: e.tensor_copy(out=Xp[:, oc, 8 + t0:8 + t0 + n], in_=ps[bk][:, 0:n]), reads=[R_ps[bk]], writes=[R_xp[oc]])
                    bkA = fm_group(1280, 64)
                    bkB = fm_group(1344, 64)
                    t1, R_t1, _ = t1_r.next()
                    t2, R_t2, _ = t2_r.next()
                    kr_t, R_kr, d_kr = kr_st.next()
                    k.op("dve", lambda e: e.tensor_tensor(out=t1[0:64, 0:n], in0=ps[bkA][0:64, 0:n], in1=cos_t[0:64, 0:n], op=ALU.mult),
                         reads=[R_ps[bkA], R_cos], writes=[R_t1])
                    k.op("dve", lambda e: e.tensor_tensor(out=t2[0:64, 0:n], in0=ps[bkB][0:64, 0:n], in1=sin_t[0:64, 0:n], op=ALU.mult),
                         reads=[R_ps[bkB], R_sin], writes=[R_t2])
                    k.op("pool", lambda e: e.tensor_tensor(out=kr_t[:, 0:n], in0=t1[0:64, 0:n], in1=t2[0:64, 0:n], op=ALU.add),
                         reads=[R_t1, R_t2], writes=[R_kr])
                    k.op("sp", lambda e: e.dma_start(out=kT_s[s, 512:576, t0:t0 + n], in_=kr_t[:, 0:n]), reads=[R_kr], dma=d_kr)

                def QKG(u):
                    t0, n = STILES[u]
                    cnT, R_cnT = cnT_ring[u % 2]
                    ntl = len(tiles_of(t0, n))
                    cos_t, R_cos, sin_t, R_sin = st_state[u]["tab"]

                    def cn_group(Wt, nk, kc0, col0):
                        bk = FMB[fmi[0] % len(FMB)]
                        fmi[0] += 1
                        for c in range(nk):
                            k.op("pe", lambda e: e.matmul(ps[bk][:, 0:n], lhsT=Wt[:, c, col0:col0 + 128], rhs=cnT[:, kc0 + c, 0:n], start=(c == 0), stop=(c == nk - 1)),
                                 reads=R_cnT[0:ntl] + [R_w], writes=[R_ps[bk]])
                        return bk

                    q_t, R_q, d_q = q_st.next()
                    for oc in range(4):
                        bk = cn_group(Wuq, 3, 0, oc * 128)
                        if oc % 2 == 0:
                            k.op("act", lambda e: e.activation(out=q_t[:, oc, 0:n], in_=ps[bk][:, 0:n], func=AF.Copy), reads=[R_ps[bk]], writes=[R_q])
                        else:
                            k.op("dve", lambda e: e.tensor_copy(out=q_t[:, oc, 0:n], in_=ps[bk][:, 0:n]), reads=[R_ps[bk]], writes=[R_q])
                    for a in range(2):
                        bkA = cn_group(Wuq, 3, 0, 512 + a * 128)
                        bkB = cn_group(Wuq, 3, 0, 768 + a * 128)
                        t1, R_t1, _ = t1_r.next()
                        t2, R_t2, _ = t2_r.next()
                        k.op("dve", lambda e: e.tensor_tensor(out=t1[:, 0:n], in0=ps[bkA][:, 0:n], in1=cos_t[:, 0:n], op=ALU.mult),
                             reads=[R_ps[bkA], R_cos], writes=[R_t1])
                        k.op("dve", lambda e: e.tensor_tensor(out=t2[:, 0:n], in0=ps[bkB][:, 0:n], in1=sin_t[:, 0:n], op=ALU.mult),
                             reads=[R_ps[bkB], R_sin], writes=[R_t2])
                        k.op("pool", lambda e: e.tensor_tensor(out=q_t[:, 4 + a, 0:n], in0=t1[:, 0:n], in1=t2[:, 0:n], op=ALU.add),
                             reads=[R_t1, R_t2], writes=[R_q])
                    k.op("sp", lambda e: e.dma_start(out=qT_s[s, :, t0:t0 + n].rearrange("(c p) t -> p c t", p=128), in_=q_t[:, :, 0:n]), reads=[R_q], dma=d_q)
                    k_t, R_k, d_k = k_st.next()
                    for oc in range(4):
                        bk = cn_group(Wukv, 2, 3, oc * 128)
                        if oc % 2 == 0:
                            k.op("act", lambda e: e.activation(out=k_t[:, oc, 0:n], in_=ps[bk][:, 0:n], func=AF.Copy), reads=[R_ps[bk]], writes=[R_k])
                        else:
                            k.op("dve", lambda e: e.tensor_copy(out=k_t[:, oc, 0:n], in_=ps[bk][:, 0:n]), reads=[R_ps[bk]], writes=[R_k])
                    k.op("sp", lambda e: e.dma_start(out=kT_s[s, 0:512, t0:t0 + n].rearrange("(c p) t -> p c t", p=128), in_=k_t[:, :, 0:n]), reads=[R_k], dma=d_k)

                NU = len(STILES)
                S1a(0)
                S1b(0)
                for u in range(NU):
                    load_tables(u)
                    S2(u)
                    if u + 1 < NU:
                        S1a(u + 1)
                    FM(u)
                    S3t(u)
                    if u + 1 < NU:
                        S1b(u + 1)
                    S3v(u)
                    QKG(u)

                k.barrier()
                ls2.close()
                CB = 1028
                sA = Ring(k, nc, ls, "poolA", [128, CB + 16], F32, 2, with_dsem=False)
                sB = Ring(k, nc, ls, "poolB", [128, CB + 16], F32, 2, with_dsem=False)
                pm_st = Ring(k, nc, ls, "pm_st", [128, CB], BF16, 2)
                ym = Ring(k, nc, ls, "ym", [128, CB], BF16, 2, with_dsem=False)
                for a in range(2):
                    for b in range(4):
                        o = b * CB
                        A_, R_A, _ = sA.next()
                        B_, R_B, _ = sB.next()
                        X = Xp[:, a, o:o + CB + 16]
                        W_ = CB + 16
                        eng1 = "pool" if (b % 2 == 0) else "dve"
                        eng2 = "dve"
                        k.op(eng1, lambda e, A_=A_, X=X, W_=W_: e.tensor_tensor(out=A_[:, 1:W_], in0=X[:, 0:W_ - 1], in1=X[:, 1:W_], op=ALU.add), reads=[R_xp[a]], writes=[R_A])
                        k.op(eng1, lambda e, A_=A_, B_=B_, W_=W_: e.tensor_tensor(out=B_[64:128, 2:W_ - 1] if a == 0 else B_[:, 2:W_ - 1],
                                                                               in0=A_[64:128, 1:W_ - 2] if a == 0 else A_[:, 1:W_ - 2],
                                                                               in1=A_[64:128, 3:W_] if a == 0 else A_[:, 3:W_], op=ALU.add), reads=[R_A], writes=[R_B])
                        if a == 0:
                            k.op(eng1, lambda e, A_=A_, B_=B_, W_=W_: e.tensor_copy(out=B_[0:64, 2:W_ - 1], in_=A_[0:64, 2:W_ - 1]), reads=[R_A], writes=[R_B])
                            Sfin, R_S = B_, R_B
                        else:
                            k.op(eng1, lambda e, A_=A_, B_=B_, W_=W_: e.tensor_tensor(out=A_[:, 4:W_ - 3], in0=B_[:, 2:W_ - 5], in1=B_[:, 6:W_ - 1], op=ALU.add), reads=[R_B], writes=[R_A])
                            k.op(eng1, lambda e, A_=A_, B_=B_, W_=W_: e.tensor_tensor(out=B_[64:128, 8:W_ - 7], in0=A_[64:128, 4:W_ - 11], in1=A_[64:128, 12:W_ - 3], op=ALU.add), reads=[R_A], writes=[R_B])
                            k.op(eng1, lambda e, A_=A_, B_=B_, W_=W_: e.tensor_copy(out=B_[0:64, 8:W_ - 7], in_=A_[0:64, 8:W_ - 7]), reads=[R_A], writes=[R_B])
                            Sfin, R_S = B_, R_B
                        y_, R_y, _ = ym.next()
                        k.op(eng2, lambda e, Sfin=Sfin, X=X, y_=y_, a=a: e.scalar_tensor_tensor(out=y_[:, 0:CB], in0=Sfin[:, 8:8 + CB], scalar=pinvw[:, a:a + 1], in1=X[:, 8:8 + CB],
                                                                                              op0=ALU.mult, op1=ALU.subtract), reads=[R_S, R_xp[a], R_c], writes=[R_y])
                        if b == 0:
                            k.op(eng2, lambda e, Sfin=Sfin, a=a: e.tensor_tensor(out=Sfin[:, 8:16], in0=Sfin[:, 8:16], in1=pelo[:, a, :], op=ALU.mult), reads=[R_S, R_c, R_y], writes=[R_S])
                            k.op(eng2, lambda e, Sfin=Sfin, X=X, y_=y_: e.tensor_tensor(out=y_[:, 0:8], in0=Sfin[:, 8:16], in1=X[:, 8:16], op=ALU.subtract), reads=[R_S, R_xp[a]], writes=[R_y])
                        if b == 3:
                            k.op(eng2, lambda e, Sfin=Sfin, a=a: e.tensor_tensor(out=Sfin[:, CB:CB + 8], in0=Sfin[:, CB:CB + 8], in1=pehi[:, a, :], op=ALU.mult), reads=[R_S, R_c, R_y], writes=[R_S])
                            k.op(eng2, lambda e, Sfin=Sfin, X=X, y_=y_: e.tensor_tensor(out=y_[:, CB - 8:CB], in0=Sfin[:, CB:CB + 8], in1=X[:, CB:CB + 8], op=ALU.subtract), reads=[R_S, R_xp[a]], writes=[R_y])
                        pm_t, R_pm, d_pm = pm_st.next()
                        for (c0, cw) in [(0, 512), (512, 512), (1024, CB - 1024)]:
                            bk = FMB[fmi[0] % len(FMB)]
                            fmi[0] += 1
                            k.op("pe", lambda e, bk=bk, y_=y_, c0=c0, cw=cw, a=a: e.matmul(ps[bk][:, 0:cw], lhsT=Pbd[:, a, :], rhs=y_[:, c0:c0 + cw], start=True, stop=True),
                                 reads=[R_y, R_w], writes=[R_ps[bk]])
                            k.op("dve", lambda e, bk=bk, c0=c0, cw=cw, a=a, pm_t=pm_t, o=o: e.scalar_tensor_tensor(out=pm_t[:, c0:c0 + cw], in0=ps[bk][:, 0:cw], scalar=psc[:, a:a + 1],
                                                                                                             in1=spg[:, a, o + c0:o + c0 + cw], op0=ALU.mult, op1=ALU.mult),
                                 reads=[R_ps[bk], R_spg[a], R_w], writes=[R_pm])
                        k.op("sp", lambda e, pm_t=pm_t, a=a, o=o: e.dma_start(out=mixT_s[s, 256 + a * 128:256 + (a + 1) * 128, o:o + CB], in_=pm_t[:, :]), reads=[R_pm], dma=d_pm)
            k.end_phase()

        def phase_B(l, s):
            with contextlib.ExitStack() as ls:
                kT = sb("kT", [128, 4, L], BF16, ls)
                krT = sb("krT", [128, L], BF16, ls)
                V = sb("V", [128, NT, 512], BF16, ls)
                R_kv = Res("kv")
                dkv = k.dsem(f"d_kv{l}_{s}")
                k.op("sp", lambda e: e.dma_start(out=kT[:], in_=kT_s[s, 0:512, :].rearrange("(c p) t -> p c t", p=128)), writes=[R_kv], dma=dkv)
                k.op("sp", lambda e: e.dma_start(out=krT[0:64, :], in_=kT_s[s, 512:576, :]), writes=[R_kv], dma=dkv)
                k.op("pool", lambda e: e.memset(krT[64:128, :], 0.0), writes=[R_kv])
                k.op("sp", lambda e: e.dma_start(out=V[:, 0:32, :], in_=v_s[s, 0:4096, :].rearrange("(j p) n -> p j n", p=128)), writes=[R_kv], dma=dkv)
                k.op("sp", lambda e: e.dma_start(out=V[0:16, 32, :], in_=v_s[s, 4096:L, :]), writes=[R_kv], dma=dkv)
                q_r = Ring(k, nc, ls, "qb", [128, 4, 512], BF16, 2)
                qr_r = Ring(k, nc, ls, "qrb", [128, 4, 512], BF16, 2)
                for (qt_, Rq_, _) in qr_r.items:
                    k.op("pool", lambda e: e.memset(qt_[64:128, :, :], 0.0), writes=[Rq_])
                ga_r = Ring(k, nc, ls, "gab", [128, 4, 512], BF16, 3)
                mx_r = Ring(k, nc, ls, "mxb", [128, 4, 512], BF16, 2)
                pt_r = Ring(k, nc, ls, "ptb", [128, 2, 512], BF16, 3, with_dsem=False)
                acc0_r = Ring(k, nc, ls, "acc0", [128, 2, 512], F32, 2, with_dsem=False)
                rinv_r = Ring(k, nc, ls, "rinv", [128, 512], F32, 1, with_dsem=False)
                o1_r = Ring(k, nc, ls, "o1", [128, 512], F32, 1, with_dsem=False)
                SPAIR = [0, 1]
                OB = [4, 5]
                SUMBS = [6, 7]
                spi = [0]
                obi = [0]
                NPAIR = 17
                def b_loads(u):
                    t0, n = STILES[u]
                    q_t, R_q, d_q = q_r.next()
                    qr_t, R_qr, d_qr = qr_r.next()
                    ga_t, R_ga, d_ga = ga_r.next()
                    k.op("sp", lambda e: e.dma_start(out=q_t[:, :, 0:n], in_=qT_s[s, 0:512, t0:t0 + n].rearrange("(c p) t -> p c t", p=128)), writes=[R_q], dma=d_q)
                    k.op("sp", lambda e: e.dma_start(out=qr_t[0:64, :, 0:n], in_=qT_s[s, 512:768, t0:t0 + n].rearrange("(c p) t -> p c t", p=64)), writes=[R_qr], dma=d_qr)
                    k.op("sp", lambda e: e.dma_start(out=ga_t[:, :, 0:n], in_=gTa_s[s, :, t0:t0 + n].rearrange("(c p) t -> p c t", p=128)), writes=[R_ga], dma=d_ga)
                    return (q_t, R_q, qr_t, R_qr, ga_t, R_ga)

                pending_fin = [None]
                nxt = b_loads(0)
                for u, (t0, n) in enumerate(STILES):
                    q_t, R_q, qr_t, R_qr, ga_t, R_ga = nxt
                    if u + 1 < len(STILES):
                        nxt = b_loads(u + 1)
                    mx_t, R_mx, d_mx = mx_r.next()
                    for h in range(4):
                        ob = OB[obi[0] % 2]
                        obi[0] += 1
                        acc0, R_a0, _ = acc0_r.next()
                        sumb = SUMBS[(obi[0] - 1) % 2]

                        def qk(p):
                            bp = SPAIR[spi[0] % 2]
                            spi[0] += 1
                            kts = [2 * p, 2 * p + 1] if p < NPAIR - 1 else [2 * p]
                            for i, kt in enumerate(kts):
                                kn = 128 if kt < 32 else 16
                                bk = 2 * bp + i
                                k.op("pe", lambda e: e.matmul(ps[bk][0:kn, 0:n], lhsT=kT[:, h, kt * 128:kt * 128 + kn], rhs=q_t[:, h, 0:n], start=True, stop=False),
                                     reads=[R_kv, R_q], writes=[R_ps[bk]])
                                k.op("pe", lambda e: e.matmul(ps[bk][0:kn, 0:n], lhsT=krT[:, kt * 128:kt * 128 + kn], rhs=qr_t[:, h, 0:n], start=False, stop=True),
                                     reads=[R_kv, R_qr], writes=[R_ps[bk]])
                            p_t, R_p, _ = pt_r.next()
                            nh = len(kts)
                            kn = 128 if kts[-1] < 32 else 16
                            src = psd[bp][0:kn, :].rearrange("p (b m) -> p b m", b=2)[:, 0:nh, 0:n]
                            k.op("act", lambda e: e.activation(out=p_t[0:kn, 0:nh, 0:n], in_=src, func=AF.Exp), reads=[R_ps[2 * bp], R_ps[2 * bp + 1]], writes=[R_p])
                            return (p, kts, p_t, R_p)

                        def pv(item):
                            p, kts, p_t, R_p = item
                            for i, kt in enumerate(kts):
                                kn = 128 if kt < 32 else 16
                                k.op("pe", lambda e: e.matmul(ps[ob][:, 0:n], lhsT=V[0:kn, kt, h * 128:(h + 1) * 128], rhs=p_t[0:kn, i, 0:n], start=(kt == 0), stop=(kt == NT - 1)),
                                     reads=[R_kv, R_p], writes=[R_ps[ob]])
                            nh = len(kts)
                            kn = 128 if kts[-1] < 32 else 16
                            if p % 3 == 2:
                                for i, kt in enumerate(kts):
                                    k.op("pe", lambda e: e.matmul(ps[sumb][:, 0:n], lhsT=ones_b[0:kn, :], rhs=p_t[0:kn, i, 0:n], start=(p == 2 and i == 0), stop=False),
                                         reads=[R_c, R_p], writes=[R_ps[sumb]])
                            elif p == 0:
                                k.op("dve", lambda e: e.tensor_copy(out=acc0[:, :, 0:n], in_=p_t[:, :, 0:n]), reads=[R_p], writes=[R_a0])
                            else:
                                k.op("dve", lambda e: e.tensor_tensor(out=acc0[0:kn, 0:nh, 0:n], in0=acc0[0:kn, 0:nh, 0:n], in1=p_t[0:kn, 0:nh, 0:n], op=ALU.add), reads=[R_p, R_a0], writes=[R_a0])

                        def make_fin(acc0=acc0, R_a0=R_a0, sumb=sumb, ob=ob, n=n, h=h, mx_t=mx_t, R_mx=R_mx, d_mx=d_mx, ga_t=ga_t, R_ga=R_ga, t0=t0):
                            def fin():
                                k.op("dve", lambda e: e.tensor_tensor(out=acc0[:, 0, 0:n], in0=acc0[:, 0, 0:n], in1=acc0[:, 1, 0:n], op=ALU.add), reads=[R_a0], writes=[R_a0])
                                k.op("pe", lambda e: e.matmul(ps[sumb][:, 0:n], lhsT=ones_f[:], rhs=acc0[:, 0, 0:n], start=False, stop=True), reads=[R_a0, R_c], writes=[R_ps[sumb]])
                                rinv, R_ri, _ = rinv_r.next()
                                k.op("dve", lambda e: e.reciprocal(out=rinv[:, 0:n], in_=ps[sumb][:, 0:n]), reads=[R_ps[sumb]], writes=[R_ri])
                                o1, R_o1, _ = o1_r.next()
                                k.op("dve", lambda e: e.tensor_tensor(out=o1[:, 0:n], in0=ps[ob][:, 0:n], in1=rinv[:, 0:n], op=ALU.mult), reads=[R_ps[ob], R_ri], writes=[R_o1])
                                k.op("pool", lambda e: e.tensor_tensor(out=mx_t[:, h, 0:n], in0=o1[:, 0:n], in1=ga_t[:, h, 0:n], op=ALU.mult), reads=[R_o1, R_ga], writes=[R_mx])
                                if h == 3:
                                    k.op("sp", lambda e: e.dma_start(out=mixT_s[s, 512:1024, t0:t0 + n].rearrange("(c p) t -> p c t", p=128), in_=mx_t[:, :, 0:n]), reads=[R_mx], dma=d_mx)
                            return fin

                        prev = None
                        for p in range(NPAIR):
                            cur = qk(p)
                            if prev is not None:
                                pv(prev)
                            prev = cur
                            if p == 2 and pending_fin[0] is not None:
                                pending_fin[0]()
                                pending_fin[0] = None
                        pv(prev)
                        if pending_fin[0] is not None:
                            pending_fin[0]()
                        pending_fin[0] = make_fin()
                if pending_fin[0] is not None:
                    pending_fin[0]()
            k.end_phase()

        def phase_C(l):
            with contextlib.ExitStack() as ls:
                Pc = sb("Pc", [128, NS, NT, 512], BF16, ls)
                R_pc = Res("pc")
                dpc = k.dsem(f"d_pc{l}")
                for s in range(NS):
                    k.op("sp", lambda e, s=s: e.dma_start(out=Pc[:, s, 0:32, :], in_=pcs_s[s, 0:4096, :].rearrange("(j p) n -> p j n", p=128)), writes=[R_pc], dma=dpc)
                    k.op("sp", lambda e, s=s: e.dma_start(out=Pc[0:16, s, 32, :], in_=pcs_s[s, 4096:L, :]), writes=[R_pc], dma=dpc)
                GC = 8
                c_r = Ring(k, nc, ls, "dC", [128, GC, 512], BF16, 3)
                s_r = Ring(k, nc, ls, "dS", [128, GC, 512], BF16, 3)
                gf_r = Ring(k, nc, ls, "gfl", [128, 512], BF16, 4)
                mf_r = Ring(k, nc, ls, "mfl", [128, 512], BF16, 4)
                groups = [(0, 8), (8, 8), (16, 8), (24, 8), (32, 1)]
                for ki, (k0, n) in enumerate(STILES):
                    banks = [[(ki % 2) * 4 + s2 * 2 + a for a in range(2)] for s2 in range(NS)]
                    first = True
                    for (g0, gn) in groups:
                        ct, R_ct, d_ct = c_r.next()
                        st_, R_st, d_st = s_r.next()
                        if gn == 8:
                            k.op("sp", lambda e, ct=ct, g0=g0: e.dma_start(out=ct[:, :, 0:n], in_=dftC[g0 * 128:(g0 + 8) * 128, k0:k0 + n].rearrange("(c p) k -> p c k", p=128)), writes=[R_ct], dma=d_ct)
                            k.op("sp", lambda e, st_=st_, g0=g0: e.dma_start(out=st_[:, :, 0:n], in_=dftS[g0 * 128:(g0 + 8) * 128, k0:k0 + n].rearrange("(c p) k -> p c k", p=128)), writes=[R_st], dma=d_st)
                        else:
                            k.op("sp", lambda e, ct=ct: e.dma_start(out=ct[0:16, 0, 0:n], in_=dftC[4096:L, k0:k0 + n]), writes=[R_ct], dma=d_ct)
                            k.op("sp", lambda e, st_=st_: e.dma_start(out=st_[0:16, 0, 0:n], in_=dftS[4096:L, k0:k0 + n]), writes=[R_st], dma=d_st)
                        for ci in range(gn):
                            ch = g0 + ci
                            ln = 128 if ch < 32 else 16
                            for cs, (xt, R_x) in enumerate([(ct, R_ct), (st_, R_st)]):
                                last = (ch == NT - 1) and (cs == 1)
                                for s2 in range(NS):
                                    for a in range(2):
                                        bk = banks[s2][a]
                                        k.op("pe", lambda e, bk=bk, s2=s2, a=a, ch=ch, ln=ln, cs=cs, xt=xt, ci=ci, first=first, last=last:
                                             e.matmul(ps[bk][:, 0:n], lhsT=Pc[0:ln, s2, ch, cs * 256 + a * 128:cs * 256 + (a + 1) * 128], rhs=xt[0:ln, ci, 0:n], start=first, stop=last),
                                             reads=[R_pc, R_x], writes=[R_ps[bk]])
                                first = False
                    for s2 in range(NS):
                        for a in range(2):
                            bk = banks[s2][a]
                            gf, R_gf, d_gf = gf_r.next()
                            mf, R_mf, d_mf = mf_r.next()
                            k.op("sp", lambda e, gf=gf, s2=s2, a=a: e.dma_start(out=gf[:, 0:n], in_=gTf_s[s2, a * 128:(a + 1) * 128, k0:k0 + n]), writes=[R_gf], dma=d_gf)
                            k.op("dve", lambda e, gf=gf, mf=mf, bk=bk: e.tensor_tensor(out=mf[:, 0:n], in0=ps[bk][:, 0:n], in1=gf[:, 0:n], op=ALU.mult), reads=[R_ps[bk], R_gf], writes=[R_mf])
                            k.op("sp", lambda e, mf=mf, s2=s2, a=a: e.dma_start(out=mixT_s[s2, a * 128:(a + 1) * 128, k0:k0 + n], in_=mf[:, 0:n]), reads=[R_mf], dma=d_mf)
            k.end_phase()

        def phase_D(l, s, h_src, last):
            with contextlib.ExitStack() as ls:
                mx_r = Ring(k, nc, ls, "dmx", [128, 8, 512], BF16, 2)
                h_r = Ring(k, nc, ls, "dh", [128, 1024], F32, 3)
                o_r = Ring(k, nc, ls, "do", [128, 1024], F32, 3)
                ss_r = Ring(k, nc, ls, "dss", [128, 1], F32, 4, with_dsem=False)
                junk = sb("djunk", [128, 1024], BF16, ls)
                R_fn = Res("fnw")
                if last:
                    fnw = sb("fnw", [128, 1024], F32, ls)
                    dfn = k.dsem(f"d_fn{s}")
                    k.op("sp", lambda e: e.dma_start(out=fnw[:], in_=final_norm_w.partition_broadcast(128)), writes=[R_fn], dma=dfn)
                OBK = [0, 1, 2, 3]
                obi = [0]
                for (t0, n) in STILES:
                    mx, R_mx, d_mx = mx_r.next()
                    k.op("sp", lambda e, mx=mx: e.dma_start(out=mx[:, :, 0:n], in_=mixT_s[s, :, t0:t0 + n].rearrange("(c p) t -> p c t", p=128)), writes=[R_mx], dma=d_mx)
                    for j, (r0, nr) in enumerate(tiles_of(t0, n)):
                        ht, R_ht, d_ht = h_r.next()
                        ot, R_ot, d_ot = o_r.next()
                        k.op("sp", lambda e, ht=ht, nr=nr, r0=r0: e.dma_start(out=ht[0:nr, :], in_=h_src[s, t0 + r0:t0 + r0 + nr, :]), writes=[R_ht], dma=d_ht)
                        for half in range(2):
                            bk = OBK[obi[0] % 4]
                            obi[0] += 1
                            for c in range(8):
                                k.op("pe", lambda e, bk=bk, c=c, half=half, mx=mx, r0=r0, nr=nr: e.matmul(ps[bk][0:nr, :], lhsT=mx[:, c, r0:r0 + nr], rhs=Wout[:, c, half * 512:(half + 1) * 512],
                                                                                                     start=(c == 0), stop=(c == 7)), reads=[R_mx, R_w], writes=[R_ps[bk]])
                            k.op("dve", lambda e, bk=bk, half=half, ht=ht, ot=ot, nr=nr: e.tensor_tensor(out=ot[0:nr, half * 512:(half + 1) * 512], in0=ps[bk][0:nr, :], in1=ht[0:nr, half * 512:(half + 1) * 512], op=ALU.add),
                                 reads=[R_ps[bk], R_ht], writes=[R_ot])
                        if not last:
                            k.op("sp", lambda e, ot=ot, nr=nr, r0=r0: e.dma_start(out=hbuf[s, t0 + r0:t0 + r0 + nr, :], in_=ot[0:nr, :]), reads=[R_ot], dma=d_ot)
                        else:
                            ss, R_ss, _ = ss_r.next()
                            k.op("act", lambda e, ot=ot, nr=nr, ss=ss: e.activation(out=junk[0:nr, :], in_=ot[0:nr, :], func=AF.Square, scale=float(DM ** -0.5), accum_out=ss[0:nr, 0:1]), reads=[R_ot], writes=[R_ss])
                            k.op("act", lambda e, nr=nr, ss=ss: e.activation(out=ss[0:nr, 0:1], in_=ss[0:nr, 0:1], func=AF.Sqrt, bias=eps_t[0:nr, 0:1]), reads=[R_ss], writes=[R_ss])
                            k.op("dve", lambda e, nr=nr, ss=ss: e.reciprocal(out=ss[0:nr, 0:1], in_=ss[0:nr, 0:1]), reads=[R_ss], writes=[R_ss])
                            k.op("dve", lambda e, ot=ot, nr=nr, ss=ss: e.scalar_tensor_tensor(out=ot[0:nr, :], in0=ot[0:nr, :], scalar=ss[0:nr, 0:1], in1=fnw[0:nr, :], op0=ALU.mult, op1=ALU.mult),
                                 reads=[R_ot, R_ss, R_fn], writes=[R_ot])
                            g0 = t0 + r0
                            lo = max(g0, NMETA)
                            hi = g0 + nr
                            if hi > lo:
                                k.op("sp", lambda e, ot=ot, lo=lo, hi=hi, g0=g0: e.dma_start(out=out_d[s, lo - NMETA:hi - NMETA, :], in_=ot[lo - g0:hi - g0, :]), reads=[R_ot], dma=d_ot)
            k.end_phase()

        for li, l in enumerate(layers):
            h_src = h0 if (li == 0 and first_from_input) else hbuf
            if "P" in phases:
                phase_P(l)
            for s in range(NS):
                if "A" in phases:
                    phase_A(l, s, h_src)
            for s in range(NS):
                if "B" in phases:
                    phase_B(l, s)
            if "C" in phases:
                phase_C(l)
            last = final_norm and (li == len(layers) - 1)
            for s in range(NS):
                if "D" in phases:
                    phase_D(l, s, h_src, last)
        k.emit()
        build_program.nops = k.nops
    return nc


_PROG = {}


def get_prog(key, **kw):
    if key not in _PROG:
        _PROG[key] = build_program(**kw)
    return _PROG[key]


def make_in_maps(inputs, h0_full):
    c = host_consts()
    shared = {
        "norm_w": inputs["norm_w"], "w_in": inputs["w_in"], "fourier_w": inputs["fourier_w"], "pool_w": inputs["pool_w"],
        "pool_scale": inputs["pool_scale"], "q_norm_w": inputs["q_norm_w"], "w_uq": inputs["w_uq"], "kv_norm_w": inputs["kv_norm_w"],
        "w_ukv": inputs["w_ukv"], "w_out": inputs["w_out"], "final_norm_w": inputs["final_norm_w"],
    }
    shared = {k_: np.ascontiguousarray(np.asarray(v, dtype=np.float32)) for k_, v in shared.items()}
    shared.update(c)
    maps = []
    for i in range(NCORES):
        m = dict(shared)
        m["h0"] = h0_full[i * NS:(i + 1) * NS]
        maps.append(m)
    return maps


def kernel(x, meta_tokens, norm_w, w_in, fourier_w, pool_w, pool_scale, q_norm_w, w_uq, kv_norm_w, w_ukv, w_out, final_norm_w):
    inputs = dict(norm_w=norm_w, w_in=w_in, fourier_w=fourier_w, pool_w=pool_w, pool_scale=pool_scale, q_norm_w=q_norm_w,
                  w_uq=w_uq, kv_norm_w=kv_norm_w, w_ukv=w_ukv, w_out=w_out, final_norm_w=final_norm_w)
    x = np.asarray(x, dtype=np.float32)
    B = x.shape[0]
    meta = np.asarray(meta_tokens, dtype=np.float32)
    h0 = np.concatenate([np.broadcast_to(meta[None], (B, NMETA, DM)), x], axis=1)
    h0 = np.ascontiguousarray(h0)
    nc = get_prog("full", layers=list(range(DEPTH)))
    maps = make_in_maps(inputs, h0)
    res = run_bass_kernel_spmd(nc, maps, core_ids=list(range(NCORES)))
    out = np.concatenate([np.asarray(r["out"]) for r in res.results], axis=0)
    return out.astype(np.float32)
```

```python
import contextlib
import numpy as np
import ml_dtypes
import concourse.bass as bass
import concourse.mybir as mybir
from concourse.bass_utils import run_bass_kernel_spmd

F32 = mybir.dt.float32
BF16 = mybir.dt.bfloat16
AF = mybir.ActivationFunctionType
ALU = mybir.AluOpType

NCORES = 8
NS = 2
DM = 1024
SEQ = 4096
NMETA = 16
L = SEQ + NMETA
DEPTH = 4
DIN = 2240
EPS = 1e-6
NT = 33
TM_COLS = 1152
FM_COLS = 1408
SCALE = 192 ** -0.5

STILES = [(i * 512, 512) for i in range(8)] + [(4096, 16)]


def tiles_of(t0, n):
    if n == 512:
        return [(j * 128, 128) for j in range(4)]
    return [(0, n)]


class SemObj:
    def __init__(self, handle, name):
        self.h = handle
        self.name = name
        self.count = 0


class Res:
    __slots__ = ("name", "w", "r")

    def __init__(self, name=""):
        self.name = name
        self.w = None
        self.r = {}


class _Rec:
    def __init__(self):
        self.call = None

    def __getattr__(self, name):
        def f(*a, **kw):
            self.call = (name, a, kw)
            return None
        return f


class K:
    ENGS = ("pe", "act", "dve", "pool", "sp")

    def __init__(self, nc, stack):
        self.nc = nc
        self.stack = stack
        self.ops = {e: [] for e in self.ENGS}
        self.esem = {e: SemObj(stack.enter_context(nc.semaphore("s_" + e)), "s_" + e) for e in self.ENGS}
        self.waited = {e: {} for e in self.ENGS}
        self.dsems = []
        self.free_dsems = []
        self.phase_dsems = []
        self.nops = 0

    def dsem(self, name):
        if self.free_dsems:
            s = self.free_dsems.pop()
        else:
            s = SemObj(self.stack.enter_context(self.nc.semaphore(f"dsem{len(self.dsems)}")), f"dsem{len(self.dsems)}")
            self.dsems.append(s)
        self.phase_dsems.append(s)
        return s

    def op(self, eng, fn, reads=(), writes=(), dma=None):
        rec = _Rec()
        fn(rec)
        call = rec.call
        assert call is not None
        deps = []
        for r in reads:
            if r.w is not None:
                deps.append(r.w)
        for w in writes:
            if w.w is not None:
                deps.append(w.w)
            deps.extend(w.r.values())
        my = self.esem[eng]
        waits = {}
        wd = self.waited[eng]
        for (s, v) in deps:
            if s is my and eng == "pe":
                continue
            if wd.get(s.name, 0) >= v:
                continue
            if waits.get(s.name, (None, 0))[1] < v:
                waits[s.name] = (s, v)
        for (s, v) in waits.values():
            wd[s.name] = v
        if dma is not None:
            dma.count += 16
            mark = (dma, dma.count)
            inc = (dma, 16)
        else:
            my.count += 1
            mark = (my, my.count)
            inc = (my, 1)
        for r in reads:
            r.r[mark[0].name] = mark
        for w in writes:
            w.w = mark
            w.r = {}
        self.ops[eng].append((list(waits.values()), call, inc))
        self.nops += 1
        return mark

    def barrier(self):
        allsems = list(self.esem.values()) + self.dsems
        for eng in self.ENGS:
            waits = []
            wd = self.waited[eng]
            for s in allsems:
                if s is self.esem[eng]:
                    continue
                if s.count > wd.get(s.name, 0):
                    waits.append((s, s.count))
                    wd[s.name] = s.count
            if waits:
                self.ops[eng].append((waits, None, None))

    def end_phase(self):
        self.barrier()
        self.free_dsems.extend(self.phase_dsems)
        self.phase_dsems = []

    def emit(self):
        nc = self.nc
        with nc.Block() as block:
            def run(engname):
                def body(e):
                    for (waits, call, inc) in self.ops[engname]:
                        for (s, v) in waits:
                            e.wait_ge(s.h, v)
                        if call is not None:
                            name, a, kw = call
                            ins = getattr(e, name)(*a, **kw)
                            ins.then_inc(inc[0].h, inc[1])
                return body
            block.tensor(run("pe"))
            block.scalar(run("act"))
            block.vector(run("dve"))
            block.gpsimd(run("pool"))
            block.sync(run("sp"))


_UID = [0]


class Ring:
    def __init__(self, k, nc, st, name, shape, dtype, n, with_dsem=True):
        self.items = []
        for i in range(n):
            _UID[0] += 1
            t = st.enter_context(nc.sbuf_tensor(f"rg_{name}{i}_{_UID[0]}", shape, dtype))
            self.items.append((t, Res(f"{name}{i}"), k.dsem(f"d_{name}{i}") if with_dsem else None))
        self.i = 0

    def next(self):
        it = self.items[self.i % len(self.items)]
        self.i += 1
        return it


_CONSTS = None


def host_consts():
    global _CONSTS
    if _CONSTS is not None:
        return _CONSTS
    bf = ml_dtypes.bfloat16
    c = {}
    c["ident"] = np.eye(128, dtype=np.float32).astype(bf)
    idx = np.arange(L, dtype=np.int64)
    lk = (idx[:, None] * idx[None, :]) % L
    ang = lk.astype(np.float64) * (2.0 * np.pi / L)
    c["dftC"] = (np.cos(ang) / np.sqrt(L)).astype(np.float32).astype(bf)
    c["dftS"] = (np.sin(ang) / np.sqrt(L)).astype(np.float32).astype(bf)
    del lk, ang
    i64 = np.arange(64, dtype=np.int64)
    a64 = ((i64[:, None] * i64[None, :]) % 64).astype(np.float64) * (2.0 * np.pi / 64)
    c64 = np.cos(a64) / 8.0
    s64 = -np.sin(a64) / 8.0
    cb = np.zeros((128, 128), np.float32)
    sb = np.zeros((128, 128), np.float32)
    for b in range(2):
        cb[b * 64:(b + 1) * 64, b * 64:(b + 1) * 64] = c64
        sb[b * 64:(b + 1) * 64, b * 64:(b + 1) * 64] = s64
    c["c64bd"] = cb.astype(bf)
    c["s64bd"] = sb.astype(bf)
    inv = (1.0 / (np.float32(10000.0) ** (np.arange(0, 64, 2, dtype=np.float32) / np.float32(64)))).astype(np.float32)
    angr = (np.arange(L, dtype=np.float32)[:, None] * inv[None, :]).astype(np.float32)
    cosr = np.cos(angr).astype(np.float32).T
    sinr = np.sin(angr).astype(np.float32).T
    c["cos4"] = np.ascontiguousarray(np.tile(cosr, (4, 1)))
    c["sin4"] = np.ascontiguousarray(np.tile(sinr, (4, 1)))
    wins = (2, 4, 8, 16)
    invw = np.zeros((128, 2), np.float32)
    elo = np.zeros((128, 2, 8), np.float32)
    ehi = np.zeros((128, 2, 8), np.float32)
    for a in range(2):
        for p in range(128):
            w = wins[2 * a + p // 64]
            invw[p, a] = 1.0 / w
            for j in range(8):
                i = j
                cnt = min(i + w // 2, L) - max(i - w // 2, 0)
                elo[p, a, j] = 1.0 / cnt
                i = L - 8 + j
                cnt = min(i + w // 2, L) - max(i - w // 2, 0)
                ehi[p, a, j] = 1.0 / cnt
    c["pinvw"] = invw
    c["pelo"] = elo
    c["pehi"] = ehi
    _CONSTS = c
    return c


def build_program(layers, first_from_input=True, final_norm=True, debug=False, phases="PABCD"):
    nc = bass.Bass("TRN2", target_bir_lowering=False)
    dk = "ExternalOutput" if debug else "Internal"

    def din(name, shape, dt=F32):
        return nc.dram_tensor(name, list(shape), dt, kind="ExternalInput").ap()

    h0 = din("h0", [NS, L, DM])
    norm_w = din("norm_w", [DEPTH, DM])
    w_in = din("w_in", [DEPTH, DM, DIN])
    fourier_w = din("fourier_w", [DEPTH, 4, 64, 64])
    pool_w = din("pool_w", [DEPTH, 4, 64, 64])
    pool_scale = din("pool_scale", [DEPTH, 256])
    q_norm_w = din("q_norm_w", [DEPTH, 384])
    w_uq = din("w_uq", [DEPTH, 384, 768])
    kv_norm_w = din("kv_norm_w", [DEPTH, 256])
    w_ukv = din("w_ukv", [DEPTH, 256, 1024])
    w_out = din("w_out", [DEPTH, DM, DM])
    final_norm_w = din("final_norm_w", [DM])
    ident_d = din("ident", [128, 128], BF16)
    dftC = din("dftC", [L, L], BF16)
    dftS = din("dftS", [L, L], BF16)
    c64bd_d = din("c64bd", [128, 128], BF16)
    s64bd_d = din("s64bd", [128, 128], BF16)
    cos4_d = din("cos4", [128, L])
    sin4_d = din("sin4", [128, L])
    pinvw_d = din("pinvw", [128, 2])
    pelo_d = din("pelo", [128, 2, 8])
    pehi_d = din("pehi", [128, 2, 8])

    if final_norm:
        out_d = nc.dram_tensor("out", [NS, SEQ, DM], F32, kind="ExternalOutput").ap()
        hbuf = nc.dram_tensor("hbuf", [NS, L, DM], F32, kind="Internal").ap()
    else:
        out_d = None
        hbuf = nc.dram_tensor("hbuf", [NS, L, DM], F32, kind="ExternalOutput").ap()
    pcs_s = nc.dram_tensor("pcs_s", [NS, L, 512], BF16, kind=dk).ap()
    qT_s = nc.dram_tensor("qT_s", [NS, 768, L], BF16, kind=dk).ap()
    kT_s = nc.dram_tensor("kT_s", [NS, 576, L], BF16, kind=dk).ap()
    v_s = nc.dram_tensor("v_s", [NS, L, 512], BF16, kind=dk).ap()
    gTa_s = nc.dram_tensor("gTa_s", [NS, 512, L], BF16, kind=dk).ap()
    gTf_s = nc.dram_tensor("gTf_s", [NS, 256, L], BF16, kind=dk).ap()
    mixT_s = nc.dram_tensor("mixT_s", [NS, 1024, L], BF16, kind=dk).ap()

    with contextlib.ExitStack() as st:
        k = K(nc, st)

        def sb(name, shape, dt, stack=st):
            _UID[0] += 1
            return stack.enter_context(nc.sbuf_tensor(f"sb_{name}_{_UID[0]}", list(shape), dt))

        ident = sb("ident", [128, 128], BF16)
        ones_f = sb("ones_f", [128, 128], F32)
        ones_b = sb("ones_b", [128, 128], BF16)
        eps_t = sb("eps_t", [128, 1], F32)
        Win_tm = sb("Win_tm", [128, 8, TM_COLS], BF16)
        Win_fm = sb("Win_fm", [128, 8, FM_COLS], BF16)
        Wuq = sb("Wuq", [128, 3, 1024], BF16)
        Wukv = sb("Wukv", [128, 2, 1024], BF16)
        Wout = sb("Wout", [128, 8, 1024], BF16)
        Pbd = sb("Pbd", [128, 2, 128], BF16)
        psc = sb("psc", [128, 2], F32)
        c64bd = sb("c64bd", [128, 128], BF16)
        s64bd = sb("s64bd", [128, 128], BF16)
        pinvw = sb("pinvw", [128, 2], F32)
        pelo = sb("pelo", [128, 2, 8], F32)
        pehi = sb("pehi", [128, 2, 8], F32)
        R_w = Res("weights")
        R_c = Res("consts")
        psd = [st.enter_context(nc.psum_tensor(f"psd{i}", [128, 1024], F32)) for i in range(4)]
        ps = [psd[i // 2][:, (i % 2) * 512:(i % 2 + 1) * 512] for i in range(8)]
        R_ps = [Res(f"ps{i}") for i in range(8)]
        dconst = k.dsem("d_const")

        def ld_const(dst, src):
            k.op("sp", lambda e: e.dma_start(out=dst, in_=src), writes=[R_c], dma=dconst)

        ld_const(ident[:], ident_d)
        ld_const(c64bd[:], c64bd_d)
        ld_const(s64bd[:], s64bd_d)
        ld_const(pinvw[:], pinvw_d)
        ld_const(pelo[:], pelo_d)
        ld_const(pehi[:], pehi_d)
        k.op("pool", lambda e: e.memset(ones_f[:], 1.0), writes=[R_c])
        k.op("pool", lambda e: e.memset(ones_b[:], 1.0), writes=[R_c])
        k.op("pool", lambda e: e.memset(eps_t[:], EPS), writes=[R_c])
        k.end_phase()

        def phase_P(l):
            with contextlib.ExitStack() as ls:
                stg = Ring(k, nc, ls, "wst", [128, 8, 640], F32, 2)
                nw = sb("nw", [128, 8], F32, ls)
                qnw = sb("qnw", [128, 3], F32, ls)
                kvnw = sb("kvnw", [128, 2], F32, ls)
                wf_st = sb("wf_st", [128, 2, 64], F32, ls)
                pw_st = sb("pw_st", [128, 2, 64], F32, ls)
                wf_b = sb("wf_b", [128, 2, 64], BF16, ls)
                Mbd = sb("Mbd", [128, 2, 2, 128], BF16, ls)
                Wf_bf = sb("Wf_bf", [128, 8, 256], BF16, ls)
                WfT = sb("WfT", [128, 2, 1024], BF16, ls)
                R_small = Res("small")
                R_wfb = Res("wfb")
                R_mbd = Res("mbd")
                R_wfbf = Res("wfbf")
                R_wft = Res("wft")
                dsm = k.dsem(f"d_small{l}")

                def ld_small(dst, src):
                    k.op("sp", lambda e: e.dma_start(out=dst, in_=src, allow_slow_non_contiguous=True),
                         writes=[R_small], dma=dsm)

                ld_small(nw[:], norm_w[l].rearrange("(c p) -> p c", p=128))
                ld_small(qnw[:], q_norm_w[l].rearrange("(c p) -> p c", p=128))
                ld_small(kvnw[:], kv_norm_w[l].rearrange("(c p) -> p c", p=128))
                ld_small(psc[:], pool_scale[l].rearrange("(a p) -> p a", p=128))
                ld_small(wf_st[:], fourier_w[l].rearrange("(a hh) c d -> (hh c) a d", hh=2))
                ld_small(pw_st[:], pool_w[l].rearrange("(a hh) c d -> (hh c) a d", hh=2))

                engs = ["act", "dve"]
                ei = [0]

                def scaled_cast(dst, src, sc_ap, mul2=None, eng=None):
                    if eng is None:
                        eng = engs[ei[0] % 2]
                        ei[0] += 1
                    if eng == "act":
                        if mul2 is None:
                            fn = lambda e: e.activation(out=dst, in_=src, func=AF.Copy, scale=sc_ap)
                        else:
                            eng = "dve"
                    if eng != "act":
                        if mul2 is None:
                            fn = lambda e: e.tensor_scalar(out=dst, in0=src, scalar1=sc_ap, scalar2=None, op0=ALU.mult)
                        else:
                            fn = lambda e: e.tensor_scalar(out=dst, in0=src, scalar1=sc_ap, scalar2=float(mul2),
                                                           op0=ALU.mult, op1=ALU.mult)
                    return eng, fn

                Wv = w_in[l].rearrange("(c p) n -> p c n", p=128)
                pieces = [(0, 512), (512, 1024), (1024, 1664), (1664, 2240)]
                for (c0, c1) in pieces:
                    t, R_t, ds = stg.next()
                    wcols = c1 - c0
                    k.op("sp", lambda e, t=t, c0=c0, c1=c1, wcols=wcols: e.dma_start(out=t[:, :, 0:wcols], in_=Wv[:, :, c0:c1]),
                         writes=[R_t], dma=ds)
                    jobs = []
                    if c0 == 0:
                        jobs.append((Wf_bf, 0, 0, 256, None, R_wfbf))
                        jobs.append((Win_fm, 0, 256, 256, None, R_w))
                    elif c0 == 512:
                        jobs.append((Win_fm, 1024, 512, 256, None, R_w))
                        jobs.append((Win_fm, 256, 768, 256, None, R_w))
                    elif c0 == 1024:
                        jobs.append((Win_tm, 512, 1024, 384, None, R_w))
                        jobs.append((Win_tm, 896, 1408, 256, None, R_w))
                    else:
                        jobs.append((Win_fm, 1280, 1664, 64, None, R_w))
                        jobs.append((Win_fm, 1344, 1696, 32, -1.0, R_w))
                        jobs.append((Win_fm, 1376, 1664, 32, None, R_w))
                        jobs.append((Win_fm, 512, 1728, 512, None, R_w))
                    for (dt_, d0, s0, ncol, mul2, rr) in jobs:
                        for c in range(8):
                            eng, fn = scaled_cast(dt_[:, c, d0:d0 + ncol], t[:, c, s0 - c0:s0 - c0 + ncol], nw[:, c:c + 1], mul2)
                            k.op(eng, fn, reads=[R_t, R_small], writes=[rr])

                t, R_t, ds = stg.next()
                tq = t[:].rearrange("p c n -> p (c n)")[:, 0:3 * 768].rearrange("p (c n) -> p c n", c=3)
                k.op("sp", lambda e: e.dma_start(out=tq, in_=w_uq[l].rearrange("(c p) n -> p c n", p=128)),
                     writes=[R_t], dma=ds)
                for c in range(3):
                    srcv = tq[:, c, :].rearrange("p (h e) -> p h e", h=4)
                    sc = qnw[:, c:c + 1]
                    eng, fn = scaled_cast(Wuq[:, c, 0:512].rearrange("p (h e) -> p h e", h=4), srcv[:, :, 0:128], sc, SCALE, "dve")
                    k.op(eng, fn, reads=[R_t, R_small], writes=[R_w])
                    eng, fn = scaled_cast(Wuq[:, c, 512:768].rearrange("p (h e) -> p h e", h=4), srcv[:, :, 128:192], sc, SCALE, "dve")
                    k.op(eng, fn, reads=[R_t, R_small], writes=[R_w])
                    rot = Wuq[:, c, 768:1024].rearrange("p (h e) -> p h e", h=4)
                    eng, fn = scaled_cast(rot[:, :, 0:32], srcv[:, :, 160:192], sc, -SCALE, "dve")
                    k.op(eng, fn, reads=[R_t, R_small], writes=[R_w])
                    eng, fn = scaled_cast(rot[:, :, 32:64], srcv[:, :, 128:160], sc, SCALE, "dve")
                    k.op(eng, fn, reads=[R_t, R_small], writes=[R_w])
                t, R_t, ds = stg.next()
                tkv = t[:].rearrange("p c n -> p (c n)")[:, 0:2 * 1024].rearrange("p (c n) -> p c n", c=2)
                k.op("sp", lambda e: e.dma_start(out=tkv, in_=w_ukv[l].rearrange("(c p) n -> p c n", p=128)),
                     writes=[R_t], dma=ds)
                for c in range(2):
                    srcv = tkv[:, c, :].rearrange("p (h e) -> p h e", h=4)
                    sc = kvnw[:, c:c + 1]
                    eng, fn = scaled_cast(Wukv[:, c, 0:512].rearrange("p (h e) -> p h e", h=4), srcv[:, :, 0:128], sc, None, "act")
                    k.op(eng, fn, reads=[R_t, R_small], writes=[R_w])
                    eng, fn = scaled_cast(Wukv[:, c, 512:1024].rearrange("p (h e) -> p h e", h=4), srcv[:, :, 128:256], sc, None, "dve")
                    k.op(eng, fn, reads=[R_t, R_small], writes=[R_w])
                Wo = w_out[l].rearrange("(c p) n -> p c n", p=128)
                for half in range(2):
                    t, R_t, ds = stg.next()
                    tv = t[:].rearrange("p c n -> p (c n)")[:, 0:4096].rearrange("p (c n) -> p c n", c=4)
                    k.op("sp", lambda e, tv=tv, half=half: e.dma_start(out=tv, in_=Wo[:, half * 4:(half + 1) * 4, :]),
                         writes=[R_t], dma=ds)
                    for c in range(4):
                        eng = ["act", "dve"][c % 2]
                        if eng == "act":
                            fn = lambda e, tv=tv, c=c, half=half: e.activation(out=Wout[:, half * 4 + c, :], in_=tv[:, c, :], func=AF.Copy)
                        else:
                            fn = lambda e, tv=tv, c=c, half=half: e.tensor_copy(out=Wout[:, half * 4 + c, :], in_=tv[:, c, :])
                        k.op(eng, fn, reads=[R_t], writes=[R_w])
                k.op("pool", lambda e: e.memset(Pbd[:], 0.0), writes=[R_w])
                for a in range(2):
                    k.op("pool", lambda e, a=a: e.tensor_copy(out=Pbd[0:64, a, 0:64], in_=pw_st[0:64, a, :]), reads=[R_small], writes=[R_w])
                    k.op("pool", lambda e, a=a: e.tensor_copy(out=Pbd[64:128, a, 64:128], in_=pw_st[64:128, a, :]), reads=[R_small], writes=[R_w])
                k.op("dve", lambda e: e.tensor_copy(out=wf_b[:], in_=wf_st[:]), reads=[R_small], writes=[R_wfb])
                k.op("pool", lambda e: e.memset(Mbd[:], 0.0), writes=[R_mbd])
                for a in range(2):
                    for cs in range(2):
                        blk = (a * 2 + cs) * 64
                        lh = c64bd if cs == 0 else s64bd
                        k.op("pe", lambda e, a=a, blk=blk, lh=lh: e.matmul(ps[0][:, blk:blk + 64], lhsT=lh[:], rhs=wf_b[:, a, :], start=True, stop=True),
                             reads=[R_c, R_wfb], writes=[R_ps[0]])
                for a in range(2):
                    for cs in range(2):
                        blk = (a * 2 + cs) * 64
                        k.op("dve", lambda e, a=a, cs=cs, blk=blk: e.tensor_copy(out=Mbd[0:64, a, cs, 0:64], in_=ps[0][0:64, blk:blk + 64]),
                             reads=[R_ps[0]], writes=[R_mbd])
                        k.op("dve", lambda e, a=a, cs=cs, blk=blk: e.tensor_copy(out=Mbd[64:128, a, cs, 64:128], in_=ps[0][64:128, blk:blk + 64]),
                             reads=[R_ps[0]], writes=[R_mbd])
                for a in range(2):
                    pt = ps[1 + a][:].bitcast(BF16)
                    for c in range(8):
                        k.op("pe", lambda e, a=a, c=c, pt=pt: e.transpose(out=pt[:, c * 128:(c + 1) * 128], in_=Wf_bf[:, c, a * 128:(a + 1) * 128], identity=ident[:]),
                             reads=[R_wfbf, R_c], writes=[R_ps[1 + a]])
                    k.op("act", lambda e, a=a, pt=pt: e.activation(out=WfT[:, a, :], in_=pt, func=AF.Copy), reads=[R_ps[1 + a]], writes=[R_wft])
                for c in range(8):
                    pb = 3 + (c % 2)
                    for cs in range(2):
                        for a in range(2):
                            col = cs * 256 + a * 128
                            k.op("pe", lambda e, c=c, cs=cs, a=a, col=col, pb=pb: e.matmul(ps[pb][:, col:col + 128], lhsT=WfT[:, a, c * 128:(c + 1) * 128],
                                                                                         rhs=Mbd[:, a, cs, :], start=True, stop=True),
                                 reads=[R_wft, R_mbd], writes=[R_ps[pb]])
                    k.op("dve", lambda e, c=c, pb=pb: e.tensor_copy(out=Win_tm[:, c, 0:512], in_=ps[pb][:, :]), reads=[R_ps[pb]], writes=[R_w])
            k.end_phase()

        def phase_A(l, s, h_src):
            with contextlib.ExitStack() as ls:
                Xp = sb("Xp", [128, 2, L + 16], BF16, ls)
                spg = sb("spg", [128, 2, L], BF16, ls)
                ls2 = ls.enter_context(contextlib.ExitStack())
                hring = Ring(k, nc, ls2, "ht", [128, 1024], F32, 4)
                xn_ring = Ring(k, nc, ls2, "xn", [128, 1024], BF16, 4, with_dsem=False)
                cn_ring = Ring(k, nc, ls2, "cn", [128, 640], BF16, 4, with_dsem=False)
                xnT_ring = [(sb(f"xnT{i}", [128, 8, 512], BF16, ls2), [Res() for _ in range(4)]) for i in range(2)]
                cnT_ring = [(sb(f"cnT{i}", [128, 5, 512], BF16, ls2), [Res() for _ in range(4)]) for i in range(2)]
                pcs_st = Ring(k, nc, ls2, "pcs_st", [128, 4, 512], BF16, 1)
                v_st = Ring(k, nc, ls2, "v_st", [128, 4, 512], BF16, 1)
                gf_st = Ring(k, nc, ls2, "gf_st", [128, 2, 512], BF16, 1)
                ga_st = Ring(k, nc, ls2, "ga_st", [128, 4, 512], BF16, 1)
                q_st = Ring(k, nc, ls2, "q_st", [128, 6, 512], BF16, 1)
                k_st = Ring(k, nc, ls2, "k_st", [128, 4, 512], BF16, 1)
                kr_st = Ring(k, nc, ls2, "kr_st", [64, 512], BF16, 1)
                cos_r = Ring(k, nc, ls2, "cos_r", [128, 512], F32, 2)
                sin_r = Ring(k, nc, ls2, "sin_r", [128, 512], F32, 2)
                t1_r = Ring(k, nc, ls2, "t1_r", [128, 512], F32, 2, with_dsem=False)
                t2_r = Ring(k, nc, ls2, "t2_r", [128, 512], F32, 2, with_dsem=False)
                junk_a = sb("junk_a", [128, 1024], BF16, ls2)
                ss_r = Ring(k, nc, ls2, "ss_r", [128, 4], F32, 8, with_dsem=False)
                R_xp = [Res("xp0"), Res("xp1")]
                R_spg = [Res("spg0"), Res("spg1")]
                PT = [0, 1]
                TMB = [2, 3, 4]
                VB = 5
                FMB = [2, 3, 6, 7]
                fmi = [0]
                pti = [0]

                k.op("pool", lambda e: e.memset(Xp[:, :, 0:8], 0.0), writes=R_xp)
                k.op("pool", lambda e: e.memset(Xp[:, :, L + 8:L + 16], 0.0), writes=R_xp)

                TMSETS = [[2, 3, 4], [5, 6, 7]]
                tmi = [0]
                st_state = {}

                def S1a(u):
                    t0, n = STILES[u]
                    xnT, R_xnT = xnT_ring[u % 2]
                    stt = {"tiles": []}
                    st_state[u] = stt
                    for j, (r0, nr) in enumerate(tiles_of(t0, n)):
                        ht, R_ht, d_ht = hring.next()
                        k.op("sp", lambda e: e.dma_start(out=ht[0:nr, :], in_=h_src[s, t0 + r0:t0 + r0 + nr, :]), writes=[R_ht], dma=d_ht)
                        ss, R_ss, _ = ss_r.next()
                        k.op("act", lambda e: e.activation(out=junk_a[0:nr, :], in_=ht[0:nr, :], func=AF.Square, scale=float(DM ** -0.5), accum_out=ss[0:nr, 0:1]),
                             reads=[R_ht], writes=[R_ss])
                        k.op("act", lambda e: e.activation(out=ss[0:nr, 0:1], in_=ss[0:nr, 0:1], func=AF.Sqrt, bias=eps_t[0:nr, 0:1]),
                             reads=[R_ss], writes=[R_ss])
                        k.op("dve", lambda e: e.reciprocal(out=ss[0:nr, 0:1], in_=ss[0:nr, 0:1]), reads=[R_ss], writes=[R_ss])
                        xn, R_xn, _ = xn_ring.next()
                        k.op("dve", lambda e: e.tensor_scalar(out=xn[0:nr, :], in0=ht[0:nr, :], scalar1=ss[0:nr, 0:1], scalar2=None, op0=ALU.mult),
                             reads=[R_ht, R_ss], writes=[R_xn])
                        stt["tiles"].append({"ss": ss, "R_ss": R_ss, "xn": (xn, R_xn)})

                def S1b(u):
                    t0, n = STILES[u]
                    xnT, R_xnT = xnT_ring[u % 2]
                    stt = st_state[u]
                    for j, (r0, nr) in enumerate(tiles_of(t0, n)):
                        xn, R_xn = stt["tiles"][j]["xn"]
                        pb = PT[pti[0] % 2]
                        pti[0] += 1
                        ptv = ps[pb][:].bitcast(BF16)
                        for c in range(8):
                            k.op("pe", lambda e: e.transpose(out=ptv[:, c * 128:c * 128 + nr], in_=xn[0:nr, c * 128:(c + 1) * 128], identity=ident[0:nr, 0:nr]),
                                 reads=[R_xn, R_c], writes=[R_ps[pb]])
                        k.op("act", lambda e: e.activation(out=xnT[:, :, r0:r0 + nr], in_=ptv.rearrange("p (c t) -> p c t", c=8)[:, :, 0:nr], func=AF.Copy),
                             reads=[R_ps[pb]], writes=[R_xnT[j]])

                def S2(u):
                    t0, n = STILES[u]
                    xnT, R_xnT = xnT_ring[u % 2]
                    stt = st_state[u]
                    pcs_t, R_pcs, d_pcs = pcs_st.next()
                    stt["pcs"] = (pcs_t, R_pcs, d_pcs)
                    for j, (r0, nr) in enumerate(tiles_of(t0, n)):
                        ss, R_ss = stt["tiles"][j]["ss"], stt["tiles"][j]["R_ss"]
                        TMB = TMSETS[tmi[0] % 2]
                        tmi[0] += 1
                        tm_specs = [(TMB[0], 0, 512), (TMB[1], 512, 384), (TMB[2], 896, 256)]
                        for (bk, c0, ncol) in tm_specs:
                            for c in range(8):
                                k.op("pe", lambda e: e.matmul(ps[bk][0:nr, 0:ncol], lhsT=xnT[:, c, r0:r0 + nr], rhs=Win_tm[:, c, c0:c0 + ncol], start=(c == 0), stop=(c == 7)),
                                     reads=[R_xnT[j], R_w], writes=[R_ps[bk]])
                        k.op("act", lambda e: e.activation(out=pcs_t[0:nr, j, :], in_=ps[TMB[0]][0:nr, :], func=AF.Copy),
                             reads=[R_ps[TMB[0]]], writes=[R_pcs])
                        k.op("act", lambda e: e.activation(out=junk_a[0:nr, 0:384], in_=ps[TMB[1]][0:nr, 0:384], func=AF.Square, scale=float(384 ** -0.5), accum_out=ss[0:nr, 1:2]),
                             reads=[R_ps[TMB[1]]], writes=[R_ss])
                        k.op("act", lambda e: e.activation(out=junk_a[0:nr, 0:256], in_=ps[TMB[2]][0:nr, 0:256], func=AF.Square, scale=float(256 ** -0.5), accum_out=ss[0:nr, 2:3]),
                             reads=[R_ps[TMB[2]]], writes=[R_ss])
                        k.op("act", lambda e: e.activation(out=ss[0:nr, 1:3], in_=ss[0:nr, 1:3], func=AF.Sqrt, bias=eps_t[0:nr, 0:1]),
                             reads=[R_ss], writes=[R_ss])
                        k.op("dve", lambda e: e.reciprocal(out=ss[0:nr, 1:3], in_=ss[0:nr, 1:3]), reads=[R_ss], writes=[R_ss])
                        cn, R_cn, _ = cn_ring.next()
                        k.op("dve", lambda e: e.tensor_scalar(out=cn[0:nr, 0:384], in0=ps[TMB[1]][0:nr, 0:384], scalar1=ss[0:nr, 1:2], scalar2=None, op0=ALU.mult),
                             reads=[R_ps[TMB[1]], R_ss], writes=[R_cn])
                        k.op("dve", lambda e: e.tensor_scalar(out=cn[0:nr, 384:640], in0=ps[TMB[2]][0:nr, 0:256], scalar1=ss[0:nr, 2:3], scalar2=None, op0=ALU.mult),
                             reads=[R_ps[TMB[2]], R_ss], writes=[R_cn])
                        stt["tiles"][j]["cn"] = (cn, R_cn)
                    if n == 512:
                        k.op("sp", lambda e: e.dma_start(out=pcs_s[s, t0:t0 + 512, :].rearrange("(j p) n -> p j n", p=128), in_=pcs_t[:]), reads=[R_pcs], dma=d_pcs)
                    else:
                        k.op("sp", lambda e: e.dma_start(out=pcs_s[s, t0:t0 + n, :], in_=pcs_t[0:n, 0, :]), reads=[R_pcs], dma=d_pcs)

                def S3t(u):
                    t0, n = STILES[u]
                    cnT, R_cnT = cnT_ring[u % 2]
                    stt = st_state[u]
                    v_t, R_v, d_v = v_st.next()
                    for j, (r0, nr) in enumerate(tiles_of(t0, n)):
                        cn, R_cn = stt["tiles"][j]["cn"]
                        pb = PT[pti[0] % 2]
                        pti[0] += 1
                        ptv = ps[pb][:].bitcast(BF16)
                        for c in range(5):
                            k.op("pe", lambda e: e.transpose(out=ptv[:, c * 128:c * 128 + nr], in_=cn[0:nr, c * 128:(c + 1) * 128], identity=ident[0:nr, 0:nr]),
                                 reads=[R_cn, R_c], writes=[R_ps[pb]])
                        k.op("dve", lambda e: e.tensor_copy(out=cnT[:, :, r0:r0 + nr], in_=ptv[:, 0:640].rearrange("p (c t) -> p c t", c=5)[:, :, 0:nr]),
                             reads=[R_ps[pb]], writes=[R_cnT[j]])
                    stt["v"] = (v_t, R_v, d_v)

                def S3v(u):
                    t0, n = STILES[u]
                    cnT, R_cnT = cnT_ring[u % 2]
                    stt = st_state[u]
                    v_t, R_v, d_v = stt["v"]
                    for j, (r0, nr) in enumerate(tiles_of(t0, n)):
                        vb = [5, 4][j % 2]
                        for c in range(2):
                            k.op("pe", lambda e: e.matmul(ps[vb][0:nr, :], lhsT=cnT[:, 3 + c, r0:r0 + nr], rhs=Wukv[:, c, 512:1024], start=(c == 0), stop=(c == 1)),
                                 reads=[R_cnT[j], R_w], writes=[R_ps[vb]])
                        k.op("dve", lambda e: e.tensor_copy(out=v_t[0:nr, j, :], in_=ps[vb][0:nr, :]), reads=[R_ps[vb]], writes=[R_v])
                    if n == 512:
                        k.op("sp", lambda e: e.dma_start(out=v_s[s, t0:t0 + 512, :].rearrange("(j p) n -> p j n", p=128), in_=v_t[:]), reads=[R_v], dma=d_v)
                    else:
                        k.op("sp", lambda e: e.dma_start(out=v_s[s, t0:t0 + n, :], in_=v_t[0:n, 0, :]), reads=[R_v], dma=d_v)

                def load_tables(u):
                    t0, n = STILES[u]
                    cos_t, R_cos, d_cos = cos_r.next()
                    sin_t, R_sin, d_sin = sin_r.next()
                    k.op("sp", lambda e: e.dma_start(out=cos_t[:, 0:n], in_=cos4_d[:, t0:t0 + n]), writes=[R_cos], dma=d_cos)
                    k.op("sp", lambda e: e.dma_start(out=sin_t[:, 0:n], in_=sin4_d[:, t0:t0 + n]), writes=[R_sin], dma=d_sin)
                    st_state[u]["tab"] = (cos_t, R_cos, sin_t, R_sin)

                def FM(u):
                    t0, n = STILES[u]
                    xnT, R_xnT = xnT_ring[u % 2]
                    ntl = len(tiles_of(t0, n))
                    cos_t, R_cos, sin_t, R_sin = st_state[u]["tab"]

                    def fm_group(col0, m):
                        bk = FMB[fmi[0] % len(FMB)]
                        fmi[0] += 1
                        for c in range(8):
                            k.op("pe", lambda e: e.matmul(ps[bk][0:m, 0:n], lhsT=Win_fm[:, c, col0:col0 + m], rhs=xnT[:, c, 0:n], start=(c == 0), stop=(c == 7)),
                                 reads=R_xnT[0:ntl] + [R_w], writes=[R_ps[bk]])
                        return bk

                    gf_t, R_gf, d_gf = gf_st.next()
                    ga_t, R_ga, d_ga = ga_st.next()
                    for oc in range(2):
                        bk = fm_group(oc * 128, 128)
                        k.op("act", lambda e: e.activation(out=gf_t[:, oc, 0:n], in_=ps[bk][:, 0:n], func=AF.Silu), reads=[R_ps[bk]], writes=[R_gf])
                    k.op("sp", lambda e: e.dma_start(out=gTf_s[s, :, t0:t0 + n].rearrange("(c p) t -> p c t", p=128), in_=gf_t[:, :, 0:n]), reads=[R_gf], dma=d_gf)
                    for oc in range(2):
                        bk = fm_group(256 + oc * 128, 128)
                        k.op("act", lambda e: e.activation(out=spg[:, oc, t0:t0 + n], in_=ps[bk][:, 0:n], func=AF.Silu), reads=[R_ps[bk]], writes=[R_spg[oc]])
                    for oc in range(4):
                        bk = fm_group(512 + oc * 128, 128)
                        k.op("act", lambda e: e.activation(out=ga_t[:, oc, 0:n], in_=ps[bk][:, 0:n], func=AF.Silu), reads=[R_ps[bk]], writes=[R_ga])
                    k.op("sp", lambda e: e.dma_start(out=gTa_s[s, :, t0:t0 + n].rearrange("(c p) t -> p c t", p=128), in_=ga_t[:, :, 0:n]), reads=[R_ga], dma=d_ga)
                    for oc in range(2):
                        bk = fm_group(1024 + oc * 128, 128)
                        k.op("dve", lambda e: e.tensor_copy(out=Xp[:, oc, 8 + t0:8 + t0 + n], in_=ps[bk][:, 0:n]), reads=[R_ps[bk]], writes=[R_xp[oc]])
                    bkA = fm_group(1280, 64)
                    bkB = fm_group(1344, 64)
                    t1, R_t1, _ = t1_r.next()
                    t2, R_t2, _ = t2_r.next()
                    kr_t, R_kr, d_kr = kr_st.next()
                    k.op("dve", lambda e: e.tensor_tensor(out=t1[0:64, 0:n], in0=ps[bkA][0:64, 0:n], in1=cos_t[0:64, 0:n], op=ALU.mult),
                         reads=[R_ps[bkA], R_cos], writes=[R_t1])
                    k.op("dve", lambda e: e.tensor_tensor(out=t2[0:64, 0:n], in0=ps[bkB][0:64, 0:n], in1=sin_t[0:64, 0:n], op=ALU.mult),
                         reads=[R_ps[bkB], R_sin], writes=[R_t2])
                    k.op("pool", lambda e: e.tensor_tensor(out=kr_t[:, 0:n], in0=t1[0:64, 0:n], in1=t2[0:64, 0:n], op=ALU.add),
                         reads=[R_t1, R_t2], writes=[R_kr])
                    k.op("sp", lambda e: e.dma_start(out=kT_s[s, 512:576, t0:t0 + n], in_=kr_t[:, 0:n]), reads=[R_kr], dma=d_kr)

                def QKG(u):
                    t0, n = STILES[u]
                    cnT, R_cnT = cnT_ring[u % 2]
                    ntl = len(tiles_of(t0, n))
                    cos_t, R_cos, sin_t, R_sin = st_state[u]["tab"]

                    def cn_group(Wt, nk, kc0, col0):
                        bk = FMB[fmi[0] % len(FMB)]
                        fmi[0] += 1
                        for c in range(nk):
                            k.op("pe", lambda e: e.matmul(ps[bk][:, 0:n], lhsT=Wt[:, c, col0:col0 + 128], rhs=cnT[:, kc0 + c, 0:n], start=(c == 0), stop=(c == nk - 1)),
                                 reads=R_cnT[0:ntl] + [R_w], writes=[R_ps[bk]])
                        return bk

                    q_t, R_q, d_q = q_st.next()
                    for oc in range(4):
                        bk = cn_group(Wuq, 3, 0, oc * 128)
                        if oc % 2 == 0:
                            k.op("act", lambda e: e.activation(out=q_t[:, oc, 0:n], in_=ps[bk][:, 0:n], func=AF.Copy), reads=[R_ps[bk]], writes=[R_q])
                        else:
                            k.op("dve", lambda e: e.tensor_copy(out=q_t[:, oc, 0:n], in_=ps[bk][:, 0:n]), reads=[R_ps[bk]], writes=[R_q])
                    for a in range(2):
                        bkA = cn_group(Wuq, 3, 0, 512 + a * 128)
                        bkB = cn_group(Wuq, 3, 0, 768 + a * 128)
                        t1, R_t1, _ = t1_r.next()
                        t2, R_t2, _ = t2_r.next()
                        k.op("dve", lambda e: e.tensor_tensor(out=t1[:, 0:n], in0=ps[bkA][:, 0:n], in1=cos_t[:, 0:n], op=ALU.mult),
                             reads=[R_ps[bkA], R_cos], writes=[R_t1])
                        k.op("dve", lambda e: e.tensor_tensor(out=t2[:, 0:n], in0=ps[bkB][:, 0:n], in1=sin_t[:, 0:n], op=ALU.mult),
                             reads=[R_ps[bkB], R_sin], writes=[R_t2])
                        k.op("pool", lambda e: e.tensor_tensor(out=q_t[:, 4 + a, 0:n], in0=t1[:, 0:n], in1=t2[:, 0:n], op=ALU.add),
                             reads=[R_t1, R_t2], writes=[R_q])
                    k.op("sp", lambda e: e.dma_start(out=qT_s[s, :, t0:t0 + n].rearrange("(c p) t -> p c t", p=128), in_=q_t[:, :, 0:n]), reads=[R_q], dma=d_q)
                    k_t, R_k, d_k = k_st.next()
                    for oc in range(4):
                        bk = cn_group(Wukv, 2, 3, oc * 128)
                        if oc % 2 == 0:
                            k.op("act", lambda e: e.activation(out=k_t[:, oc, 0:n], in_=ps[bk][:, 0:n], func=AF.Copy), reads=[R_ps[bk]], writes=[R_k])
                        else:
                            k.op("dve", lambda e: e.tensor_copy(out=k_t[:, oc, 0:n], in_=ps[bk][:, 0:n]), reads=[R_ps[bk]], writes=[R_k])
                    k.op("sp", lambda e: e.dma_start(out=kT_s[s, 0:512, t0:t0 + n].rearrange("(c p) t -> p c t", p=128), in_=k_t[:, :, 0:n]), reads=[R_k], dma=d_k)

                NU = len(STILES)
                S1a(0)
                S1b(0)
                for u in range(NU):
                    load_tables(u)
                    S2(u)
                    if u + 1 < NU:
                        S1a(u + 1)
                    FM(u)
                    S3t(u)
                    if u + 1 < NU:
                        S1b(u + 1)
                    S3v(u)
                    QKG(u)

                k.barrier()
                ls2.close()
                CB = 1028
                sA = Ring(k, nc, ls, "poolA", [128, CB + 16], F32, 2, with_dsem=False)
                sB = Ring(k, nc, ls, "poolB", [128, CB + 16], F32, 2, with_dsem=False)
                pm_st = Ring(k, nc, ls, "pm_st", [128, CB], BF16, 2)
                ym = Ring(k, nc, ls, "ym", [128, CB], BF16, 2, with_dsem=False)
                for a in range(2):
                    for b in range(4):
                        o = b * CB
                        A_, R_A, _ = sA.next()
                        B_, R_B, _ = sB.next()
                        X = Xp[:, a, o:o + CB + 16]
                        W_ = CB + 16
                        eng1 = "pool" if (b % 2 == 0) else "dve"
                        eng2 = "dve"
                        k.op(eng1, lambda e, A_=A_, X=X, W_=W_: e.tensor_tensor(out=A_[:, 1:W_], in0=X[:, 0:W_ - 1], in1=X[:, 1:W_], op=ALU.add), reads=[R_xp[a]], writes=[R_A])
                        k.op(eng1, lambda e, A_=A_, B_=B_, W_=W_: e.tensor_tensor(out=B_[64:128, 2:W_ - 1] if a == 0 else B_[:, 2:W_ - 1],
                                                                               in0=A_[64:128, 1:W_ - 2] if a == 0 else A_[:, 1:W_ - 2],
                                                                               in1=A_[64:128, 3:W_] if a == 0 else A_[:, 3:W_], op=ALU.add), reads=[R_A], writes=[R_B])
                        if a == 0:
                            k.op(eng1, lambda e, A_=A_, B_=B_, W_=W_: e.tensor_copy(out=B_[0:64, 2:W_ - 1], in_=A_[0:64, 2:W_ - 1]), reads=[R_A], writes=[R_B])
                            Sfin, R_S = B_, R_B
                        else:
                            k.op(eng1, lambda e, A_=A_, B_=B_, W_=W_: e.tensor_tensor(out=A_[:, 4:W_ - 3], in0=B_[:, 2:W_ - 5], in1=B_[:, 6:W_ - 1], op=ALU.add), reads=[R_B], writes=[R_A])
                            k.op(eng1, lambda e, A_=A_, B_=B_, W_=W_: e.tensor_tensor(out=B_[64:128, 8:W_ - 7], in0=A_[64:128, 4:W_ - 11], in1=A_[64:128, 12:W_ - 3], op=ALU.add), reads=[R_A], writes=[R_B])
                            k.op(eng1, lambda e, A_=A_, B_=B_, W_=W_: e.tensor_copy(out=B_[0:64, 8:W_ - 7], in_=A_[0:64, 8:W_ - 7]), reads=[R_A], writes=[R_B])
                            Sfin, R_S = B_, R_B
                        y_, R_y, _ = ym.next()
                        k.op(eng2, lambda e, Sfin=Sfin, X=X, y_=y_, a=a: e.scalar_tensor_tensor(out=y_[:, 0:CB], in0=Sfin[:, 8:8 + CB], scalar=pinvw[:, a:a + 1], in1=X[:, 8:8 + CB],
                                                                                              op0=ALU.mult, op1=ALU.subtract), reads=[R_S, R_xp[a], R_c], writes=[R_y])
                        if b == 0:
                            k.op(eng2, lambda e, Sfin=Sfin, a=a: e.tensor_tensor(out=Sfin[:, 8:16], in0=Sfin[:, 8:16], in1=pelo[:, a, :], op=ALU.mult), reads=[R_S, R_c, R_y], writes=[R_S])
                            k.op(eng2, lambda e, Sfin=Sfin, X=X, y_=y_: e.tensor_tensor(out=y_[:, 0:8], in0=Sfin[:, 8:16], in1=X[:, 8:16], op=ALU.subtract), reads=[R_S, R_xp[a]], writes=[R_y])
                        if b == 3:
                            k.op(eng2, lambda e, Sfin=Sfin, a=a: e.tensor_tensor(out=Sfin[:, CB:CB + 8], in0=Sfin[:, CB:CB + 8], in1=pehi[:, a, :], op=ALU.mult), reads=[R_S, R_c, R_y], writes=[R_S])
                            k.op(eng2, lambda e, Sfin=Sfin, X=X, y_=y_: e.tensor_tensor(out=y_[:, CB - 8:CB], in0=Sfin[:, CB:CB + 8], in1=X[:, CB:CB + 8], op=ALU.subtract), reads=[R_S, R_xp[a]], writes=[R_y])
                        pm_t, R_pm, d_pm = pm_st.next()
                        for (c0, cw) in [(0, 512), (512, 512), (1024, CB - 1024)]:
                            bk = FMB[fmi[0] % len(FMB)]
                            fmi[0] += 1
                            k.op("pe", lambda e, bk=bk, y_=y_, c0=c0, cw=cw, a=a: e.matmul(ps[bk][:, 0:cw], lhsT=Pbd[:, a, :], rhs=y_[:, c0:c0 + cw], start=True, stop=True),
                                 reads=[R_y, R_w], writes=[R_ps[bk]])
                            k.op("dve", lambda e, bk=bk, c0=c0, cw=cw, a=a, pm_t=pm_t, o=o: e.scalar_tensor_tensor(out=pm_t[:, c0:c0 + cw], in0=ps[bk][:, 0:cw], scalar=psc[:, a:a + 1],
                                                                                                             in1=spg[:, a, o + c0:o + c0 + cw], op0=ALU.mult, op1=ALU.mult),
                                 reads=[R_ps[bk], R_spg[a], R_w], writes=[R_pm])
                        k.op("sp", lambda e, pm_t=pm_t, a=a, o=o: e.dma_start(out=mixT_s[s, 256 + a * 128:256 + (a + 1) * 128, o:o + CB], in_=pm_t[:, :]), reads=[R_pm], dma=d_pm)
            k.end_phase()

        def phase_B(l, s):
            with contextlib.ExitStack() as ls:
                kT = sb("kT", [128, 4, L], BF16, ls)
                krT = sb("krT", [128, L], BF16, ls)
                V = sb("V", [128, NT, 512], BF16, ls)
                R_kv = Res("kv")
                dkv = k.dsem(f"d_kv{l}_{s}")
                k.op("sp", lambda e: e.dma_start(out=kT[:], in_=kT_s[s, 0:512, :].rearrange("(c p) t -> p c t", p=128)), writes=[R_kv], dma=dkv)
                k.op("sp", lambda e: e.dma_start(out=krT[0:64, :], in_=kT_s[s, 512:576, :]), writes=[R_kv], dma=dkv)
                k.op("pool", lambda e: e.memset(krT[64:128, :], 0.0), writes=[R_kv])
                k.op("sp", lambda e: e.dma_start(out=V[:, 0:32, :], in_=v_s[s, 0:4096, :].rearrange("(j p) n -> p j n", p=128)), writes=[R_kv], dma=dkv)
                k.op("sp", lambda e: e.dma_start(out=V[0:16, 32, :], in_=v_s[s, 4096:L, :]), writes=[R_kv], dma=dkv)
                q_r = Ring(k, nc, ls, "qb", [128, 4, 512], BF16, 2)
                qr_r = Ring(k, nc, ls, "qrb", [128, 4, 512], BF16, 2)
                for (qt_, Rq_, _) in qr_r.items:
                    k.op("pool", lambda e: e.memset(qt_[64:128, :, :], 0.0), writes=[Rq_])
                ga_r = Ring(k, nc, ls, "gab", [128, 4, 512], BF16, 3)
                mx_r = Ring(k, nc, ls, "mxb", [128, 4, 512], BF16, 2)
                pt_r = Ring(k, nc, ls, "ptb", [128, 2, 512], BF16, 3, with_dsem=False)
                acc0_r = Ring(k, nc, ls, "acc0", [128, 2, 512], F32, 2, with_dsem=False)
                rinv_r = Ring(k, nc, ls, "rinv", [128, 512], F32, 1, with_dsem=False)
                o1_r = Ring(k, nc, ls, "o1", [128, 512], F32, 1, with_dsem=False)
                SPAIR = [0, 1]
                OB = [4, 5]
                SUMBS = [6, 7]
                spi = [0]
                obi = [0]
                NPAIR = 17
                def b_loads(u):
                    t0, n = STILES[u]
                    q_t, R_q, d_q = q_r.next()
                    qr_t, R_qr, d_qr = qr_r.next()
                    ga_t, R_ga, d_ga = ga_r.next()
                    k.op("sp", lambda e: e.dma_start(out=q_t[:, :, 0:n], in_=qT_s[s, 0:512, t0:t0 + n].rearrange("(c p) t -> p c t", p=128)), writes=[R_q], dma=d_q)
                    k.op("sp", lambda e: e.dma_start(out=qr_t[0:64, :, 0:n], in_=qT_s[s, 512:768, t0:t0 + n].rearrange("(c p) t -> p c t", p=64)), writes=[R_qr], dma=d_qr)
                    k.op("sp", lambda e: e.dma_start(out=ga_t[:, :, 0:n], in_=gTa_s[s, :, t0:t0 + n].rearrange("(c p) t -> p c t", p=128)), writes=[R_ga], dma=d_ga)
                    return (q_t, R_q, qr_t, R_qr, ga_t, R_ga)

                pending_fin = [None]
                nxt = b_loads(0)
                for u, (t0, n) in enumerate(STILES):
                    q_t, R_q, qr_t, R_qr, ga_t, R_ga = nxt
                    if u + 1 < len(STILES):
                        nxt = b_loads(u + 1)
                    mx_t, R_mx, d_mx = mx_r.next()
                    for h in range(4):
                        ob = OB[obi[0] % 2]
                        obi[0] += 1
                        acc0, R_a0, _ = acc0_r.next()
                        sumb = SUMBS[(obi[0] - 1) % 2]

                        def qk(p):
                            bp = SPAIR[spi[0] % 2]
                            spi[0] += 1
                            kts = [2 * p, 2 * p + 1] if p < NPAIR - 1 else [2 * p]
                            for i, kt in enumerate(kts):
                                kn = 128 if kt < 32 else 16
                                bk = 2 * bp + i
                                k.op("pe", lambda e: e.matmul(ps[bk][0:kn, 0:n], lhsT=kT[:, h, kt * 128:kt * 128 + kn], rhs=q_t[:, h, 0:n], start=True, stop=False),
                                     reads=[R_kv, R_q], writes=[R_ps[bk]])
                                k.op("pe", lambda e: e.matmul(ps[bk][0:kn, 0:n], lhsT=krT[:, kt * 128:kt * 128 + kn], rhs=qr_t[:, h, 0:n], start=False, stop=True),
                                     reads=[R_kv, R_qr], writes=[R_ps[bk]])
                            p_t, R_p, _ = pt_r.next()
                            nh = len(kts)
                            kn = 128 if kts[-1] < 32 else 16
                            src = psd[bp][0:kn, :].rearrange("p (b m) -> p b m", b=2)[:, 0:nh, 0:n]
                            k.op("act", lambda e: e.activation(out=p_t[0:kn, 0:nh, 0:n], in_=src, func=AF.Exp), reads=[R_ps[2 * bp], R_ps[2 * bp + 1]], writes=[R_p])
                            return (p, kts, p_t, R_p)

                        def pv(item):
                            p, kts, p_t, R_p = item
                            for i, kt in enumerate(kts):
                                kn = 128 if kt < 32 else 16
                                k.op("pe", lambda e: e.matmul(ps[ob][:, 0:n], lhsT=V[0:kn, kt, h * 128:(h + 1) * 128], rhs=p_t[0:kn, i, 0:n], start=(kt == 0), stop=(kt == NT - 1)),
                                     reads=[R_kv, R_p], writes=[R_ps[ob]])
                            nh = len(kts)
                            kn = 128 if kts[-1] < 32 else 16
                            if p % 3 == 2:
                                for i, kt in enumerate(kts):
                                    k.op("pe", lambda e: e.matmul(ps[sumb][:, 0:n], lhsT=ones_b[0:kn, :], rhs=p_t[0:kn, i, 0:n], start=(p == 2 and i == 0), stop=False),
                                         reads=[R_c, R_p], writes=[R_ps[sumb]])
                            elif p == 0:
                                k.op("dve", lambda e: e.tensor_copy(out=acc0[:, :, 0:n], in_=p_t[:, :, 0:n]), reads=[R_p], writes=[R_a0])
                            else:
                                k.op("dve", lambda e: e.tensor_tensor(out=acc0[0:kn, 0:nh, 0:n], in0=acc0[0:kn, 0:nh, 0:n], in1=p_t[0:kn, 0:nh, 0:n], op=ALU.add), reads=[R_p, R_a0], writes=[R_a0])

                        def make_fin(acc0=acc0, R_a0=R_a0, sumb=sumb, ob=ob, n=n, h=h, mx_t=mx_t, R_mx=R_mx, d_mx=d_mx, ga_t=ga_t, R_ga=R_ga, t0=t0):
                            def fin():
                                k.op("dve", lambda e: e.tensor_tensor(out=acc0[:, 0, 0:n], in0=acc0[:, 0, 0:n], in1=acc0[:, 1, 0:n], op=ALU.add), reads=[R_a0], writes=[R_a0])
                                k.op("pe", lambda e: e.matmul(ps[sumb][:, 0:n], lhsT=ones_f[:], rhs=acc0[:, 0, 0:n], start=False, stop=True), reads=[R_a0, R_c], writes=[R_ps[sumb]])
                                rinv, R_ri, _ = rinv_r.next()
                                k.op("dve", lambda e: e.reciprocal(out=rinv[:, 0:n], in_=ps[sumb][:, 0:n]), reads=[R_ps[sumb]], writes=[R_ri])
                                o1, R_o1, _ = o1_r.next()
                                k.op("dve", lambda e: e.tensor_tensor(out=o1[:, 0:n], in0=ps[ob][:, 0:n], in1=rinv[:, 0:n], op=ALU.mult), reads=[R_ps[ob], R_ri], writes=[R_o1])
                                k.op("pool", lambda e: e.tensor_tensor(out=mx_t[:, h, 0:n], in0=o1[:, 0:n], in1=ga_t[:, h, 0:n], op=ALU.mult), reads=[R_o1, R_ga], writes=[R_mx])
                                if h == 3:
                                    k.op("sp", lambda e: e.dma_start(out=mixT_s[s, 512:1024, t0:t0 + n].rearrange("(c p) t -> p c t", p=128), in_=mx_t[:, :, 0:n]), reads=[R_mx], dma=d_mx)
                            return fin

                        prev = None
                        for p in range(NPAIR):
                            cur = qk(p)
                            if prev is not None:
                                pv(prev)
                            prev = cur
                            if p == 2 and pending_fin[0] is not None:
                                pending_fin[0]()
                                pending_fin[0] = None
                        pv(prev)
                        if pending_fin[0] is not None:
                            pending_fin[0]()
                        pending_fin[0] = make_fin()
                if pending_fin[0] is not None:
                    pending_fin[0]()
            k.end_phase()

        def phase_C(l):
            with contextlib.ExitStack() as ls:
                Pc = sb("Pc", [128, NS, NT, 512], BF16, ls)
                R_pc = Res("pc")
                dpc = k.dsem(f"d_pc{l}")
                for s in range(NS):
                    k.op("sp", lambda e, s=s: e.dma_start(out=Pc[:, s, 0:32, :], in_=pcs_s[s, 0:4096, :].rearrange("(j p) n -> p j n", p=128)), writes=[R_pc], dma=dpc)
                    k.op("sp", lambda e, s=s: e.dma_start(out=Pc[0:16, s, 32, :], in_=pcs_s[s, 4096:L, :]), writes=[R_pc], dma=dpc)
                GC = 8
                c_r = Ring(k, nc, ls, "dC", [128, GC, 512], BF16, 3)
                s_r = Ring(k, nc, ls, "dS", [128, GC, 512], BF16, 3)
                gf_r = Ring(k, nc, ls, "gfl", [128, 512], BF16, 4)
                mf_r = Ring(k, nc, ls, "mfl", [128, 512], BF16, 4)
                groups = [(0, 8), (8, 8), (16, 8), (24, 8), (32, 1)]
                for ki, (k0, n) in enumerate(STILES):
                    banks = [[(ki % 2) * 4 + s2 * 2 + a for a in range(2)] for s2 in range(NS)]
                    first = True
                    for (g0, gn) in groups:
                        ct, R_ct, d_ct = c_r.next()
                        st_, R_st, d_st = s_r.next()
                        if gn == 8:
                            k.op("sp", lambda e, ct=ct, g0=g0: e.dma_start(out=ct[:, :, 0:n], in_=dftC[g0 * 128:(g0 + 8) * 128, k0:k0 + n].rearrange("(c p) k -> p c k", p=128)), writes=[R_ct], dma=d_ct)
                            k.op("sp", lambda e, st_=st_, g0=g0: e.dma_start(out=st_[:, :, 0:n], in_=dftS[g0 * 128:(g0 + 8) * 128, k0:k0 + n].rearrange("(c p) k -> p c k", p=128)), writes=[R_st], dma=d_st)
                        else:
                            k.op("sp", lambda e, ct=ct: e.dma_start(out=ct[0:16, 0, 0:n], in_=dftC[4096:L, k0:k0 + n]), writes=[R_ct], dma=d_ct)
                            k.op("sp", lambda e, st_=st_: e.dma_start(out=st_[0:16, 0, 0:n], in_=dftS[4096:L, k0:k0 + n]), writes=[R_st], dma=d_st)
                        for ci in range(gn):
                            ch = g0 + ci
                            ln = 128 if ch < 32 else 16
                            for cs, (xt, R_x) in enumerate([(ct, R_ct), (st_, R_st)]):
                                last = (ch == NT - 1) and (cs == 1)
                                for s2 in range(NS):
                                    for a in range(2):
                                        bk = banks[s2][a]
                                        k.op("pe", lambda e, bk=bk, s2=s2, a=a, ch=ch, ln=ln, cs=cs, xt=xt, ci=ci, first=first, last=last:
                                             e.matmul(ps[bk][:, 0:n], lhsT=Pc[0:ln, s2, ch, cs * 256 + a * 128:cs * 256 + (a + 1) * 128], rhs=xt[0:ln, ci, 0:n], start=first, stop=last),
                                             reads=[R_pc, R_x], writes=[R_ps[bk]])
                                first = False
                    for s2 in range(NS):
                        for a in range(2):
                            bk = banks[s2][a]
                            gf, R_gf, d_gf = gf_r.next()
                            mf, R_mf, d_mf = mf_r.next()
                            k.op("sp", lambda e, gf=gf, s2=s2, a=a: e.dma_start(out=gf[:, 0:n], in_=gTf_s[s2, a * 128:(a + 1) * 128, k0:k0 + n]), writes=[R_gf], dma=d_gf)
                            k.op("dve", lambda e, gf=gf, mf=mf, bk=bk: e.tensor_tensor(out=mf[:, 0:n], in0=ps[bk][:, 0:n], in1=gf[:, 0:n], op=ALU.mult), reads=[R_ps[bk], R_gf], writes=[R_mf])
                            k.op("pool", lambda e, mf=mf, s2=s2, a=a: e.dma_start(out=mixT_s[s2, a * 128:(a + 1) * 128, k0:k0 + n], in_=mf[:, 0:n]), reads=[R_mf], dma=d_mf)
            k.end_phase()

        def phase_D(l, s, h_src, last):
            with contextlib.ExitStack() as ls:
                mx_r = Ring(k, nc, ls, "dmx", [128, 8, 512], BF16, 2)
                h_r = Ring(k, nc, ls, "dh", [128, 1024], F32, 4)
                o_r = Ring(k, nc, ls, "do", [128, 1024], F32, 3)
                ss_r = Ring(k, nc, ls, "dss", [128, 1], F32, 4, with_dsem=False)
                junk = sb("djunk", [128, 1024], BF16, ls)
                R_fn = Res("fnw")
                if last:
                    fnw = sb("fnw", [128, 1024], F32, ls)
                    dfn = k.dsem(f"d_fn{s}")
                    k.op("sp", lambda e: e.dma_start(out=fnw[:], in_=final_norm_w.partition_broadcast(128)), writes=[R_fn], dma=dfn)
                OBK = [0, 1, 2, 3, 4, 5, 6, 7]
                obi = [0]
                flat = []
                for u, (t0, n) in enumerate(STILES):
                    for j, (r0, nr) in enumerate(tiles_of(t0, n)):
                        flat.append((u, j, t0, n, r0, nr))
                mxs = {}
                hts = {}

                def d_load_mx(u):
                    t0, n = STILES[u]
                    mx, R_mx, d_mx = mx_r.next()
                    k.op("sp", lambda e: e.dma_start(out=mx[:, :, 0:n], in_=mixT_s[s, :, t0:t0 + n].rearrange("(c p) t -> p c t", p=128)), writes=[R_mx], dma=d_mx)
                    mxs[u] = (mx, R_mx)

                def d_load_h(i):
                    u, j, t0, n, r0, nr = flat[i]
                    ht, R_ht, d_ht = h_r.next()
                    k.op("sp", lambda e: e.dma_start(out=ht[0:nr, :], in_=h_src[s, t0 + r0:t0 + r0 + nr, :]), writes=[R_ht], dma=d_ht)
                    hts[i] = (ht, R_ht)

                d_load_mx(0)
                d_load_h(0)
                d_load_h(1)
                for i, (u, j, t0, n, r0, nr) in enumerate(flat):
                    if j == 0 and u + 1 < len(STILES):
                        d_load_mx(u + 1)
                    if i + 2 < len(flat):
                        d_load_h(i + 2)
                    mx, R_mx = mxs[u]
                    ht, R_ht = hts[i]
                    ot, R_ot, d_ot = o_r.next()
                    for half in range(2):
                        bk = OBK[obi[0] % len(OBK)]
                        obi[0] += 1
                        for c in range(8):
                            k.op("pe", lambda e: e.matmul(ps[bk][0:nr, :], lhsT=mx[:, c, r0:r0 + nr], rhs=Wout[:, c, half * 512:(half + 1) * 512],
                                                          start=(c == 0), stop=(c == 7)), reads=[R_mx, R_w], writes=[R_ps[bk]])
                        k.op("dve", lambda e: e.tensor_tensor(out=ot[0:nr, half * 512:(half + 1) * 512], in0=ps[bk][0:nr, :], in1=ht[0:nr, half * 512:(half + 1) * 512], op=ALU.add),
                             reads=[R_ps[bk], R_ht], writes=[R_ot])
                    if not last:
                        k.op("pool", lambda e: e.dma_start(out=hbuf[s, t0 + r0:t0 + r0 + nr, :], in_=ot[0:nr, :]), reads=[R_ot], dma=d_ot)
                    else:
                        ss, R_ss, _ = ss_r.next()
                        k.op("act", lambda e: e.activation(out=junk[0:nr, :], in_=ot[0:nr, :], func=AF.Square, scale=float(DM ** -0.5), accum_out=ss[0:nr, 0:1]), reads=[R_ot], writes=[R_ss])
                        k.op("act", lambda e: e.activation(out=ss[0:nr, 0:1], in_=ss[0:nr, 0:1], func=AF.Sqrt, bias=eps_t[0:nr, 0:1]), reads=[R_ss], writes=[R_ss])
                        k.op("dve", lambda e: e.reciprocal(out=ss[0:nr, 0:1], in_=ss[0:nr, 0:1]), reads=[R_ss], writes=[R_ss])
                        k.op("dve", lambda e: e.scalar_tensor_tensor(out=ot[0:nr, :], in0=ot[0:nr, :], scalar=ss[0:nr, 0:1], in1=fnw[0:nr, :], op0=ALU.mult, op1=ALU.mult),
                             reads=[R_ot, R_ss, R_fn], writes=[R_ot])
                        g0 = t0 + r0
                        lo = max(g0, NMETA)
                        hi = g0 + nr
                        if hi > lo:
                            k.op("pool", lambda e: e.dma_start(out=out_d[s, lo - NMETA:hi - NMETA, :], in_=ot[lo - g0:hi - g0, :]), reads=[R_ot], dma=d_ot)
            k.end_phase()

        for li, l in enumerate(layers):
            h_src = h0 if (li == 0 and first_from_input) else hbuf
            if "P" in phases:
                phase_P(l)
            for s in range(NS):
                if "A" in phases:
                    phase_A(l, s, h_src)
            for s in range(NS):
                if "B" in phases:
                    phase_B(l, s)
            if "C" in phases:
                phase_C(l)
            last = final_norm and (li == len(layers) - 1)
            for s in range(NS):
                if "D" in phases:
                    phase_D(l, s, h_src, last)
        k.emit()
        build_program.nops = k.nops
    return nc


_PROG = {}


def get_prog(key, **kw):
    if key not in _PROG:
        _PROG[key] = build_program(**kw)
    return _PROG[key]


def make_in_maps(inputs, h0_full):
    c = host_consts()
    shared = {
        "norm_w": inputs["norm_w"], "w_in": inputs["w_in"], "fourier_w": inputs["fourier_w"], "pool_w": inputs["pool_w"],
        "pool_scale": inputs["pool_scale"], "q_norm_w": inputs["q_norm_w"], "w_uq": inputs["w_uq"], "kv_norm_w": inputs["kv_norm_w"],
        "w_ukv": inputs["w_ukv"], "w_out": inputs["w_out"], "final_norm_w": inputs["final_norm_w"],
    }
    shared = {k_: np.ascontiguousarray(np.asarray(v, dtype=np.float32)) for k_, v in shared.items()}
    shared.update(c)
    maps = []
    for i in range(NCORES):
        m = dict(shared)
        m["h0"] = h0_full[i * NS:(i + 1) * NS]
        maps.append(m)
    return maps


def kernel(x, meta_tokens, norm_w, w_in, fourier_w, pool_w, pool_scale, q_norm_w, w_uq, kv_norm_w, w_ukv, w_out, final_norm_w):
    inputs = dict(norm_w=norm_w, w_in=w_in, fourier_w=fourier_w, pool_w=pool_w, pool_scale=pool_scale, q_norm_w=q_norm_w,
                  w_uq=w_uq, kv_norm_w=kv_norm_w, w_ukv=w_ukv, w_out=w_out, final_norm_w=final_norm_w)
    x = np.asarray(x, dtype=np.float32)
    B = x.shape[0]
    meta = np.asarray(meta_tokens, dtype=np.float32)
    h0 = np.concatenate([np.broadcast_to(meta[None], (B, NMETA, DM)), x], axis=1)
    h0 = np.ascontiguousarray(h0)
    nc = get_prog("full", layers=list(range(DEPTH)))
    maps = make_in_maps(inputs, h0)
    res = run_bass_kernel_spmd(nc, maps, core_ids=list(range(NCORES)))
    out = np.concatenate([np.asarray(r["out"]) for r in res.results], axis=0)
    return out.astype(np.float32)
```

```python
import contextlib
import numpy as np
import ml_dtypes
import concourse.bass as bass
import concourse.mybir as mybir
from concourse.bass_utils import run_bass_kernel_spmd

F32 = mybir.dt.float32
BF16 = mybir.dt.bfloat16
AF = mybir.ActivationFunctionType
ALU = mybir.AluOpType

NCORES = 8
NS = 2
DM = 1024
SEQ = 4096
NMETA = 16
L = SEQ + NMETA
DEPTH = 4
DIN = 2240
EPS = 1e-6
NT = 33
TM_COLS = 1152
FM_COLS = 1408
SCALE = 192 ** -0.5

STILES = [(i * 512, 512) for i in range(8)] + [(4096, 16)]


def tiles_of(t0, n):
    if n == 512:
        return [(j * 128, 128) for j in range(4)]
    return [(0, n)]


class SemObj:
    def __init__(self, handle, name):
        self.h = handle
        self.name = name
        self.count = 0


class Res:
    __slots__ = ("name", "w", "r")

    def __init__(self, name=""):
        self.name = name
        self.w = None
        self.r = {}


class _Rec:
    def __init__(self):
        self.call = None

    def __getattr__(self, name):
        def f(*a, **kw):
            self.call = (name, a, kw)
            return None
        return f


class K:
    ENGS = ("pe", "act", "dve", "pool", "sp")

    def __init__(self, nc, stack):
        self.nc = nc
        self.stack = stack
        self.ops = {e: [] for e in self.ENGS}
        self.esem = {e: SemObj(stack.enter_context(nc.semaphore("s_" + e)), "s_" + e) for e in self.ENGS}
        self.waited = {e: {} for e in self.ENGS}
        self.dsems = []
        self.free_dsems = []
        self.phase_dsems = []
        self.nops = 0

    def dsem(self, name):
        if self.free_dsems:
            s = self.free_dsems.pop()
        else:
            s = SemObj(self.stack.enter_context(self.nc.semaphore(f"dsem{len(self.dsems)}")), f"dsem{len(self.dsems)}")
            self.dsems.append(s)
        self.phase_dsems.append(s)
        return s

    def op(self, eng, fn, reads=(), writes=(), dma=None):
        rec = _Rec()
        fn(rec)
        call = rec.call
        assert call is not None
        deps = []
        for r in reads:
            if r.w is not None:
                deps.append(r.w)
        for w in writes:
            if w.w is not None:
                deps.append(w.w)
            deps.extend(w.r.values())
        my = self.esem[eng]
        waits = {}
        wd = self.waited[eng]
        for (s, v) in deps:
            if s is my and eng == "pe":
                continue
            if wd.get(s.name, 0) >= v:
                continue
            if waits.get(s.name, (None, 0))[1] < v:
                waits[s.name] = (s, v)
        for (s, v) in waits.values():
            wd[s.name] = v
        if dma is not None:
            dma.count += 16
            mark = (dma, dma.count)
            inc = (dma, 16)
        else:
            my.count += 1
            mark = (my, my.count)
            inc = (my, 1)
        for r in reads:
            r.r[mark[0].name] = mark
        for w in writes:
            w.w = mark
            w.r = {}
        self.ops[eng].append((list(waits.values()), call, inc))
        self.nops += 1
        return mark

    def barrier(self):
        allsems = list(self.esem.values()) + self.dsems
        for eng in self.ENGS:
            waits = []
            wd = self.waited[eng]
            for s in allsems:
                if s is self.esem[eng]:
                    continue
                if s.count > wd.get(s.name, 0):
                    waits.append((s, s.count))
                    wd[s.name] = s.count
            if waits:
                self.ops[eng].append((waits, None, None))

    def end_phase(self):
        self.barrier()
        self.free_dsems.extend(self.phase_dsems)
        self.phase_dsems = []

    def emit(self):
        nc = self.nc
        with nc.Block() as block:
            def run(engname):
                def body(e):
                    for (waits, call, inc) in self.ops[engname]:
                        for (s, v) in waits:
                            e.wait_ge(s.h, v)
                        if call is not None:
                            name, a, kw = call
                            ins = getattr(e, name)(*a, **kw)
                            ins.then_inc(inc[0].h, inc[1])
                return body
            block.tensor(run("pe"))
            block.scalar(run("act"))
            block.vector(run("dve"))
            block.gpsimd(run("pool"))
            block.sync(run("sp"))


_UID = [0]


class Ring:
    def __init__(self, k, nc, st, name, shape, dtype, n, with_dsem=True):
        self.items = []
        for i in range(n):
            _UID[0] += 1
            t = st.enter_context(nc.sbuf_tensor(f"rg_{name}{i}_{_UID[0]}", shape, dtype))
            self.items.append((t, Res(f"{name}{i}"), k.dsem(f"d_{name}{i}") if with_dsem else None))
        self.i = 0

    def next(self):
        it = self.items[self.i % len(self.items)]
        self.i += 1
        return it


_CONSTS = None


def host_consts():
    global _CONSTS
    if _CONSTS is not None:
        return _CONSTS
    bf = ml_dtypes.bfloat16
    c = {}
    c["ident"] = np.eye(128, dtype=np.float32).astype(bf)
    idx = np.arange(L, dtype=np.int64)
    lk = (idx[:, None] * idx[None, :]) % L
    ang = lk.astype(np.float64) * (2.0 * np.pi / L)
    c["dftC"] = (np.cos(ang) / np.sqrt(L)).astype(np.float32).astype(bf)
    c["dftS"] = (np.sin(ang) / np.sqrt(L)).astype(np.float32).astype(bf)
    del lk, ang
    i64 = np.arange(64, dtype=np.int64)
    a64 = ((i64[:, None] * i64[None, :]) % 64).astype(np.float64) * (2.0 * np.pi / 64)
    c64 = np.cos(a64) / 8.0
    s64 = -np.sin(a64) / 8.0
    cb = np.zeros((128, 128), np.float32)
    sb = np.zeros((128, 128), np.float32)
    for b in range(2):
        cb[b * 64:(b + 1) * 64, b * 64:(b + 1) * 64] = c64
        sb[b * 64:(b + 1) * 64, b * 64:(b + 1) * 64] = s64
    c["c64bd"] = cb.astype(bf)
    c["s64bd"] = sb.astype(bf)
    inv = (1.0 / (np.float32(10000.0) ** (np.arange(0, 64, 2, dtype=np.float32) / np.float32(64)))).astype(np.float32)
    angr = (np.arange(L, dtype=np.float32)[:, None] * inv[None, :]).astype(np.float32)
    cosr = np.cos(angr).astype(np.float32).T
    sinr = np.sin(angr).astype(np.float32).T
    c["cos4"] = np.ascontiguousarray(np.tile(cosr, (4, 1)))
    c["sin4"] = np.ascontiguousarray(np.tile(sinr, (4, 1)))
    wins = (2, 4, 8, 16)
    invw = np.zeros((128, 2), np.float32)
    elo = np.zeros((128, 2, 8), np.float32)
    ehi = np.zeros((128, 2, 8), np.float32)
    for a in range(2):
        for p in range(128):
            w = wins[2 * a + p // 64]
            invw[p, a] = 1.0 / w
            for j in range(8):
                i = j
                cnt = min(i + w // 2, L) - max(i - w // 2, 0)
                elo[p, a, j] = 1.0 / cnt
                i = L - 8 + j
                cnt = min(i + w // 2, L) - max(i - w // 2, 0)
                ehi[p, a, j] = 1.0 / cnt
    c["pinvw"] = invw
    c["pelo"] = elo
    c["pehi"] = ehi
    _CONSTS = c
    return c


def build_program(layers, first_from_input=True, final_norm=True, debug=False, phases="PABCD"):
    nc = bass.Bass("TRN2", target_bir_lowering=False)
    dk = "ExternalOutput" if debug else "Internal"

    def din(name, shape, dt=F32):
        return nc.dram_tensor(name, list(shape), dt, kind="ExternalInput").ap()

    h0 = din("h0", [NS, L, DM])
    norm_w = din("norm_w", [DEPTH, DM])
    w_in = din("w_in", [DEPTH, DM, DIN])
    fourier_w = din("fourier_w", [DEPTH, 4, 64, 64])
    pool_w = din("pool_w", [DEPTH, 4, 64, 64])
    pool_scale = din("pool_scale", [DEPTH, 256])
    q_norm_w = din("q_norm_w", [DEPTH, 384])
    w_uq = din("w_uq", [DEPTH, 384, 768])
    kv_norm_w = din("kv_norm_w", [DEPTH, 256])
    w_ukv = din("w_ukv", [DEPTH, 256, 1024])
    w_out = din("w_out", [DEPTH, DM, DM])
    final_norm_w = din("final_norm_w", [DM])
    ident_d = din("ident", [128, 128], BF16)
    dftC = din("dftC", [L, L], BF16)
    dftS = din("dftS", [L, L], BF16)
    c64bd_d = din("c64bd", [128, 128], BF16)
    s64bd_d = din("s64bd", [128, 128], BF16)
    cos4_d = din("cos4", [128, L])
    sin4_d = din("sin4", [128, L])
    pinvw_d = din("pinvw", [128, 2])
    pelo_d = din("pelo", [128, 2, 8])
    pehi_d = din("pehi", [128, 2, 8])

    if final_norm:
        out_d = nc.dram_tensor("out", [NS, SEQ, DM], F32, kind="ExternalOutput").ap()
        hbuf = nc.dram_tensor("hbuf", [NS, L, DM], F32, kind="Internal").ap()
    else:
        out_d = None
        hbuf = nc.dram_tensor("hbuf", [NS, L, DM], F32, kind="ExternalOutput").ap()
    pcs_s = nc.dram_tensor("pcs_s", [NS, L, 512], BF16, kind=dk).ap()
    qT_s = nc.dram_tensor("qT_s", [NS, 768, L], BF16, kind=dk).ap()
    kT_s = nc.dram_tensor("kT_s", [NS, 576, L], BF16, kind=dk).ap()
    v_s = nc.dram_tensor("v_s", [NS, L, 512], BF16, kind=dk).ap()
    gTa_s = nc.dram_tensor("gTa_s", [NS, 512, L], BF16, kind=dk).ap()
    gTf_s = nc.dram_tensor("gTf_s", [NS, 256, L], BF16, kind=dk).ap()
    mixT_s = nc.dram_tensor("mixT_s", [NS, 1024, L], BF16, kind=dk).ap()

    with contextlib.ExitStack() as st:
        k = K(nc, st)

        def sb(name, shape, dt, stack=st):
            _UID[0] += 1
            return stack.enter_context(nc.sbuf_tensor(f"sb_{name}_{_UID[0]}", list(shape), dt))

        ident = sb("ident", [128, 128], BF16)
        ones_f = sb("ones_f", [128, 128], F32)
        ones_b = sb("ones_b", [128, 128], BF16)
        eps_t = sb("eps_t", [128, 1], F32)
        Win_tm = sb("Win_tm", [128, 8, TM_COLS], BF16)
        Win_fm = sb("Win_fm", [128, 8, FM_COLS], BF16)
        Wuq = sb("Wuq", [128, 3, 1024], BF16)
        Wukv = sb("Wukv", [128, 2, 1024], BF16)
        Wout = sb("Wout", [128, 8, 1024], BF16)
        Pbd = sb("Pbd", [128, 2, 128], BF16)
        psc = sb("psc", [128, 2], F32)
        c64bd = sb("c64bd", [128, 128], BF16)
        s64bd = sb("s64bd", [128, 128], BF16)
        pinvw = sb("pinvw", [128, 2], F32)
        pelo = sb("pelo", [128, 2, 8], F32)
        pehi = sb("pehi", [128, 2, 8], F32)
        R_w = Res("weights")
        R_c = Res("consts")
        psd = [st.enter_context(nc.psum_tensor(f"psd{i}", [128, 1024], F32)) for i in range(4)]
        ps = [psd[i // 2][:, (i % 2) * 512:(i % 2 + 1) * 512] for i in range(8)]
        R_ps = [Res(f"ps{i}") for i in range(8)]
        dconst = k.dsem("d_const")

        def ld_const(dst, src):
            k.op("sp", lambda e: e.dma_start(out=dst, in_=src), writes=[R_c], dma=dconst)

        ld_const(ident[:], ident_d)
        ld_const(c64bd[:], c64bd_d)
        ld_const(s64bd[:], s64bd_d)
        ld_const(pinvw[:], pinvw_d)
        ld_const(pelo[:], pelo_d)
        ld_const(pehi[:], pehi_d)
        k.op("pool", lambda e: e.memset(ones_f[:], 1.0), writes=[R_c])
        k.op("pool", lambda e: e.memset(ones_b[:], 1.0), writes=[R_c])
        k.op("pool", lambda e: e.memset(eps_t[:], EPS), writes=[R_c])
        k.end_phase()

        def phase_P(l):
            with contextlib.ExitStack() as ls:
                stg = Ring(k, nc, ls, "wst", [128, 8, 640], F32, 2)
                nw = sb("nw", [128, 8], F32, ls)
                qnw = sb("qnw", [128, 3], F32, ls)
                kvnw = sb("kvnw", [128, 2], F32, ls)
                wf_st = sb("wf_st", [128, 2, 64], F32, ls)
                pw_st = sb("pw_st", [128, 2, 64], F32, ls)
                wf_b = sb("wf_b", [128, 2, 64], BF16, ls)
                Mbd = sb("Mbd", [128, 2, 2, 128], BF16, ls)
                Wf_bf = sb("Wf_bf", [128, 8, 256], BF16, ls)
                WfT = sb("WfT", [128, 2, 1024], BF16, ls)
                R_small = Res("small")
                R_wfb = Res("wfb")
                R_mbd = Res("mbd")
                R_wfbf = Res("wfbf")
                R_wft = Res("wft")
                dsm = k.dsem(f"d_small{l}")

                def ld_small(dst, src):
                    k.op("sp", lambda e: e.dma_start(out=dst, in_=src, allow_slow_non_contiguous=True),
                         writes=[R_small], dma=dsm)

                ld_small(nw[:], norm_w[l].rearrange("(c p) -> p c", p=128))
                ld_small(qnw[:], q_norm_w[l].rearrange("(c p) -> p c", p=128))
                ld_small(kvnw[:], kv_norm_w[l].rearrange("(c p) -> p c", p=128))
                ld_small(psc[:], pool_scale[l].rearrange("(a p) -> p a", p=128))
                ld_small(wf_st[:], fourier_w[l].rearrange("(a hh) c d -> (hh c) a d", hh=2))
                ld_small(pw_st[:], pool_w[l].rearrange("(a hh) c d -> (hh c) a d", hh=2))

                engs = ["act", "dve"]
                ei = [0]

                def scaled_cast(dst, src, sc_ap, mul2=None, eng=None):
                    if eng is None:
                        eng = engs[ei[0] % 2]
                        ei[0] += 1
                    if eng == "act":
                        if mul2 is None:
                            fn = lambda e: e.activation(out=dst, in_=src, func=AF.Copy, scale=sc_ap)
                        else:
                            eng = "dve"
                    if eng != "act":
                        if mul2 is None:
                            fn = lambda e: e.tensor_scalar(out=dst, in0=src, scalar1=sc_ap, scalar2=None, op0=ALU.mult)
                        else:
                            fn = lambda e: e.tensor_scalar(out=dst, in0=src, scalar1=sc_ap, scalar2=float(mul2),
                                                           op0=ALU.mult, op1=ALU.mult)
                    return eng, fn

                Wv = w_in[l].rearrange("(c p) n -> p c n", p=128)
                pieces = [(0, 512), (512, 1024), (1024, 1664), (1664, 2240)]
                for (c0, c1) in pieces:
                    t, R_t, ds = stg.next()
                    wcols = c1 - c0
                    k.op("sp", lambda e, t=t, c0=c0, c1=c1, wcols=wcols: e.dma_start(out=t[:, :, 0:wcols], in_=Wv[:, :, c0:c1]),
                         writes=[R_t], dma=ds)
                    jobs = []
                    if c0 == 0:
                        jobs.append((Wf_bf, 0, 0, 256, None, R_wfbf))
                        jobs.append((Win_fm, 0, 256, 256, None, R_w))
                    elif c0 == 512:
                        jobs.append((Win_fm, 1024, 512, 256, None, R_w))
                        jobs.append((Win_fm, 256, 768, 256, None, R_w))
                    elif c0 == 1024:
                        jobs.append((Win_tm, 512, 1024, 384, None, R_w))
                        jobs.append((Win_tm, 896, 1408, 256, None, R_w))
                    else:
                        jobs.append((Win_fm, 1280, 1664, 64, None, R_w))
                        jobs.append((Win_fm, 1344, 1696, 32, -1.0, R_w))
                        jobs.append((Win_fm, 1376, 1664, 32, None, R_w))
                        jobs.append((Win_fm, 512, 1728, 512, None, R_w))
                    for (dt_, d0, s0, ncol, mul2, rr) in jobs:
                        for c in range(8):
                            eng, fn = scaled_cast(dt_[:, c, d0:d0 + ncol], t[:, c, s0 - c0:s0 - c0 + ncol], nw[:, c:c + 1], mul2)
                            k.op(eng, fn, reads=[R_t, R_small], writes=[rr])

                t, R_t, ds = stg.next()
                tq = t[:].rearrange("p c n -> p (c n)")[:, 0:3 * 768].rearrange("p (c n) -> p c n", c=3)
                k.op("sp", lambda e: e.dma_start(out=tq, in_=w_uq[l].rearrange("(c p) n -> p c n", p=128)),
                     writes=[R_t], dma=ds)
                for c in range(3):
                    srcv = tq[:, c, :].rearrange("p (h e) -> p h e", h=4)
                    sc = qnw[:, c:c + 1]
                    eng, fn = scaled_cast(Wuq[:, c, 0:512].rearrange("p (h e) -> p h e", h=4), srcv[:, :, 0:128], sc, SCALE, "dve")
                    k.op(eng, fn, reads=[R_t, R_small], writes=[R_w])
                    eng, fn = scaled_cast(Wuq[:, c, 512:768].rearrange("p (h e) -> p h e", h=4), srcv[:, :, 128:192], sc, SCALE, "dve")
                    k.op(eng, fn, reads=[R_t, R_small], writes=[R_w])
                    rot = Wuq[:, c, 768:1024].rearrange("p (h e) -> p h e", h=4)
                    eng, fn = scaled_cast(rot[:, :, 0:32], srcv[:, :, 160:192], sc, -SCALE, "dve")
                    k.op(eng, fn, reads=[R_t, R_small], writes=[R_w])
                    eng, fn = scaled_cast(rot[:, :, 32:64], srcv[:, :, 128:160], sc, SCALE, "dve")
                    k.op(eng, fn, reads=[R_t, R_small], writes=[R_w])
                t, R_t, ds = stg.next()
                tkv = t[:].rearrange("p c n -> p (c n)")[:, 0:2 * 1024].rearrange("p (c n) -> p c n", c=2)
                k.op("sp", lambda e: e.dma_start(out=tkv, in_=w_ukv[l].rearrange("(c p) n -> p c n", p=128)),
                     writes=[R_t], dma=ds)
                for c in range(2):
                    srcv = tkv[:, c, :].rearrange("p (h e) -> p h e", h=4)
                    sc = kvnw[:, c:c + 1]
                    eng, fn = scaled_cast(Wukv[:, c, 0:512].rearrange("p (h e) -> p h e", h=4), srcv[:, :, 0:128], sc, None, "act")
                    k.op(eng, fn, reads=[R_t, R_small], writes=[R_w])
                    eng, fn = scaled_cast(Wukv[:, c, 512:1024].rearrange("p (h e) -> p h e", h=4), srcv[:, :, 128:256], sc, None, "dve")
                    k.op(eng, fn, reads=[R_t, R_small], writes=[R_w])
                Wo = w_out[l].rearrange("(c p) n -> p c n", p=128)
                for half in range(2):
                    t, R_t, ds = stg.next()
                    tv = t[:].rearrange("p c n -> p (c n)")[:, 0:4096].rearrange("p (c n) -> p c n", c=4)
                    k.op("sp", lambda e, tv=tv, half=half: e.dma_start(out=tv, in_=Wo[:, half * 4:(half + 1) * 4, :]),
                         writes=[R_t], dma=ds)
                    for c in range(4):
                        eng = ["act", "dve"][c % 2]
                        if eng == "act":
                            fn = lambda e, tv=tv, c=c, half=half: e.activation(out=Wout[:, half * 4 + c, :], in_=tv[:, c, :], func=AF.Copy)
                        else:
                            fn = lambda e, tv=tv, c=c, half=half: e.tensor_copy(out=Wout[:, half * 4 + c, :], in_=tv[:, c, :])
                        k.op(eng, fn, reads=[R_t], writes=[R_w])
                k.op("pool", lambda e: e.memset(Pbd[:], 0.0), writes=[R_w])
                for a in range(2):
                    k.op("pool", lambda e, a=a: e.tensor_copy(out=Pbd[0:64, a, 0:64], in_=pw_st[0:64, a, :]), reads=[R_small], writes=[R_w])
                    k.op("pool", lambda e, a=a: e.tensor_copy(out=Pbd[64:128, a, 64:128], in_=pw_st[64:128, a, :]), reads=[R_small], writes=[R_w])
                k.op("dve", lambda e: e.tensor_copy(out=wf_b[:], in_=wf_st[:]), reads=[R_small], writes=[R_wfb])
                k.op("pool", lambda e: e.memset(Mbd[:], 0.0), writes=[R_mbd])
                for a in range(2):
                    for cs in range(2):
                        blk = (a * 2 + cs) * 64
                        lh = c64bd if cs == 0 else s64bd
                        k.op("pe", lambda e, a=a, blk=blk, lh=lh: e.matmul(ps[0][:, blk:blk + 64], lhsT=lh[:], rhs=wf_b[:, a, :], start=True, stop=True),
                             reads=[R_c, R_wfb], writes=[R_ps[0]])
                for a in range(2):
                    for cs in range(2):
                        blk = (a * 2 + cs) * 64
                        k.op("dve", lambda e, a=a, cs=cs, blk=blk: e.tensor_copy(out=Mbd[0:64, a, cs, 0:64], in_=ps[0][0:64, blk:blk + 64]),
                             reads=[R_ps[0]], writes=[R_mbd])
                        k.op("dve", lambda e, a=a, cs=cs, blk=blk: e.tensor_copy(out=Mbd[64:128, a, cs, 64:128], in_=ps[0][64:128, blk:blk + 64]),
                             reads=[R_ps[0]], writes=[R_mbd])
                for a in range(2):
                    pt = ps[1 + a][:].bitcast(BF16)
                    for c in range(8):
                        k.op("pe", lambda e, a=a, c=c, pt=pt: e.transpose(out=pt[:, c * 128:(c + 1) * 128], in_=Wf_bf[:, c, a * 128:(a + 1) * 128], identity=ident[:]),
                             reads=[R_wfbf, R_c], writes=[R_ps[1 + a]])
                    k.op("act", lambda e, a=a, pt=pt: e.activation(out=WfT[:, a, :], in_=pt, func=AF.Copy), reads=[R_ps[1 + a]], writes=[R_wft])
                for c in range(8):
                    pb = 3 + (c % 2)
                    for cs in range(2):
                        for a in range(2):
                            col = cs * 256 + a * 128
                            k.op("pe", lambda e, c=c, cs=cs, a=a, col=col, pb=pb: e.matmul(ps[pb][:, col:col + 128], lhsT=WfT[:, a, c * 128:(c + 1) * 128],
                                                                                         rhs=Mbd[:, a, cs, :], start=True, stop=True),
                                 reads=[R_wft, R_mbd], writes=[R_ps[pb]])
                    k.op("dve", lambda e, c=c, pb=pb: e.tensor_copy(out=Win_tm[:, c, 0:512], in_=ps[pb][:, :]), reads=[R_ps[pb]], writes=[R_w])
            k.end_phase()

        def phase_A(l, s, h_src):
            with contextlib.ExitStack() as ls:
                Xp = sb("Xp", [128, 2, L + 16], BF16, ls)
                spg = sb("spg", [128, 2, L], BF16, ls)
                ls2 = ls.enter_context(contextlib.ExitStack())
                hring = Ring(k, nc, ls2, "ht", [128, 1024], F32, 4)
                xn_ring = Ring(k, nc, ls2, "xn", [128, 1024], BF16, 4, with_dsem=False)
                cn_ring = Ring(k, nc, ls2, "cn", [128, 640], BF16, 4, with_dsem=False)
                xnT_ring = [(sb(f"xnT{i}", [128, 8, 512], BF16, ls2), [Res() for _ in range(4)]) for i in range(2)]
                cnT_ring = [(sb(f"cnT{i}", [128, 5, 512], BF16, ls2), [Res() for _ in range(4)]) for i in range(2)]
                pcs_st = Ring(k, nc, ls2, "pcs_st", [128, 4, 512], BF16, 1)
                v_st = Ring(k, nc, ls2, "v_st", [128, 4, 512], BF16, 1)
                gf_st = Ring(k, nc, ls2, "gf_st", [128, 2, 512], BF16, 1)
                ga_st = Ring(k, nc, ls2, "ga_st", [128, 4, 512], BF16, 1)
                q_st = Ring(k, nc, ls2, "q_st", [128, 6, 512], BF16, 1)
                k_st = Ring(k, nc, ls2, "k_st", [128, 4, 512], BF16, 1)
                kr_st = Ring(k, nc, ls2, "kr_st", [64, 512], BF16, 1)
                cos_r = Ring(k, nc, ls2, "cos_r", [128, 512], F32, 2)
                sin_r = Ring(k, nc, ls2, "sin_r", [128, 512], F32, 2)
                t1_r = Ring(k, nc, ls2, "t1_r", [128, 512], F32, 2, with_dsem=False)
                t2_r = Ring(k, nc, ls2, "t2_r", [128, 512], F32, 2, with_dsem=False)
                junk_a = sb("junk_a", [128, 1024], BF16, ls2)
                ss_r = Ring(k, nc, ls2, "ss_r", [128, 4], F32, 8, with_dsem=False)
                R_xp = [Res("xp0"), Res("xp1")]
                R_spg = [Res("spg0"), Res("spg1")]
                PT = [0, 1]
                TMB = [2, 3, 4]
                VB = 5
                FMB = [2, 3, 6, 7]
                fmi = [0]
                pti = [0]

                k.op("pool", lambda e: e.memset(Xp[:, :, 0:8], 0.0), writes=R_xp)
                k.op("pool", lambda e: e.memset(Xp[:, :, L + 8:L + 16], 0.0), writes=R_xp)

                TMSETS = [[2, 3, 4], [5, 6, 7]]
                tmi = [0]
                st_state = {}

                def S1a(u):
                    t0, n = STILES[u]
                    xnT, R_xnT = xnT_ring[u % 2]
                    stt = {"tiles": []}
                    st_state[u] = stt
                    for j, (r0, nr) in enumerate(tiles_of(t0, n)):
                        ht, R_ht, d_ht = hring.next()
                        k.op("sp", lambda e: e.dma_start(out=ht[0:nr, :], in_=h_src[s, t0 + r0:t0 + r0 + nr, :]), writes=[R_ht], dma=d_ht)
                        ss, R_ss, _ = ss_r.next()
                        k.op("act", lambda e: e.activation(out=junk_a[0:nr, :], in_=ht[0:nr, :], func=AF.Square, scale=float(DM ** -0.5), accum_out=ss[0:nr, 0:1]),
                             reads=[R_ht], writes=[R_ss])
                        k.op("act", lambda e: e.activation(out=ss[0:nr, 0:1], in_=ss[0:nr, 0:1], func=AF.Sqrt, bias=eps_t[0:nr, 0:1]),
                             reads=[R_ss], writes=[R_ss])
                        k.op("dve", lambda e: e.reciprocal(out=ss[0:nr, 0:1], in_=ss[0:nr, 0:1]), reads=[R_ss], writes=[R_ss])
                        xn, R_xn, _ = xn_ring.next()
                        k.op("dve", lambda e: e.tensor_scalar(out=xn[0:nr, :], in0=ht[0:nr, :], scalar1=ss[0:nr, 0:1], scalar2=None, op0=ALU.mult),
                             reads=[R_ht, R_ss], writes=[R_xn])
                        stt["tiles"].append({"ss": ss, "R_ss": R_ss, "xn": (xn, R_xn)})

                def S1b(u):
                    t0, n = STILES[u]
                    xnT, R_xnT = xnT_ring[u % 2]
                    stt = st_state[u]
                    for j, (r0, nr) in enumerate(tiles_of(t0, n)):
                        xn, R_xn = stt["tiles"][j]["xn"]
                        pb = PT[pti[0] % 2]
                        pti[0] += 1
                        ptv = ps[pb][:].bitcast(BF16)
                        for c in range(8):
                            k.op("pe", lambda e: e.transpose(out=ptv[:, c * 128:c * 128 + nr], in_=xn[0:nr, c * 128:(c + 1) * 128], identity=ident[0:nr, 0:nr]),
                                 reads=[R_xn, R_c], writes=[R_ps[pb]])
                        k.op("act", lambda e: e.activation(out=xnT[:, :, r0:r0 + nr], in_=ptv.rearrange("p (c t) -> p c t", c=8)[:, :, 0:nr], func=AF.Copy),
                             reads=[R_ps[pb]], writes=[R_xnT[j]])

                def S2(u):
                    t0, n = STILES[u]
                    xnT, R_xnT = xnT_ring[u % 2]
                    stt = st_state[u]
                    pcs_t, R_pcs, d_pcs = pcs_st.next()
                    stt["pcs"] = (pcs_t, R_pcs, d_pcs)
                    for j, (r0, nr) in enumerate(tiles_of(t0, n)):
                        ss, R_ss = stt["tiles"][j]["ss"], stt["tiles"][j]["R_ss"]
                        TMB = TMSETS[tmi[0] % 2]
                        tmi[0] += 1
                        tm_specs = [(TMB[0], 0, 512), (TMB[1], 512, 384), (TMB[2], 896, 256)]
                        for (bk, c0, ncol) in tm_specs:
                            for c in range(8):
                                k.op("pe", lambda e: e.matmul(ps[bk][0:nr, 0:ncol], lhsT=xnT[:, c, r0:r0 + nr], rhs=Win_tm[:, c, c0:c0 + ncol], start=(c == 0), stop=(c == 7)),
                                     reads=[R_xnT[j], R_w], writes=[R_ps[bk]])
                        k.op("act", lambda e: e.activation(out=pcs_t[0:nr, j, :], in_=ps[TMB[0]][0:nr, :], func=AF.Copy),
                             reads=[R_ps[TMB[0]]], writes=[R_pcs])
                        k.op("act", lambda e: e.activation(out=junk_a[0:nr, 0:384], in_=ps[TMB[1]][0:nr, 0:384], func=AF.Square, scale=float(384 ** -0.5), accum_out=ss[0:nr, 1:2]),
                             reads=[R_ps[TMB[1]]], writes=[R_ss])
                        k.op("act", lambda e: e.activation(out=junk_a[0:nr, 0:256], in_=ps[TMB[2]][0:nr, 0:256], func=AF.Square, scale=float(256 ** -0.5), accum_out=ss[0:nr, 2:3]),
                             reads=[R_ps[TMB[2]]], writes=[R_ss])
                        k.op("act", lambda e: e.activation(out=ss[0:nr, 1:3], in_=ss[0:nr, 1:3], func=AF.Sqrt, bias=eps_t[0:nr, 0:1]),
                             reads=[R_ss], writes=[R_ss])
                        k.op("dve", lambda e: e.reciprocal(out=ss[0:nr, 1:3], in_=ss[0:nr, 1:3]), reads=[R_ss], writes=[R_ss])
                        cn, R_cn, _ = cn_ring.next()
                        k.op("dve", lambda e: e.tensor_scalar(out=cn[0:nr, 0:384], in0=ps[TMB[1]][0:nr, 0:384], scalar1=ss[0:nr, 1:2], scalar2=None, op0=ALU.mult),
                             reads=[R_ps[TMB[1]], R_ss], writes=[R_cn])
                        k.op("dve", lambda e: e.tensor_scalar(out=cn[0:nr, 384:640], in0=ps[TMB[2]][0:nr, 0:256], scalar1=ss[0:nr, 2:3], scalar2=None, op0=ALU.mult),
                             reads=[R_ps[TMB[2]], R_ss], writes=[R_cn])
                        stt["tiles"][j]["cn"] = (cn, R_cn)
                    if n == 512:
                        k.op("sp", lambda e: e.dma_start(out=pcs_s[s, t0:t0 + 512, :].rearrange("(j p) n -> p j n", p=128), in_=pcs_t[:]), reads=[R_pcs], dma=d_pcs)
                    else:
                        k.op("sp", lambda e: e.dma_start(out=pcs_s[s, t0:t0 + n, :], in_=pcs_t[0:n, 0, :]), reads=[R_pcs], dma=d_pcs)

                def S3t(u):
                    t0, n = STILES[u]
                    cnT, R_cnT = cnT_ring[u % 2]
                    stt = st_state[u]
                    v_t, R_v, d_v = v_st.next()
                    for j, (r0, nr) in enumerate(tiles_of(t0, n)):
                        cn, R_cn = stt["tiles"][j]["cn"]
                        pb = PT[pti[0] % 2]
                        pti[0] += 1
                        ptv = ps[pb][:].bitcast(BF16)
                        for c in range(5):
                            k.op("pe", lambda e: e.transpose(out=ptv[:, c * 128:c * 128 + nr], in_=cn[0:nr, c * 128:(c + 1) * 128], identity=ident[0:nr, 0:nr]),
                                 reads=[R_cn, R_c], writes=[R_ps[pb]])
                        k.op("dve", lambda e: e.tensor_copy(out=cnT[:, :, r0:r0 + nr], in_=ptv[:, 0:640].rearrange("p (c t) -> p c t", c=5)[:, :, 0:nr]),
                             reads=[R_ps[pb]], writes=[R_cnT[j]])
                    stt["v"] = (v_t, R_v, d_v)

                def S3v(u):
                    t0, n = STILES[u]
                    cnT, R_cnT = cnT_ring[u % 2]
                    stt = st_state[u]
                    v_t, R_v, d_v = stt["v"]
                    for j, (r0, nr) in enumerate(tiles_of(t0, n)):
                        vb = [5, 4][j % 2]
                        for c in range(2):
                            k.op("pe", lambda e: e.matmul(ps[vb][0:nr, :], lhsT=cnT[:, 3 + c, r0:r0 + nr], rhs=Wukv[:, c, 512:1024], start=(c == 0), stop=(c == 1)),
                                 reads=[R_cnT[j], R_w], writes=[R_ps[vb]])
                        k.op("dve", lambda e: e.tensor_copy(out=v_t[0:nr, j, :], in_=ps[vb][0:nr, :]), reads=[R_ps[vb]], writes=[R_v])
                    if n == 512:
                        k.op("sp", lambda e: e.dma_start(out=v_s[s, t0:t0 + 512, :].rearrange("(j p) n -> p j n", p=128), in_=v_t[:]), reads=[R_v], dma=d_v)
                    else:
                        k.op("sp", lambda e: e.dma_start(out=v_s[s, t0:t0 + n, :], in_=v_t[0:n, 0, :]), reads=[R_v], dma=d_v)

                def load_tables(u):
                    t0, n = STILES[u]
                    cos_t, R_cos, d_cos = cos_r.next()
                    sin_t, R_sin, d_sin = sin_r.next()
                    k.op("sp", lambda e: e.dma_start(out=cos_t[:, 0:n], in_=cos4_d[:, t0:t0 + n]), writes=[R_cos], dma=d_cos)
                    k.op("sp", lambda e: e.dma_start(out=sin_t[:, 0:n], in_=sin4_d[:, t0:t0 + n]), writes=[R_sin], dma=d_sin)
                    st_state[u]["tab"] = (cos_t, R_cos, sin_t, R_sin)

                def FM(u):
                    t0, n = STILES[u]
                    xnT, R_xnT = xnT_ring[u % 2]
                    ntl = len(tiles_of(t0, n))
                    cos_t, R_cos, sin_t, R_sin = st_state[u]["tab"]

                    def fm_group(col0, m):
                        bk = FMB[fmi[0] % len(FMB)]
                        fmi[0] += 1
                        for c in range(8):
                            k.op("pe", lambda e: e.matmul(ps[bk][0:m, 0:n], lhsT=Win_fm[:, c, col0:col0 + m], rhs=xnT[:, c, 0:n], start=(c == 0), stop=(c == 7)),
                                 reads=R_xnT[0:ntl] + [R_w], writes=[R_ps[bk]])
                        return bk

                    gf_t, R_gf, d_gf = gf_st.next()
                    ga_t, R_ga, d_ga = ga_st.next()
                    for oc in range(2):
                        bk = fm_group(oc * 128, 128)
                        k.op("act", lambda e: e.activation(out=gf_t[:, oc, 0:n], in_=ps[bk][:, 0:n], func=AF.Silu), reads=[R_ps[bk]], writes=[R_gf])
                    k.op("sp", lambda e: e.dma_start(out=gTf_s[s, :, t0:t0 + n].rearrange("(c p) t -> p c t", p=128), in_=gf_t[:, :, 0:n]), reads=[R_gf], dma=d_gf)
                    for oc in range(2):
                        bk = fm_group(256 + oc * 128, 128)
                        k.op("act", lambda e: e.activation(out=spg[:, oc, t0:t0 + n], in_=ps[bk][:, 0:n], func=AF.Silu), reads=[R_ps[bk]], writes=[R_spg[oc]])
                    for oc in range(4):
                        bk = fm_group(512 + oc * 128, 128)
                        k.op("act", lambda e: e.activation(out=ga_t[:, oc, 0:n], in_=ps[bk][:, 0:n], func=AF.Silu), reads=[R_ps[bk]], writes=[R_ga])
                    k.op("sp", lambda e: e.dma_start(out=gTa_s[s, :, t0:t0 + n].rearrange("(c p) t -> p c t", p=128), in_=ga_t[:, :, 0:n]), reads=[R_ga], dma=d_ga)
                    for oc in range(2):
                        bk = fm_group(1024 + oc * 128, 128)
                        k.op("dve", lambda e: e.tensor_copy(out=Xp[:, oc, 8 + t0:8 + t0 + n], in_=ps[bk][:, 0:n]), reads=[R_ps[bk]], writes=[R_xp[oc]])
                    bkA = fm_group(1280, 64)
                    bkB = fm_group(1344, 64)
                    t1, R_t1, _ = t1_r.next()
                    t2, R_t2, _ = t2_r.next()
                    kr_t, R_kr, d_kr = kr_st.next()
                    k.op("dve", lambda e: e.tensor_tensor(out=t1[0:64, 0:n], in0=ps[bkA][0:64, 0:n], in1=cos_t[0:64, 0:n], op=ALU.mult),
                         reads=[R_ps[bkA], R_cos], writes=[R_t1])
                    k.op("dve", lambda e: e.tensor_tensor(out=t2[0:64, 0:n], in0=ps[bkB][0:64, 0:n], in1=sin_t[0:64, 0:n], op=ALU.mult),
                         reads=[R_ps[bkB], R_sin], writes=[R_t2])
                    k.op("pool", lambda e: e.tensor_tensor(out=kr_t[:, 0:n], in0=t1[0:64, 0:n], in1=t2[0:64, 0:n], op=ALU.add),
                         reads=[R_t1, R_t2], writes=[R_kr])
                    k.op("sp", lambda e: e.dma_start(out=kT_s[s, 512:576, t0:t0 + n], in_=kr_t[:, 0:n]), reads=[R_kr], dma=d_kr)

                def QKG(u):
                    t0, n = STILES[u]
                    cnT, R_cnT = cnT_ring[u % 2]
                    ntl = len(tiles_of(t0, n))
                    cos_t, R_cos, sin_t, R_sin = st_state[u]["tab"]

                    def cn_group(Wt, nk, kc0, col0):
                        bk = FMB[fmi[0] % len(FMB)]
                        fmi[0] += 1
                        for c in range(nk):
                            k.op("pe", lambda e: e.matmul(ps[bk][:, 0:n], lhsT=Wt[:, c, col0:col0 + 128], rhs=cnT[:, kc0 + c, 0:n], start=(c == 0), stop=(c == nk - 1)),
                                 reads=R_cnT[0:ntl] + [R_w], writes=[R_ps[bk]])
                        return bk

                    q_t, R_q, d_q = q_st.next()
                    for oc in range(4):
                        bk = cn_group(Wuq, 3, 0, oc * 128)
                        if oc % 2 == 0:
                            k.op("act", lambda e: e.activation(out=q_t[:, oc, 0:n], in_=ps[bk][:, 0:n], func=AF.Copy), reads=[R_ps[bk]], writes=[R_q])
                        else:
                            k.op("dve", lambda e: e.tensor_copy(out=q_t[:, oc, 0:n], in_=ps[bk][:, 0:n]), reads=[R_ps[bk]], writes=[R_q])
                    for a in range(2):
                        bkA = cn_group(Wuq, 3, 0, 512 + a * 128)
                        bkB = cn_group(Wuq, 3, 0, 768 + a * 128)
                        t1, R_t1, _ = t1_r.next()
                        t2, R_t2, _ = t2_r.next()
                        k.op("dve", lambda e: e.tensor_tensor(out=t1[:, 0:n], in0=ps[bkA][:, 0:n], in1=cos_t[:, 0:n], op=ALU.mult),
                             reads=[R_ps[bkA], R_cos], writes=[R_t1])
                        k.op("dve", lambda e: e.tensor_tensor(out=t2[:, 0:n], in0=ps[bkB][:, 0:n], in1=sin_t[:, 0:n], op=ALU.mult),
                             reads=[R_ps[bkB], R_sin], writes=[R_t2])
                        k.op("pool", lambda e: e.tensor_tensor(out=q_t[:, 4 + a, 0:n], in0=t1[:, 0:n], in1=t2[:, 0:n], op=ALU.add),
                             reads=[R_t1, R_t2], writes=[R_q])
                    k.op("sp", lambda e: e.dma_start(out=qT_s[s, :, t0:t0 + n].rearrange("(c p) t -> p c t", p=128), in_=q_t[:, :, 0:n]), reads=[R_q], dma=d_q)
                    k_t, R_k, d_k = k_st.next()
                    for oc in range(4):
                        bk = cn_group(Wukv, 2, 3, oc * 128)
                        if oc % 2 == 0:
                            k.op("act", lambda e: e.activation(out=k_t[:, oc, 0:n], in_=ps[bk][:, 0:n], func=AF.Copy), reads=[R_ps[bk]], writes=[R_k])
                        else:
                            k.op("dve", lambda e: e.tensor_copy(out=k_t[:, oc, 0:n], in_=ps[bk][:, 0:n]), reads=[R_ps[bk]], writes=[R_k])
                    k.op("sp", lambda e: e.dma_start(out=kT_s[s, 0:512, t0:t0 + n].rearrange("(c p) t -> p c t", p=128), in_=k_t[:, :, 0:n]), reads=[R_k], dma=d_k)

                NU = len(STILES)
                S1a(0)
                S1b(0)
                for u in range(NU):
                    load_tables(u)
                    S2(u)
                    if u + 1 < NU:
                        S1a(u + 1)
                    FM(u)
                    S3t(u)
                    if u + 1 < NU:
                        S1b(u + 1)
                    S3v(u)
                    QKG(u)

                k.barrier()
                ls2.close()
                CB = 1028
                sA = Ring(k, nc, ls, "poolA", [128, CB + 16], F32, 2, with_dsem=False)
                sB = Ring(k, nc, ls, "poolB", [128, CB + 16], F32, 2, with_dsem=False)
                pm_st = Ring(k, nc, ls, "pm_st", [128, CB], BF16, 2)
                ym = Ring(k, nc, ls, "ym", [128, CB], BF16, 2, with_dsem=False)
                for a in range(2):
                    for b in range(4):
                        o = b * CB
                        A_, R_A, _ = sA.next()
                        B_, R_B, _ = sB.next()
                        X = Xp[:, a, o:o + CB + 16]
                        W_ = CB + 16
                        eng1 = "pool" if (b % 2 == 0) else "dve"
                        eng2 = "dve"
                        k.op(eng1, lambda e, A_=A_, X=X, W_=W_: e.tensor_tensor(out=A_[:, 1:W_], in0=X[:, 0:W_ - 1], in1=X[:, 1:W_], op=ALU.add), reads=[R_xp[a]], writes=[R_A])
                        k.op(eng1, lambda e, A_=A_, B_=B_, W_=W_: e.tensor_tensor(out=B_[64:128, 2:W_ - 1] if a == 0 else B_[:, 2:W_ - 1],
                                                                               in0=A_[64:128, 1:W_ - 2] if a == 0 else A_[:, 1:W_ - 2],
                                                                               in1=A_[64:128, 3:W_] if a == 0 else A_[:, 3:W_], op=ALU.add), reads=[R_A], writes=[R_B])
                        if a == 0:
                            k.op(eng1, lambda e, A_=A_, B_=B_, W_=W_: e.tensor_copy(out=B_[0:64, 2:W_ - 1], in_=A_[0:64, 2:W_ - 1]), reads=[R_A], writes=[R_B])
                            Sfin, R_S = B_, R_B
                        else:
                            k.op(eng1, lambda e, A_=A_, B_=B_, W_=W_: e.tensor_tensor(out=A_[:, 4:W_ - 3], in0=B_[:, 2:W_ - 5], in1=B_[:, 6:W_ - 1], op=ALU.add), reads=[R_B], writes=[R_A])
                            k.op(eng1, lambda e, A_=A_, B_=B_, W_=W_: e.tensor_tensor(out=B_[64:128, 8:W_ - 7], in0=A_[64:128, 4:W_ - 11], in1=A_[64:128, 12:W_ - 3], op=ALU.add), reads=[R_A], writes=[R_B])
                            k.op(eng1, lambda e, A_=A_, B_=B_, W_=W_: e.tensor_copy(out=B_[0:64, 8:W_ - 7], in_=A_[0:64, 8:W_ - 7]), reads=[R_A], writes=[R_B])
                            Sfin, R_S = B_, R_B
                        y_, R_y, _ = ym.next()
                        k.op(eng2, lambda e, Sfin=Sfin, X=X, y_=y_, a=a: e.scalar_tensor_tensor(out=y_[:, 0:CB], in0=Sfin[:, 8:8 + CB], scalar=pinvw[:, a:a + 1], in1=X[:, 8:8 + CB],
                                                                                              op0=ALU.mult, op1=ALU.subtract), reads=[R_S, R_xp[a], R_c], writes=[R_y])
                        if b == 0:
                            k.op(eng2, lambda e, Sfin=Sfin, a=a: e.tensor_tensor(out=Sfin[:, 8:16], in0=Sfin[:, 8:16], in1=pelo[:, a, :], op=ALU.mult), reads=[R_S, R_c, R_y], writes=[R_S])
                            k.op(eng2, lambda e, Sfin=Sfin, X=X, y_=y_: e.tensor_tensor(out=y_[:, 0:8], in0=Sfin[:, 8:16], in1=X[:, 8:16], op=ALU.subtract), reads=[R_S, R_xp[a]], writes=[R_y])
                        if b == 3:
                            k.op(eng2, lambda e, Sfin=Sfin, a=a: e.tensor_tensor(out=Sfin[:, CB:CB + 8], in0=Sfin[:, CB:CB + 8], in1=pehi[:, a, :], op=ALU.mult), reads=[R_S, R_c, R_y], writes=[R_S])
                            k.op(eng2, lambda e, Sfin=Sfin, X=X, y_=y_: e.tensor_tensor(out=y_[:, CB - 8:CB], in0=Sfin[:, CB:CB + 8], in1=X[:, CB:CB + 8], op=ALU.subtract), reads=[R_S, R_xp[a]], writes=[R_y])
                        pm_t, R_pm, d_pm = pm_st.next()
                        for (c0, cw) in [(0, 512), (512, 512), (1024, CB - 1024)]:
                            bk = FMB[fmi[0] % len(FMB)]
                            fmi[0] += 1
                            k.op("pe", lambda e, bk=bk, y_=y_, c0=c0, cw=cw, a=a: e.matmul(ps[bk][:, 0:cw], lhsT=Pbd[:, a, :], rhs=y_[:, c0:c0 + cw], start=True, stop=True),
                                 reads=[R_y, R_w], writes=[R_ps[bk]])
                            k.op("dve", lambda e, bk=bk, c0=c0, cw=cw, a=a, pm_t=pm_t, o=o: e.scalar_tensor_tensor(out=pm_t[:, c0:c0 + cw], in0=ps[bk][:, 0:cw], scalar=psc[:, a:a + 1],
                                                                                                             in1=spg[:, a, o + c0:o + c0 + cw], op0=ALU.mult, op1=ALU.mult),
                                 reads=[R_ps[bk], R_spg[a], R_w], writes=[R_pm])
                        k.op("sp", lambda e, pm_t=pm_t, a=a, o=o: e.dma_start(out=mixT_s[s, 256 + a * 128:256 + (a + 1) * 128, o:o + CB], in_=pm_t[:, :]), reads=[R_pm], dma=d_pm)
            k.end_phase()

        def phase_B(l, s):
            with contextlib.ExitStack() as ls:
                kT = sb("kT", [128, 4, L], BF16, ls)
                krT = sb("krT", [128, L], BF16, ls)
                V = sb("V", [128, NT, 512], BF16, ls)
                R_kv = Res("kv")
                dkv = k.dsem(f"d_kv{l}_{s}")
                k.op("sp", lambda e: e.dma_start(out=kT[:], in_=kT_s[s, 0:512, :].rearrange("(c p) t -> p c t", p=128)), writes=[R_kv], dma=dkv)
                k.op("sp", lambda e: e.dma_start(out=krT[0:64, :], in_=kT_s[s, 512:576, :]), writes=[R_kv], dma=dkv)
                k.op("pool", lambda e: e.memset(krT[64:128, :], 0.0), writes=[R_kv])
                k.op("sp", lambda e: e.dma_start(out=V[:, 0:32, :], in_=v_s[s, 0:4096, :].rearrange("(j p) n -> p j n", p=128)), writes=[R_kv], dma=dkv)
                k.op("sp", lambda e: e.dma_start(out=V[0:16, 32, :], in_=v_s[s, 4096:L, :]), writes=[R_kv], dma=dkv)
                q_r = Ring(k, nc, ls, "qb", [128, 4, 512], BF16, 2)
                qr_r = Ring(k, nc, ls, "qrb", [128, 4, 512], BF16, 2)
                for (qt_, Rq_, _) in qr_r.items:
                    k.op("pool", lambda e: e.memset(qt_[64:128, :, :], 0.0), writes=[Rq_])
                ga_r = Ring(k, nc, ls, "gab", [128, 4, 512], BF16, 3)
                mx_r = Ring(k, nc, ls, "mxb", [128, 4, 512], BF16, 2)
                pt_r = Ring(k, nc, ls, "ptb", [128, 2, 512], BF16, 3, with_dsem=False)
                acc0_r = Ring(k, nc, ls, "acc0", [128, 2, 512], F32, 2, with_dsem=False)
                rinv_r = Ring(k, nc, ls, "rinv", [128, 512], F32, 1, with_dsem=False)
                o1_r = Ring(k, nc, ls, "o1", [128, 512], F32, 1, with_dsem=False)
                SPAIR = [0, 1]
                OB = [4, 5]
                SUMBS = [6, 7]
                spi = [0]
                obi = [0]
                NPAIR = 17
                def b_loads(u):
                    t0, n = STILES[u]
                    q_t, R_q, d_q = q_r.next()
                    qr_t, R_qr, d_qr = qr_r.next()
                    ga_t, R_ga, d_ga = ga_r.next()
                    k.op("sp", lambda e: e.dma_start(out=q_t[:, :, 0:n], in_=qT_s[s, 0:512, t0:t0 + n].rearrange("(c p) t -> p c t", p=128)), writes=[R_q], dma=d_q)
                    k.op("sp", lambda e: e.dma_start(out=qr_t[0:64, :, 0:n], in_=qT_s[s, 512:768, t0:t0 + n].rearrange("(c p) t -> p c t", p=64)), writes=[R_qr], dma=d_qr)
                    k.op("sp", lambda e: e.dma_start(out=ga_t[:, :, 0:n], in_=gTa_s[s, :, t0:t0 + n].rearrange("(c p) t -> p c t", p=128)), writes=[R_ga], dma=d_ga)
                    return (q_t, R_q, qr_t, R_qr, ga_t, R_ga)

                pending_fin = [None]
                nxt = b_loads(0)
                for u, (t0, n) in enumerate(STILES):
                    q_t, R_q, qr_t, R_qr, ga_t, R_ga = nxt
                    if u + 1 < len(STILES):
                        nxt = b_loads(u + 1)
                    mx_t, R_mx, d_mx = mx_r.next()
                    for h in range(4):
                        ob = OB[obi[0] % 2]
                        obi[0] += 1
                        acc0, R_a0, _ = acc0_r.next()
                        sumb = SUMBS[(obi[0] - 1) % 2]

                        def qk(p):
                            bp = SPAIR[spi[0] % 2]
                            spi[0] += 1
                            kts = [2 * p, 2 * p + 1] if p < NPAIR - 1 else [2 * p]
                            for i, kt in enumerate(kts):
                                kn = 128 if kt < 32 else 16
                                bk = 2 * bp + i
                                k.op("pe", lambda e: e.matmul(ps[bk][0:kn, 0:n], lhsT=kT[:, h, kt * 128:kt * 128 + kn], rhs=q_t[:, h, 0:n], start=True, stop=False),
                                     reads=[R_kv, R_q], writes=[R_ps[bk]])
                                k.op("pe", lambda e: e.matmul(ps[bk][0:kn, 0:n], lhsT=krT[:, kt * 128:kt * 128 + kn], rhs=qr_t[:, h, 0:n], start=False, stop=True),
                                     reads=[R_kv, R_qr], writes=[R_ps[bk]])
                            p_t, R_p, _ = pt_r.next()
                            nh = len(kts)
                            kn = 128 if kts[-1] < 32 else 16
                            src = psd[bp][0:kn, :].rearrange("p (b m) -> p b m", b=2)[:, 0:nh, 0:n]
                            k.op("act", lambda e: e.activation(out=p_t[0:kn, 0:nh, 0:n], in_=src, func=AF.Exp), reads=[R_ps[2 * bp], R_ps[2 * bp + 1]], writes=[R_p])
                            return (p, kts, p_t, R_p)

                        def pv(item):
                            p, kts, p_t, R_p = item
                            for i, kt in enumerate(kts):
                                kn = 128 if kt < 32 else 16
                                k.op("pe", lambda e: e.matmul(ps[ob][:, 0:n], lhsT=V[0:kn, kt, h * 128:(h + 1) * 128], rhs=p_t[0:kn, i, 0:n], start=(kt == 0), stop=(kt == NT - 1)),
                                     reads=[R_kv, R_p], writes=[R_ps[ob]])
                            nh = len(kts)
                            kn = 128 if kts[-1] < 32 else 16
                            if p % 3 == 2:
                                for i, kt in enumerate(kts):
                                    k.op("pe", lambda e: e.matmul(ps[sumb][:, 0:n], lhsT=ones_b[0:kn, :], rhs=p_t[0:kn, i, 0:n], start=(p == 2 and i == 0), stop=False),
                                         reads=[R_c, R_p], writes=[R_ps[sumb]])
                            elif p == 0:
                                k.op("dve", lambda e: e.tensor_copy(out=acc0[:, :, 0:n], in_=p_t[:, :, 0:n]), reads=[R_p], writes=[R_a0])
                            else:
                                k.op("dve", lambda e: e.tensor_tensor(out=acc0[0:kn, 0:nh, 0:n], in0=acc0[0:kn, 0:nh, 0:n], in1=p_t[0:kn, 0:nh, 0:n], op=ALU.add), reads=[R_p, R_a0], writes=[R_a0])

                        def make_fin(acc0=acc0, R_a0=R_a0, sumb=sumb, ob=ob, n=n, h=h, mx_t=mx_t, R_mx=R_mx, d_mx=d_mx, ga_t=ga_t, R_ga=R_ga, t0=t0):
                            def fin():
                                k.op("dve", lambda e: e.tensor_tensor(out=acc0[:, 0, 0:n], in0=acc0[:, 0, 0:n], in1=acc0[:, 1, 0:n], op=ALU.add), reads=[R_a0], writes=[R_a0])
                                k.op("pe", lambda e: e.matmul(ps[sumb][:, 0:n], lhsT=ones_f[:], rhs=acc0[:, 0, 0:n], start=False, stop=True), reads=[R_a0, R_c], writes=[R_ps[sumb]])
                                rinv, R_ri, _ = rinv_r.next()
                                k.op("dve", lambda e: e.reciprocal(out=rinv[:, 0:n], in_=ps[sumb][:, 0:n]), reads=[R_ps[sumb]], writes=[R_ri])
                                o1, R_o1, _ = o1_r.next()
                                k.op("dve", lambda e: e.tensor_tensor(out=o1[:, 0:n], in0=ps[ob][:, 0:n], in1=rinv[:, 0:n], op=ALU.mult), reads=[R_ps[ob], R_ri], writes=[R_o1])
                                k.op("pool", lambda e: e.tensor_tensor(out=mx_t[:, h, 0:n], in0=o1[:, 0:n], in1=ga_t[:, h, 0:n], op=ALU.mult), reads=[R_o1, R_ga], writes=[R_mx])
                                if h == 3:
                                    k.op("sp", lambda e: e.dma_start(out=mixT_s[s, 512:1024, t0:t0 + n].rearrange("(c p) t -> p c t", p=128), in_=mx_t[:, :, 0:n]), reads=[R_mx], dma=d_mx)
                            return fin

                        prev = None
                        for p in range(NPAIR):
                            cur = qk(p)
                            if prev is not None:
                                pv(prev)
                            prev = cur
                            if p == 2 and pending_fin[0] is not None:
                                pending_fin[0]()
                                pending_fin[0] = None
                        pv(prev)
                        if pending_fin[0] is not None:
                            pending_fin[0]()
                        pending_fin[0] = make_fin()
                if pending_fin[0] is not None:
                    pending_fin[0]()
            k.end_phase()

        def phase_C(l):
            with contextlib.ExitStack() as ls:
                Pc = sb("Pc", [128, NS, NT, 512], BF16, ls)
                R_pc = Res("pc")
                dpc = k.dsem(f"d_pc{l}")
                for s in range(NS):
                    k.op("sp", lambda e: e.dma_start(out=Pc[:, s, 0:32, :], in_=pcs_s[s, 0:4096, :].rearrange("(j p) n -> p j n", p=128)), writes=[R_pc], dma=dpc)
                    k.op("sp", lambda e: e.dma_start(out=Pc[0:16, s, 32, :], in_=pcs_s[s, 4096:L, :]), writes=[R_pc], dma=dpc)
                GC = 8
                c_r = Ring(k, nc, ls, "dC", [128, GC, 512], BF16, 3)
                s_r = Ring(k, nc, ls, "dS", [128, GC, 512], BF16, 3)
                gfd_r = Ring(k, nc, ls, "gfd", [128, 512], BF16, 2)
                gfm_r = Ring(k, nc, ls, "gfm", [128, 512], BF16, 2)
                mfd_r = Ring(k, nc, ls, "mfd", [128, 512], BF16, 2)
                mfm_r = Ring(k, nc, ls, "mfm", [128, 512], BF16, 2)
                bs_r = Ring(k, nc, ls, "bsr", [128, 512], F32, 2, with_dsem=False)
                u_r = Ring(k, nc, ls, "ur", [128, 512], F32, 2, with_dsem=False)
                t_r = Ring(k, nc, ls, "tr", [128, 512], F32, 2, with_dsem=False)
                groups = [(0, 8), (8, 8), (16, 8), (24, 8), (32, 1)]
                HALF = L // 2
                KS = [(0, 512), (512, 512), (1024, 512), (1536, 512), (2048, HALF + 1 - 2048)]
                for (k0, n) in KS:
                    first = True
                    for (g0, gn) in groups:
                        ct, R_ct, d_ct = c_r.next()
                        st_, R_st, d_st = s_r.next()
                        if gn == 8:
                            k.op("sp", lambda e: e.dma_start(out=ct[:, :, 0:n], in_=dftC[g0 * 128:(g0 + 8) * 128, k0:k0 + n].rearrange("(c p) k -> p c k", p=128)), writes=[R_ct], dma=d_ct)
                            k.op("sp", lambda e: e.dma_start(out=st_[:, :, 0:n], in_=dftS[g0 * 128:(g0 + 8) * 128, k0:k0 + n].rearrange("(c p) k -> p c k", p=128)), writes=[R_st], dma=d_st)
                        else:
                            k.op("sp", lambda e: e.dma_start(out=ct[0:16, 0, 0:n], in_=dftC[4096:L, k0:k0 + n]), writes=[R_ct], dma=d_ct)
                            k.op("sp", lambda e: e.dma_start(out=st_[0:16, 0, 0:n], in_=dftS[4096:L, k0:k0 + n]), writes=[R_st], dma=d_st)
                        for ci in range(gn):
                            ch = g0 + ci
                            ln = 128 if ch < 32 else 16
                            last = (ch == NT - 1)
                            for cs, (xt, R_x) in enumerate([(ct, R_ct), (st_, R_st)]):
                                for s2 in range(NS):
                                    for a in range(2):
                                        bk = cs * 4 + s2 * 2 + a
                                        k.op("pe", lambda e: e.matmul(ps[bk][:, 0:n], lhsT=Pc[0:ln, s2, ch, cs * 256 + a * 128:cs * 256 + (a + 1) * 128], rhs=xt[0:ln, ci, 0:n], start=first, stop=last),
                                             reads=[R_pc, R_x], writes=[R_ps[bk]])
                            first = False
                    klo = max(k0, 1)
                    khi = min(k0 + n, HALF)
                    tlo = L - khi + 1
                    thi = L - klo + 1
                    nm = khi - klo
                    for s2 in range(NS):
                        for a in range(2):
                            bA = s2 * 2 + a
                            bB = 4 + s2 * 2 + a
                            gfd, R_gfd, d_gfd = gfd_r.next()
                            gfm, R_gfm, d_gfm = gfm_r.next()
                            mfd, R_mfd, d_mfd = mfd_r.next()
                            mfm, R_mfm, d_mfm = mfm_r.next()
                            bs, R_bs, _ = bs_r.next()
                            uu, R_u, _ = u_r.next()
                            tt, R_t, _ = t_r.next()
                            k.op("sp", lambda e: e.dma_start(out=gfd[:, 0:n], in_=gTf_s[s2, a * 128:(a + 1) * 128, k0:k0 + n]), writes=[R_gfd], dma=d_gfd)
                            k.op("sp", lambda e: e.dma_start(out=gfm[:, 0:nm], in_=gTf_s[s2, a * 128:(a + 1) * 128, tlo:thi]), writes=[R_gfm], dma=d_gfm)
                            k.op("act", lambda e: e.activation(out=bs[:, 0:n], in_=ps[bB][:, 0:n], func=AF.Copy), reads=[R_ps[bB]], writes=[R_bs])
                            k.op("dve", lambda e: e.tensor_tensor(out=uu[:, 0:n], in0=ps[bA][:, 0:n], in1=bs[:, 0:n], op=ALU.add), reads=[R_ps[bA], R_bs], writes=[R_u])
                            k.op("dve", lambda e: e.tensor_tensor(out=tt[:, 0:n], in0=ps[bA][:, 0:n], in1=bs[:, 0:n], op=ALU.subtract), reads=[R_ps[bA], R_bs], writes=[R_t])
                            k.op("pool", lambda e: e.tensor_tensor(out=mfd[:, 0:n], in0=uu[:, 0:n], in1=gfd[:, 0:n], op=ALU.mult), reads=[R_u, R_gfd], writes=[R_mfd])
                            k.op("dve", lambda e: e.tensor_tensor(out=mfm[:, 0:nm], in0=tt[:, klo - k0:khi - k0][:, ::-1], in1=gfm[:, 0:nm], op=ALU.mult), reads=[R_t, R_gfm], writes=[R_mfm])
                            k.op("pool", lambda e: e.dma_start(out=mixT_s[s2, a * 128:(a + 1) * 128, k0:k0 + n], in_=mfd[:, 0:n]), reads=[R_mfd], dma=d_mfd)
                            k.op("pool", lambda e: e.dma_start(out=mixT_s[s2, a * 128:(a + 1) * 128, tlo:thi], in_=mfm[:, 0:nm]), reads=[R_mfm], dma=d_mfm)
            k.end_phase()

        def phase_D(l, s, h_src, last):
            with contextlib.ExitStack() as ls:
                mx_r = Ring(k, nc, ls, "dmx", [128, 8, 512], BF16, 2)
                h_r = Ring(k, nc, ls, "dh", [128, 1024], F32, 4)
                o_r = Ring(k, nc, ls, "do", [128, 1024], F32, 3)
                ss_r = Ring(k, nc, ls, "dss", [128, 1], F32, 4, with_dsem=False)
                junk = sb("djunk", [128, 1024], BF16, ls)
                R_fn = Res("fnw")
                if last:
                    fnw = sb("fnw", [128, 1024], F32, ls)
                    dfn = k.dsem(f"d_fn{s}")
                    k.op("sp", lambda e: e.dma_start(out=fnw[:], in_=final_norm_w.partition_broadcast(128)), writes=[R_fn], dma=dfn)
                OBK = [0, 1, 2, 3, 4, 5, 6, 7]
                obi = [0]
                flat = []
                for u, (t0, n) in enumerate(STILES):
                    for j, (r0, nr) in enumerate(tiles_of(t0, n)):
                        flat.append((u, j, t0, n, r0, nr))
                mxs = {}
                hts = {}

                def d_load_mx(u):
                    t0, n = STILES[u]
                    mx, R_mx, d_mx = mx_r.next()
                    k.op("sp", lambda e: e.dma_start(out=mx[:, :, 0:n], in_=mixT_s[s, :, t0:t0 + n].rearrange("(c p) t -> p c t", p=128)), writes=[R_mx], dma=d_mx)
                    mxs[u] = (mx, R_mx)

                def d_load_h(i):
                    u, j, t0, n, r0, nr = flat[i]
                    ht, R_ht, d_ht = h_r.next()
                    k.op("sp", lambda e: e.dma_start(out=ht[0:nr, :], in_=h_src[s, t0 + r0:t0 + r0 + nr, :]), writes=[R_ht], dma=d_ht)
                    hts[i] = (ht, R_ht)

                d_load_mx(0)
                d_load_h(0)
                d_load_h(1)
                for i, (u, j, t0, n, r0, nr) in enumerate(flat):
                    if j == 0 and u + 1 < len(STILES):
                        d_load_mx(u + 1)
                    if i + 2 < len(flat):
                        d_load_h(i + 2)
                    mx, R_mx = mxs[u]
                    ht, R_ht = hts[i]
                    ot, R_ot, d_ot = o_r.next()
                    for half in range(2):
                        bk = OBK[obi[0] % len(OBK)]
                        obi[0] += 1
                        for c in range(8):
                            k.op("pe", lambda e: e.matmul(ps[bk][0:nr, :], lhsT=mx[:, c, r0:r0 + nr], rhs=Wout[:, c, half * 512:(half + 1) * 512],
                                                          start=(c == 0), stop=(c == 7)), reads=[R_mx, R_w], writes=[R_ps[bk]])
                        k.op("dve", lambda e: e.tensor_tensor(out=ot[0:nr, half * 512:(half + 1) * 512], in0=ps[bk][0:nr, :], in1=ht[0:nr, half * 512:(half + 1) * 512], op=ALU.add),
                             reads=[R_ps[bk], R_ht], writes=[R_ot])
                    if not last:
                        k.op("pool", lambda e: e.dma_start(out=hbuf[s, t0 + r0:t0 + r0 + nr, :], in_=ot[0:nr, :]), reads=[R_ot], dma=d_ot)
                    else:
                        ss, R_ss, _ = ss_r.next()
                        k.op("act", lambda e: e.activation(out=junk[0:nr, :], in_=ot[0:nr, :], func=AF.Square, scale=float(DM ** -0.5), accum_out=ss[0:nr, 0:1]), reads=[R_ot], writes=[R_ss])
                        k.op("act", lambda e: e.activation(out=ss[0:nr, 0:1], in_=ss[0:nr, 0:1], func=AF.Sqrt, bias=eps_t[0:nr, 0:1]), reads=[R_ss], writes=[R_ss])
                        k.op("dve", lambda e: e.reciprocal(out=ss[0:nr, 0:1], in_=ss[0:nr, 0:1]), reads=[R_ss], writes=[R_ss])
                        k.op("dve", lambda e: e.scalar_tensor_tensor(out=ot[0:nr, :], in0=ot[0:nr, :], scalar=ss[0:nr, 0:1], in1=fnw[0:nr, :], op0=ALU.mult, op1=ALU.mult),
                             reads=[R_ot, R_ss, R_fn], writes=[R_ot])
                        g0 = t0 + r0
                        lo = max(g0, NMETA)
                        hi = g0 + nr
                        if hi > lo:
                            k.op("pool", lambda e: e.dma_start(out=out_d[s, lo - NMETA:hi - NMETA, :], in_=ot[lo - g0:hi - g0, :]), reads=[R_ot], dma=d_ot)
            k.end_phase()

        for li, l in enumerate(layers):
            h_src = h0 if (li == 0 and first_from_input) else hbuf
            if "P" in phases:
                phase_P(l)
            for s in range(NS):
                if "A" in phases:
                    phase_A(l, s, h_src)
            for s in range(NS):
                if "B" in phases:
                    phase_B(l, s)
            if "C" in phases:
                phase_C(l)
            last = final_norm and (li == len(layers) - 1)
            for s in range(NS):
                if "D" in phases:
                    phase_D(l, s, h_src, last)
        k.emit()
        build_program.nops = k.nops
    return nc


_PROG = {}


def get_prog(key, **kw):
    if key not in _PROG:
        _PROG[key] = build_program(**kw)
    return _PROG[key]


def make_in_maps(inputs, h0_full):
    c = host_consts()
    shared = {
        "norm_w": inputs["norm_w"], "w_in": inputs["w_in"], "fourier_w": inputs["fourier_w"], "pool_w": inputs["pool_w"],
        "pool_scale": inputs["pool_scale"], "q_norm_w": inputs["q_norm_w"], "w_uq": inputs["w_uq"], "kv_norm_w": inputs["kv_norm_w"],
        "w_ukv": inputs["w_ukv"], "w_out": inputs["w_out"], "final_norm_w": inputs["final_norm_w"],
    }
    shared = {k_: np.ascontiguousarray(np.asarray(v, dtype=np.float32)) for k_, v in shared.items()}
    shared.update(c)
    maps = []
    for i in range(NCORES):
        m = dict(shared)
        m["h0"] = h0_full[i * NS:(i + 1) * NS]
        maps.append(m)
    return maps


def kernel(x, meta_tokens, norm_w, w_in, fourier_w, pool_w, pool_scale, q_norm_w, w_uq, kv_norm_w, w_ukv, w_out, final_norm_w):
    inputs = dict(norm_w=norm_w, w_in=w_in, fourier_w=fourier_w, pool_w=pool_w, pool_scale=pool_scale, q_norm_w=q_norm_w,
                  w_uq=w_uq, kv_norm_w=kv_norm_w, w_ukv=w_ukv, w_out=w_out, final_norm_w=final_norm_w)
    x = np.asarray(x, dtype=np.float32)
    B = x.shape[0]
    meta = np.asarray(meta_tokens, dtype=np.float32)
    h0 = np.concatenate([np.broadcast_to(meta[None], (B, NMETA, DM)), x], axis=1)
    h0 = np.ascontiguousarray(h0)
    nc = get_prog("full", layers=list(range(DEPTH)))
    maps = make_in_maps(inputs, h0)
    res = run_bass_kernel_spmd(nc, maps, core_ids=list(range(NCORES)))
    out = np.concatenate([np.asarray(r["out"]) for r in res.results], axis=0)
    return out.astype(np.float32)
```
